# Optimizing a Trainium2 kernel written in Bass

```python
import jax, jax.numpy as jnp
from jax import lax
import numpy as np

D_MODEL = 1024
BATCH = 4
SEQ = 8192
DEPTH = 1
DEC_BATCH = 8
DEC_SEQ = 64
PAST_LEN = 1024

CHUNK = 64
HEAD_DIM = 64
H_A = 8
H_B = 8
H_IDX = 8
D_IDX = 64
ROT_DIM = HEAD_DIM // 4
ROPE_THETA = 500000.0
TOPK_MAX = 256
BAND_CHUNKS = 8
BAND_ROWS = BAND_CHUNKS * CHUNK
REL_CLIP = 128
D_FF = 4 * D_MODEL
Q_BLOCK = 128
EPS = 1e-6
ATTN_SCALE = HEAD_DIM ** -0.5

W_A = H_A * HEAD_DIM
W_B = H_B * HEAD_DIM
SPLITS = [W_A, W_A, W_A, H_IDX * D_IDX, D_IDX, H_IDX, W_B, W_B, W_B, D_MODEL, D_MODEL]
SPLIT_POINTS = [sum(SPLITS[:i + 1]) for i in range(len(SPLITS) - 1)]
D_IN = sum(SPLITS)

kernel_name = 'hybrid_streaming_dsa_chunkband_step'


def rmsnorm(x, g):
    xf = x.astype(jnp.float32)
    y = xf * lax.rsqrt(jnp.mean(xf * xf, axis=-1, keepdims=True) + EPS)
    return (y * g.astype(jnp.float32)).astype(x.dtype)


def partial_rope(x, pos):
    half = ROT_DIM // 2
    inv = ROPE_THETA ** (-jnp.arange(half, dtype=jnp.float32) / half)
    ang = pos.astype(jnp.float32)[:, None] * inv[None, :]
    shape = (pos.shape[0],) + (1,) * (x.ndim - 3) + (half,)
    cos = jnp.cos(ang).reshape(shape).astype(x.dtype)
    sin = jnp.sin(ang).reshape(shape).astype(x.dtype)
    x1 = x[..., :half]
    x2 = x[..., half:ROT_DIM]
    return jnp.concatenate([x1 * cos - x2 * sin, x2 * cos + x1 * sin, x[..., ROT_DIM:]], axis=-1)


def rel_bias(table, dist):
    b = table[jnp.clip(dist, -REL_CLIP, REL_CLIP) + REL_CLIP]
    return jnp.moveaxis(b, -1, 0).astype(jnp.float32)


def mixer_inputs(xn, pos, w_in, qnorm_a, knorm_a, knorm_idx, qnorm_b, knorm_b):
    B, T, _ = xn.shape
    z = xn @ w_in
    qa, ka, va, qi, ki, wi, qb, kb, vb, ga, gb = jnp.split(z, SPLIT_POINTS, axis=-1)
    qa = partial_rope(rmsnorm(qa.reshape(B, T, H_A, HEAD_DIM), qnorm_a), pos)
    ka = partial_rope(rmsnorm(ka.reshape(B, T, H_A, HEAD_DIM), knorm_a), pos)
    va = va.reshape(B, T, H_A, HEAD_DIM)
    qi = partial_rope(qi.reshape(B, T, H_IDX, D_IDX), pos)
    ki = partial_rope(rmsnorm(ki, knorm_idx), pos)
    qb = rmsnorm(qb.reshape(B, T, H_B, HEAD_DIM), qnorm_b)
    kb = rmsnorm(kb.reshape(B, T, H_B, HEAD_DIM), knorm_b)
    vb = vb.reshape(B, T, H_B, HEAD_DIM)
    return qa, ka, va, qi, ki, wi, qb, kb, vb, jax.nn.sigmoid(ga), jax.nn.sigmoid(gb)


def dsa_attention(qa, ka_all, va_all, qi, ki_all, wi, q_pos, k_pos, topk):
    B, T = qa.shape[:2]
    qb = Q_BLOCK if T % Q_BLOCK == 0 else T
    nb = T // qb
    k_chunk = k_pos // CHUNK

    def block(args):
        q, q_i, w, qp = args
        s = jax.nn.relu(jnp.einsum('bqhd,bsd->bqhs', q_i, ki_all))
        isc = jnp.einsum('bqh,bqhs->bqs', w, s).astype(jnp.float32)
        q_chunk = qp // CHUNK
        vis = k_chunk[None, :] <= q_chunk[:, None]
        isc = jnp.where(vis[None], isc, -jnp.inf)
        _, idx = lax.top_k(isc, topk)
        k_sel = jax.vmap(lambda kk, ii: kk[ii])(ka_all, idx)
        v_sel = jax.vmap(lambda vv, ii: vv[ii])(va_all, idx)
        ok = k_chunk[idx] <= q_chunk[None, :, None]
        logits = jnp.einsum('bqhd,bqkhd->bhqk', q, k_sel).astype(jnp.float32) * ATTN_SCALE
        logits = jnp.where(ok[:, None], logits, -jnp.inf)
        p = jax.nn.softmax(logits, axis=-1).astype(v_sel.dtype)
        return jnp.einsum('bhqk,bqkhd->bqhd', p, v_sel)

    def blocks(a):
        return jnp.moveaxis(a.reshape((B, nb, qb) + a.shape[2:]), 1, 0)

    out = lax.map(block, (blocks(qa), blocks(qi), blocks(wi), q_pos.reshape(nb, qb)))
    return jnp.moveaxis(out, 0, 1).reshape(B, T, H_A, HEAD_DIM)


def band_attention_prompt(q, k, v, table):
    B, T, H, D = q.shape
    nc = T // CHUNK
    nbnd = BAND_CHUNKS + 1
    qc = q.reshape(B, nc, CHUNK, H, D)
    pad = jnp.zeros((B, BAND_CHUNKS, CHUNK, H, D), k.dtype)
    kp = jnp.concatenate([pad, k.reshape(B, nc, CHUNK, H, D)], axis=1)
    vp = jnp.concatenate([pad, v.reshape(B, nc, CHUNK, H, D)], axis=1)
    band_idx = jnp.arange(nc)[:, None] + jnp.arange(nbnd)[None, :]
    kband = kp[:, band_idx].reshape(B, nc, nbnd * CHUNK, H, D)
    vband = vp[:, band_idx].reshape(B, nc, nbnd * CHUNK, H, D)
    i = jnp.arange(CHUNK)
    s_rel = ((jnp.arange(nbnd)[:, None] - BAND_CHUNKS) * CHUNK + i[None, :]).reshape(-1)
    bias = rel_bias(table, i[:, None] - s_rel[None, :])
    key_ok = jnp.repeat(band_idx >= BAND_CHUNKS, CHUNK, axis=1)
    logits = jnp.einsum('bcqhd,bckhd->bchqk', qc, kband).astype(jnp.float32) * ATTN_SCALE + bias[None, None]
    logits = jnp.where(key_ok[None, :, None, None, :], logits, -jnp.inf)
    p = jax.nn.softmax(logits, axis=-1).astype(vband.dtype)
    return jnp.einsum('bchqk,bckhd->bcqhd', p, vband).reshape(B, T, H, D)


def band_attention_sample(q, k_all, v_all, q_pos, k_pos, table):
    bias = rel_bias(table, q_pos[:, None] - k_pos[None, :])
    logits = jnp.einsum('bqhd,bkhd->bhqk', q, k_all).astype(jnp.float32) * ATTN_SCALE + bias[None]
    p = jax.nn.softmax(logits, axis=-1).astype(v_all.dtype)
    return jnp.einsum('bhqk,bkhd->bqhd', p, v_all)


def merge_and_ffn(x, oa, ob, ga, gb, w_o_a, w_o_b, w_out, norm_ffn, w_up, w_down):
    B, T, _ = x.shape
    m = ga * (oa.reshape(B, T, W_A) @ w_o_a) + gb * (ob.reshape(B, T, W_B) @ w_o_b)
    x = x + m @ w_out
    h = jnp.square(jax.nn.relu(rmsnorm(x, norm_ffn) @ w_up))
    return x + h @ w_down


def setup_inputs(seed: int = 0) -> dict:
    key = jax.random.key(seed)
    ks = jax.random.split(key, 24)
    f32 = jnp.float32

    def nrm(k, shape, scale=1.0):
        return jax.random.normal(k, shape, f32) * scale

    def gain(k, n):
        return 1.0 + 0.01 * jax.random.normal(k, (DEPTH, n), f32)

    rows_b = min(BAND_ROWS, PAST_LEN)
    return {
        'x_prompt': nrm(ks[0], (BATCH, SEQ, D_MODEL)),
        'x_sample': nrm(ks[1], (DEC_BATCH, DEC_SEQ, D_MODEL)),
        'cache_k_a': nrm(ks[2], (DEPTH, DEC_BATCH, PAST_LEN, H_A, HEAD_DIM)),
        'cache_v_a': nrm(ks[3], (DEPTH, DEC_BATCH, PAST_LEN, H_A, HEAD_DIM)),
        'cache_k_idx': nrm(ks[4], (DEPTH, DEC_BATCH, PAST_LEN, D_IDX)),
        'cache_k_b': nrm(ks[5], (DEPTH, DEC_BATCH, rows_b, H_B, HEAD_DIM)),
        'cache_v_b': nrm(ks[6], (DEPTH, DEC_BATCH, rows_b, H_B, HEAD_DIM)),
        'norm_mix': gain(ks[7], D_MODEL),
        'w_in': nrm(ks[8], (DEPTH, D_MODEL, D_IN), D_MODEL ** -0.5),
        'qnorm_a': gain(ks[9], HEAD_DIM),
        'knorm_a': gain(ks[10], HEAD_DIM),
        'knorm_idx': gain(ks[11], D_IDX),
        'qnorm_b': gain(ks[12], HEAD_DIM),
        'knorm_b': gain(ks[13], HEAD_DIM),
        'rel_bias_b': nrm(ks[14], (DEPTH, 2 * REL_CLIP + 1, H_B), 0.5),
        'w_o_a': nrm(ks[15], (DEPTH, W_A, D_MODEL), W_A ** -0.5),
        'w_o_b': nrm(ks[16], (DEPTH, W_B, D_MODEL), W_B ** -0.5),
        'w_out': nrm(ks[17], (DEPTH, D_MODEL, D_MODEL), D_MODEL ** -0.5),
        'norm_ffn': gain(ks[18], D_MODEL),
        'w_up': nrm(ks[19], (DEPTH, D_MODEL, D_FF), D_MODEL ** -0.5),
        'w_down': nrm(ks[20], (DEPTH, D_FF, D_MODEL), D_FF ** -0.5),
    }


def reference(x_prompt, x_sample, cache_k_a, cache_v_a, cache_k_idx, cache_k_b, cache_v_b,
              norm_mix, w_in, qnorm_a, knorm_a, knorm_idx, qnorm_b, knorm_b, rel_bias_b,
              w_o_a, w_o_b, w_out, norm_ffn, w_up, w_down):
    T = x_prompt.shape[1]
    Ts = x_sample.shape[1]
    past = cache_k_a.shape[2]
    rows_b = cache_k_b.shape[2]
    pos_p = jnp.arange(T, dtype=jnp.int32)
    pos_s = past + jnp.arange(Ts, dtype=jnp.int32)
    kpos_a_s = jnp.arange(past + Ts, dtype=jnp.int32)
    kpos_b_s = jnp.arange(past - rows_b, past + Ts, dtype=jnp.int32)
    topk_p = min(TOPK_MAX, T // 4)
    topk_s = min(TOPK_MAX, (past + Ts) // 4)
    keep_p = min(BAND_ROWS, T)

    xp, xs = x_prompt, x_sample
    ka_p, va_p, ki_p, kb_p, vb_p = [], [], [], [], []
    ka_s, va_s, ki_s, kb_s, vb_s = [], [], [], [], []
    for l in range(DEPTH):
        norms = (qnorm_a[l], knorm_a[l], knorm_idx[l], qnorm_b[l], knorm_b[l])
        outw = (w_o_a[l], w_o_b[l], w_out[l], norm_ffn[l], w_up[l], w_down[l])
        qa, ka, va, qi, ki, wi, qb, kb, vb, ga, gb = mixer_inputs(rmsnorm(xp, norm_mix[l]), pos_p, w_in[l], *norms)
        oa = dsa_attention(qa, ka, va, qi, ki, wi, pos_p, pos_p, topk_p)
        ob = band_attention_prompt(qb, kb, vb, rel_bias_b[l])
        xp = merge_and_ffn(xp, oa, ob, ga, gb, *outw)
        ka_p.append(ka); va_p.append(va); ki_p.append(ki)
        kb_p.append(kb[:, T - keep_p:]); vb_p.append(vb[:, T - keep_p:])
        qa, ka, va, qi, ki, wi, qb, kb, vb, ga, gb = mixer_inputs(rmsnorm(xs, norm_mix[l]), pos_s, w_in[l], *norms)
        oa = dsa_attention(qa, jnp.concatenate([cache_k_a[l], ka], axis=1),
                           jnp.concatenate([cache_v_a[l], va], axis=1), qi,
                           jnp.concatenate([cache_k_idx[l], ki], axis=1), wi, pos_s, kpos_a_s, topk_s)
        ob = band_attention_sample(qb, jnp.concatenate([cache_k_b[l], kb], axis=1),
                                   jnp.concatenate([cache_v_b[l], vb], axis=1), pos_s, kpos_b_s, rel_bias_b[l])
        xs = merge_and_ffn(xs, oa, ob, ga, gb, *outw)
        ka_s.append(ka); va_s.append(va); ki_s.append(ki); kb_s.append(kb); vb_s.append(vb)

    return (xp, xs,
            jnp.stack(ka_p), jnp.stack(va_p), jnp.stack(ki_p), jnp.stack(kb_p), jnp.stack(vb_p),
            jnp.stack(ka_s), jnp.stack(va_s), jnp.stack(ki_s), jnp.stack(kb_s), jnp.stack(vb_s))
```

```python
import numpy as np
from contextlib import ExitStack
import concourse.bass as bass
import concourse.mybir as mybir
from concourse.bass_utils import run_bass_kernel_spmd

F32 = mybir.dt.float32
BF16 = mybir.dt.bfloat16
U8 = mybir.dt.uint8
ALU = mybir.AluOpType
AF = mybir.ActivationFunctionType
AX = mybir.AxisListType

D = 1024
NT = 65
NTS = 73
NSLOT = 33
DIN = 5704
KSTEPS = 16
NEG = -30000.0
EPS = 1e-6


class Buf:
    __slots__ = ("w", "r")

    def __init__(self):
        self.w = None
        self.r = {}


class Instr:
    __slots__ = ("eng", "fn", "deps", "signal", "sem", "val", "is_dma", "uid")


ENGS = ("pe", "act", "dve", "pool", "sp")
NRING = 8


class Sched:
    def __init__(self):
        self.q = {e: [] for e in ENGS}
        self.uid = 0
        self.bar = []
        self.bar_done = set()
        self.deferred = None

    def begin_defer(self):
        self.deferred = []

    def end_defer(self):
        ops = self.deferred
        self.deferred = None
        return ops

    def replay(self, ops, k):
        for _ in range(min(k, len(ops))):
            self.add(*ops.pop(0))

    def barrier(self):
        lst = []
        for e in ENGS:
            comp = [i for i in self.q[e] if not i.is_dma]
            if comp:
                lst.append(comp[-1])
            lst += [i for i in self.q[e] if i.is_dma][-NRING:]
        self.bar = lst
        self.bar_done = set()

    def add(self, eng, fn, reads=(), writes=(), dma=False):
        import os
        if self.deferred is not None:
            self.deferred.append((eng, fn, tuple(reads), tuple(writes), dma))
            return None
        if self.uid >= int(os.environ.get("KMAX", "100000000")):
            return None
        ins = Instr()
        ins.eng = eng
        ins.fn = fn
        ins.is_dma = dma
        ins.signal = dma
        ins.uid = self.uid
        self.uid += 1
        deps = {}

        def need(d, raw):
            if d is None or d is ins:
                return
            if (not d.is_dma) and (not dma) and d.eng == eng:
                if eng == "pe":
                    return
            deps[d.uid] = d

        for b in reads:
            need(b.w, True)
        for b in writes:
            need(b.w, False)
            for rd in b.r.values():
                need(rd, False)
        if self.bar and eng not in self.bar_done:
            self.bar_done.add(eng)
            for d in self.bar:
                deps[d.uid] = d
        for b in reads:
            b.r[("dma", ins.uid) if dma else eng] = ins
        for b in writes:
            b.w = ins
            b.r = {}
        ins.deps = list(deps.values())
        for d in ins.deps:
            d.signal = True
        self.q[eng].append(ins)
        return ins

    def pe(self, fn, reads=(), writes=()):
        return self.add("pe", fn, reads, writes)

    def act(self, fn, reads=(), writes=()):
        return self.add("act", fn, reads, writes)

    def dve(self, fn, reads=(), writes=()):
        return self.add("dve", fn, reads, writes)

    def pool(self, fn, reads=(), writes=()):
        return self.add("pool", fn, reads, writes)

    def dma(self, queue, out, in_, reads=(), writes=(), **kw):
        return self.add(queue, lambda e: e.dma_start(out=out, in_=in_, **kw), reads, writes, dma=True)

    def emit(self, nc, stack):
        esem = {e: stack.enter_context(nc.semaphore("s_" + e)) for e in ENGS}
        rings = {e: [stack.enter_context(nc.semaphore("r_%s%d" % (e, i))) for i in range(NRING)]
                 for e in ("sp", "pool", "act")}
        final_ring = {}
        for e in ENGS:
            cnt = 0
            nd = 0
            for ins in self.q[e]:
                if ins.is_dma:
                    ins.sem = rings[e][nd % NRING]
                    ins.val = 16 * (nd // NRING + 1)
                    final_ring[(e, nd % NRING)] = (ins.sem, ins.val)
                    nd += 1
                elif ins.signal:
                    cnt += 1
                    ins.sem = esem[e]
                    ins.val = cnt
        block = stack.enter_context(nc.Block())
        handles = {"pe": "tensor", "act": "scalar", "dve": "vector", "pool": "gpsimd", "sp": "sync"}

        def make(e):
            def body(h):
                waited = {}

                def wait(sem, val):
                    k = id(sem)
                    if waited.get(k, 0) >= val:
                        return
                    waited[k] = val
                    h.wait_ge(sem, val)

                nd = 0
                for ins in self.q[e]:
                    for d in ins.deps:
                        wait(d.sem, d.val)
                    if ins.is_dma:
                        if nd >= NRING:
                            wait(ins.sem, ins.val - 16)
                        nd += 1
                        ins.fn(h).then_inc(ins.sem, 16)
                    else:
                        r = ins.fn(h)
                        if ins.signal:
                            r.then_inc(ins.sem, 1)
                if e == "sp":
                    for (sem, val) in final_ring.values():
                        wait(sem, val)
            return body

        for e in ENGS:
            getattr(block, handles[e])(make(e))


def V(name, *a, **k):
    return lambda e: getattr(e, name)(*a, **k)


class T:
    __slots__ = ("ap", "b")

    def __init__(self, ap):
        self.ap = ap
        self.b = Buf()


_DSZ = {F32: 4, BF16: 2, U8: 1}


class Arena:
    def __init__(self, ap, base, limit):
        self.ap = ap
        self.off = base
        self.limit = limit

    def alloc(self, shape, dt):
        n = 1
        for s in shape[1:]:
            n *= s
        nb = (n * _DSZ[dt] + 31) // 32 * 32
        assert self.off + nb <= self.limit, ("arena overflow", self.off + nb, self.limit)
        v = self.ap[:, self.off // 2:(self.off + nb) // 2]
        self.off += nb
        if dt != BF16:
            v = v.bitcast(dt)
        v = v[:, 0:n]
        if len(shape) == 3:
            v = v.rearrange("p (a b) -> p a b", b=shape[2])
        elif len(shape) == 4:
            v = v.rearrange("p (a b c) -> p a b c", b=shape[2], c=shape[3])
        if shape[0] != 128:
            v = v[0:shape[0]]
        return T(v)


def build_program():
    nc = bass.Bass("TRN2", target_bir_lowering=False)

    def din(name, shape, dt=F32):
        return nc.dram_tensor(name, list(shape), dt, kind="ExternalInput").ap()

    def dout(name, shape, dt=F32):
        return nc.dram_tensor(name, list(shape), dt, kind="ExternalOutput").ap()

    def dscr(name, shape, dt):
        return nc.dram_tensor(name, list(shape), dt, kind="Internal").ap()

    x_all = din("x_all", [NT * 128, D])
    cos_all = din("cos_all", [NT * 128, 8])
    sin_all = din("sin_all", [NT * 128, 8])
    x_own = din("x_own", [NSLOT * 128, D])
    ck_a = din("ck_a", [1024, 512])
    cv_a = din("cv_a", [1024, 512])
    ck_i = din("ck_i", [1024, 64])
    ck_b = din("ck_b", [512, 512])
    cv_b = din("cv_b", [512, 512])
    w_in = din("w_in", [D, DIN])
    w_oa = din("w_oa", [512, D])
    w_ob = din("w_ob", [512, D])
    w_out = din("w_out", [D, D])
    w_up = din("w_up", [D, 4096])
    w_down = din("w_down", [4096, D])
    gmix_d = din("gmix", [128, 8])
    gffn_d = din("gffn", [128, 8])
    g_kab_d = din("g_kab", [128, 1024])
    g_qab_d = din("g_qab", [128, 1024])
    g_ki_d = din("g_ki", [128, 64])
    ident_d = din("ident", [128, 128])
    pow2_d = din("pow2", [128, 2 * (KSTEPS + 2)])
    mask_d = din("masks", [128, 256])
    bandp_d = din("band_p", [8, 128, 768])
    bands_d = din("band_s", [8, 128, 768])

    y_o = dout("y_o", [NSLOT * 128, D])
    ka_o = dout("ka_o", [NSLOT * 128, 512])
    va_o = dout("va_o", [NSLOT * 128, 512])
    ki_o = dout("ki_o", [NSLOT * 128, 64])
    kb_o = dout("kb_o", [3 * 128, 512])
    vb_o = dout("vb_o", [3 * 128, 512])

    kT_scr = dscr("kT_scr", [NTS, 128, 8, 128], BF16)
    v_scr = dscr("v_scr", [NTS, 128, 16, 65], BF16)
    q_scr = dscr("q_scr", [NSLOT, 128, 8, 128], BF16)
    qi_scr = dscr("qi_scr", [NSLOT, 128, 4, 128], BF16)
    gate_scr = dscr("gate_scr", [NSLOT, 128, 2048], BF16)
    wi_scr = dscr("wi_scr", [NSLOT, 128, 8], F32)
    oab_scr = dscr("oab_scr", [NSLOT, 128, 1024], BF16)
    wbf_scr = dscr("wbf_scr", [128, 80 * 1024], BF16)
    wscr_b = Buf()
    kscr_b = [Buf() for _ in range(NTS)]
    vscr_b = [Buf() for _ in range(NTS)]
    qscr_b = [Buf() for _ in range(NSLOT)]
    oscr_b = [Buf() for _ in range(NSLOT)]

    S = Sched()
    with ExitStack() as st:
        ARENA_BYTES = 206 * 1024
        arena_t = st.enter_context(nc.sbuf_tensor("arena", [128, ARENA_BYTES // 2], BF16))
        PS = st.enter_context(nc.psum_tensor("ps", [128, 8, 512], F32))
        psb = [Buf() for _ in range(8)]

        def PSb16(bank):
            return PS[:, bank, :].bitcast(BF16)

        A0 = Arena(arena_t, 0, ARENA_BYTES)
        ident_f = A0.alloc([128, 128], F32)
        ident_b = A0.alloc([128, 128], BF16)
        I4 = A0.alloc([128, 512], BF16)
        gmix = A0.alloc([128, 8], F32)
        gffn = A0.alloc([128, 8], F32)
        pow2 = A0.alloc([128, 2 * (KSTEPS + 2)], F32)
        masks = A0.alloc([128, 256], F32)
        kiT = A0.alloc([128, NTS * 128], BF16)
        kiT_b = [Buf() for _ in range(NTS)]
        PBASE = A0.off

        S.dma("sp", ident_f.ap, ident_d, writes=[ident_f.b])
        S.dma("sp", gmix.ap, gmix_d, writes=[gmix.b])
        S.dma("sp", gffn.ap, gffn_d, writes=[gffn.b])
        S.dma("sp", pow2.ap, pow2_d, writes=[pow2.b])
        S.dma("sp", masks.ap, mask_d, writes=[masks.b])
        S.dve(V("tensor_copy", out=ident_b.ap, in_=ident_f.ap), reads=[ident_f.b], writes=[ident_b.b])
        for j in range(4):
            S.dve(V("tensor_copy", out=I4.ap[:, j * 128:(j + 1) * 128], in_=ident_f.ap), reads=[ident_f.b], writes=[I4.b])

        A1 = Arena(arena_t, PBASE, ARENA_BYTES)
        w1 = A1.alloc([128, 8, DIN], BF16)
        g_kab = A1.alloc([128, 1024], F32)
        g_qab = A1.alloc([128, 1024], F32)
        g_ki = A1.alloc([128, 64], F32)
        xt = [A1.alloc([128, D], F32) for _ in range(2)]
        xT = [A1.alloc([128, 8, 128], BF16) for _ in range(2)]
        xT_b = [[Buf() for _ in range(8)] for _ in range(2)]
        cs = [A1.alloc([128, 8], F32) for _ in range(3)]
        sn = [A1.alloc([128, 8], F32) for _ in range(3)]
        ssx = [A1.alloc([128, 1], F32) for _ in range(2)]
        rstd = [A1.alloc([128, 1], F32) for _ in range(2)]
        z_k = [A1.alloc([128, 1024], F32) for _ in range(2)]
        z_v = [A1.alloc([128, 1024], F32) for _ in range(2)]
        z_q = [A1.alloc([128, 1024], F32) for _ in range(2)]
        z_qi = [A1.alloc([128, 512], F32) for _ in range(2)]
        z_ki = [A1.alloc([128, 64], F32) for _ in range(2)]
        wi_sb = [A1.alloc([128, 8], F32) for _ in range(2)]
        sq_tmps = [A1.alloc([128, 1024], F32) for _ in range(2)]
        hs_k = [A1.alloc([128, 16], F32) for _ in range(2)]
        hs_q = [A1.alloc([128, 16], F32) for _ in range(2)]
        hs_i = [A1.alloc([128, 1], F32) for _ in range(2)]
        knbs = [A1.alloc([128, 1024], BF16) for _ in range(2)]
        qnb = A1.alloc([128, 1024], BF16)
        qib = A1.alloc([128, 512], BF16)
        kibs = [A1.alloc([128, 128], BF16) for _ in range(2)]
        kT_sbs = [A1.alloc([128, 8, 128], BF16) for _ in range(2)]
        qT_sb = A1.alloc([128, 8, 128], BF16)
        qiT_sb = A1.alloc([128, 4, 128], BF16)
        vaug = [A1.alloc([128, 16, 65], BF16) for _ in range(2)]
        gates = A1.alloc([128, 2048], BF16)
        rt = [A1.alloc([128, 16, 8], F32) for _ in range(4)]

        w1_b = [Buf() for _ in range(3)]
        wstg = [A1.alloc([128, 1024], F32) for _ in range(2)]
        wj = 0
        for c0 in range(0, DIN, 1024):
            c1 = min(DIN, c0 + 1024)
            n = c1 - c0
            for c in range(8):
                stg = wstg[wj % 2]
                S.dma("sp", stg.ap[:, 0:n], w_in[c * 128:(c + 1) * 128, c0:c1], writes=[stg.b])
                if wj % 2 == 0:
                    S.dve(V("tensor_copy", out=w1.ap[:, c, c0:c1], in_=stg.ap[:, 0:n]), reads=[stg.b], writes=[w1_b[c0 // 2048]])
                else:
                    S.act(V("activation", out=w1.ap[:, c, c0:c1], in_=stg.ap[:, 0:n], func=AF.Copy), reads=[stg.b], writes=[w1_b[c0 // 2048]])
                wj += 1
        S.dma("sp", g_kab.ap, g_kab_d, writes=[g_kab.b])
        S.dma("sp", g_qab.ap, g_qab_d, writes=[g_qab.b])
        S.dma("sp", g_ki.ap, g_ki_d, writes=[g_ki.b])
        for i in range(2):
            S.pool(V("memset", vaug[i].ap[:, :, 64:65], 1.0), writes=[vaug[i].b])

        zrot = [0]
        trot = [0]

        def headnorm(z, nh, gains, hs, sq_tmp):
            n = nh * 64
            S.act(V("activation", out=sq_tmp.ap[:, 0:n], in_=z.ap[:, 0:n], func=AF.Square), reads=[z.b], writes=[sq_tmp.b])
            S.dve(V("tensor_reduce", out=hs.ap[:, 0:nh], in_=sq_tmp.ap[:, 0:n].rearrange("p (h d) -> p h d", d=64),
                    axis=AX.X, op=ALU.add), reads=[sq_tmp.b], writes=[hs.b])
            S.dve(V("tensor_scalar", out=hs.ap[:, 0:nh], in0=hs.ap[:, 0:nh], scalar1=1.0 / 64, scalar2=EPS,
                    op0=ALU.mult, op1=ALU.add), reads=[hs.b], writes=[hs.b])
            S.act(V("activation", out=hs.ap[:, 0:nh], in_=hs.ap[:, 0:nh], func=AF.Sqrt), reads=[hs.b], writes=[hs.b])
            S.dve(V("reciprocal", out=hs.ap[:, 0:nh], in_=hs.ap[:, 0:nh]), reads=[hs.b], writes=[hs.b])
            zv = z.ap[:, 0:n].rearrange("p (h d) -> p h d", d=64)
            S.dve(V("tensor_tensor", out=zv, in0=zv, in1=hs.ap[:, 0:nh].unsqueeze(2).to_broadcast([128, nh, 64]),
                    op=ALU.mult), reads=[z.b, hs.b], writes=[z.b])
            S.dve(V("tensor_tensor", out=z.ap[:, 0:n], in0=z.ap[:, 0:n], in1=gains.ap, op=ALU.mult),
                  reads=[z.b, gains.b], writes=[z.b])

        def rope(z, col0, nh, cs_t, sn_t):
            v = z.ap[:, col0:col0 + nh * 64].rearrange("p (h d) -> p h d", d=64)
            x1 = v[:, :, 0:8]
            x2 = v[:, :, 8:16]
            cb = cs_t.ap.unsqueeze(1).to_broadcast([128, nh, 8])
            sb_ = sn_t.ap.unsqueeze(1).to_broadcast([128, nh, 8])
            t = [r.ap[:, 0:nh, :] for r in rt]
            rd = [z.b, cs_t.b, sn_t.b]
            S.pool(V("tensor_tensor", out=t[0], in0=x1, in1=cb, op=ALU.mult), reads=rd, writes=[rt[0].b])
            S.pool(V("tensor_tensor", out=t[1], in0=x2, in1=sb_, op=ALU.mult), reads=rd, writes=[rt[1].b])
            S.pool(V("tensor_tensor", out=t[2], in0=x2, in1=cb, op=ALU.mult), reads=rd, writes=[rt[2].b])
            S.pool(V("tensor_tensor", out=t[3], in0=x1, in1=sb_, op=ALU.mult), reads=rd, writes=[rt[3].b])
            S.pool(V("tensor_tensor", out=x1, in0=t[0], in1=t[1], op=ALU.subtract), reads=[rt[0].b, rt[1].b], writes=[z.b])
            S.pool(V("tensor_tensor", out=x2, in0=t[2], in1=t[3], op=ALU.add), reads=[rt[2].b, rt[3].b], writes=[z.b])

        def transposes_bf(src, ncol_blocks, dst, dst_c0):
            bank = 6 + (trot[0] % 2)
            trot[0] += 1
            pv = PSb16(bank)
            for c in range(ncol_blocks):
                S.pe(V("transpose", out=pv[:, c * 128:(c + 1) * 128], in_=src.ap[:, c * 128:(c + 1) * 128],
                       identity=ident_b.ap), reads=[src.b, ident_b.b], writes=[psb[bank]])
            S.dve(V("tensor_copy", out=dst.ap[:, dst_c0:dst_c0 + ncol_blocks, :],
                    in_=pv[:, 0:ncol_blocks * 128].rearrange("p (c k) -> p c k", k=128)),
                  reads=[psb[bank]], writes=[dst.b])

        def ki_to_kiT(src_f32_ap, src_b, tidx, kib):
            S.pool(V("tensor_copy", out=kib.ap[:, 0:64], in_=src_f32_ap), reads=[src_b], writes=[kib.b])
            S.pool(V("tensor_copy", out=kib.ap[:, 64:128], in_=src_f32_ap), reads=[src_b], writes=[kib.b])
            bank = 6 + (trot[0] % 2)
            trot[0] += 1
            pv = PSb16(bank)
            S.pe(V("transpose", out=pv[:, 0:128], in_=kib.ap, identity=ident_b.ap), reads=[kib.b, ident_b.b], writes=[psb[bank]])
            S.dve(V("tensor_copy", out=kiT.ap[:, tidx * 128:(tidx + 1) * 128], in_=pv[:, 0:128]),
                  reads=[psb[bank]], writes=[kiT_b[tidx]])

        KCH = [(0, 512, "ka"), (512, 1024, "kb"), (1024, 1536, "va"), (1536, 2048, "vb"), (2048, 2112, "ki")]
        QCH = [(2112, 2624, "qa"), (2624, 3136, "qb"), (3136, 3648, "qi"), (3648, 4160, "g0"), (4160, 4672, "g1"),
               (4672, 5184, "g2"), (5184, 5696, "g3"), (5696, 5704, "wi")]

        def p1_load(t):
            p = t % 2
            S.dma("sp", xt[p].ap, x_all[t * 128:(t + 1) * 128, :], writes=[xt[p].b])
            S.dma("sp", cs[t % 3].ap, cos_all[t * 128:(t + 1) * 128, :], writes=[cs[t % 3].b])
            S.dma("sp", sn[t % 3].ap, sin_all[t * 128:(t + 1) * 128, :], writes=[sn[t % 3].b])

        def p1_front(t, after_chunk=None):
            own = (t % 2 == 0)
            slot = t // 2
            p = t % 2
            po = slot % 2
            xb = xt[p]
            sq_tmp = sq_tmps[p]
            S.act(V("activation", out=sq_tmp.ap, in_=xb.ap, func=AF.Square, accum_out=ssx[p].ap),
                  reads=[xb.b], writes=[sq_tmp.b, ssx[p].b])
            S.dve(V("tensor_scalar", out=rstd[p].ap, in0=ssx[p].ap, scalar1=1.0 / D, scalar2=EPS, op0=ALU.mult, op1=ALU.add),
                  reads=[ssx[p].b], writes=[rstd[p].b])
            S.act(V("activation", out=rstd[p].ap, in_=rstd[p].ap, func=AF.Sqrt), reads=[rstd[p].b], writes=[rstd[p].b])
            S.dve(V("reciprocal", out=rstd[p].ap, in_=rstd[p].ap), reads=[rstd[p].b], writes=[rstd[p].b])
            for c in range(8):
                bank = c // 4
                S.pe(V("transpose", out=PS[:, bank, (c % 4) * 128:(c % 4 + 1) * 128], in_=xb.ap[:, c * 128:(c + 1) * 128],
                       identity=ident_f.ap), reads=[xb.b, ident_f.b], writes=[psb[bank]])
            for c in range(8):
                bank = c // 4
                src = PS[:, bank, (c % 4) * 128:(c % 4 + 1) * 128]
                S.dve(V("tensor_scalar", out=xT[p].ap[:, c, :], in0=src, scalar1=gmix.ap[:, c:c + 1], scalar2=None,
                        op0=ALU.mult), reads=[psb[bank], gmix.b], writes=[xT_b[p][c]])
            if t + 1 < NT1:
                p1_load(t + 1)
            chunks = KCH + (QCH if own else [])
            for (c0, c1, kind) in chunks:
                n = c1 - c0
                bank = 2 + (zrot[0] % 4)
                zrot[0] += 1
                for c in range(8):
                    S.pe(V("matmul", PS[:, bank, 0:n], lhsT=xT[p].ap[:, c, :], rhs=w1.ap[:, c, c0:c1], start=(c == 0), stop=(c == 7)),
                         reads=[xT_b[p][c], w1_b[c0 // 2048], w1_b[(c1 - 1) // 2048]], writes=[psb[bank]])
                src = PS[:, bank, 0:n]
                rd = [psb[bank], rstd[p].b]
                sc = rstd[p].ap
                if kind == "ka":
                    S.act(V("activation", out=z_k[p].ap[:, 0:512], in_=src, func=AF.Copy, scale=sc), reads=rd, writes=[z_k[p].b])
                elif kind == "kb":
                    S.act(V("activation", out=z_k[p].ap[:, 512:1024], in_=src, func=AF.Copy, scale=sc), reads=rd, writes=[z_k[p].b])
                elif kind == "va":
                    S.act(V("activation", out=z_v[p].ap[:, 0:512], in_=src, func=AF.Copy, scale=sc), reads=rd, writes=[z_v[p].b])
                elif kind == "vb":
                    S.act(V("activation", out=z_v[p].ap[:, 512:1024], in_=src, func=AF.Copy, scale=sc), reads=rd, writes=[z_v[p].b])
                elif kind == "ki":
                    S.act(V("activation", out=z_ki[p].ap, in_=src, func=AF.Copy, scale=sc), reads=rd, writes=[z_ki[p].b])
                elif kind == "qa":
                    S.act(V("activation", out=z_q[po].ap[:, 0:512], in_=src, func=AF.Copy, scale=sc), reads=rd, writes=[z_q[po].b])
                elif kind == "qb":
                    S.act(V("activation", out=z_q[po].ap[:, 512:1024], in_=src, func=AF.Copy, scale=sc), reads=rd, writes=[z_q[po].b])
                elif kind == "qi":
                    S.act(V("activation", out=z_qi[po].ap, in_=src, func=AF.Copy, scale=sc), reads=rd, writes=[z_qi[po].b])
                elif kind[0] == "g":
                    j = int(kind[1])
                    S.act(V("activation", out=gates.ap[:, j * 512:(j + 1) * 512], in_=src, func=AF.Sigmoid, scale=sc), reads=rd, writes=[gates.b])
                elif kind == "wi":
                    S.act(V("activation", out=wi_sb[po].ap, in_=src, func=AF.Copy, scale=sc), reads=rd, writes=[wi_sb[po].b])
                if after_chunk is not None:
                    after_chunk()
        def p1_back(t):
            own = (t % 2 == 0)
            slot = t // 2
            p = t % 2
            po = slot % 2
            sq_tmp = sq_tmps[p]
            knb = knbs[p]
            kib = kibs[p]
            kT_sb = kT_sbs[p]
            headnorm(z_k[p], 16, g_kab, hs_k[p], sq_tmp)
            rope(z_k[p], 0, 8, cs[t % 3], sn[t % 3])
            headnorm(z_ki[p], 1, g_ki, hs_i[p], sq_tmp)
            rope(z_ki[p], 0, 1, cs[t % 3], sn[t % 3])
            if own:
                r0 = slot * 128
                S.dma("sp", ka_o[r0:r0 + 128, :], z_k[p].ap[:, 0:512], reads=[z_k[p].b])
                S.dma("sp", va_o[r0:r0 + 128, :], z_v[p].ap[:, 0:512], reads=[z_v[p].b])
                S.dma("sp", ki_o[r0:r0 + 128, :], z_ki[p].ap, reads=[z_ki[p].b])
                if slot >= 30:
                    rb = (slot - 30) * 128
                    S.dma("sp", kb_o[rb:rb + 128, :], z_k[p].ap[:, 512:1024], reads=[z_k[p].b])
                    S.dma("sp", vb_o[rb:rb + 128, :], z_v[p].ap[:, 512:1024], reads=[z_v[p].b])
            S.dve(V("tensor_copy", out=knb.ap, in_=z_k[p].ap), reads=[z_k[p].b], writes=[knb.b])
            transposes_bf(knb, 8, kT_sb, 0)
            S.dma("sp", kT_scr[t], kT_sb.ap, reads=[kT_sb.b], writes=[kscr_b[t]])
            S.act(V("activation", out=vaug[p].ap[:, :, 0:64], in_=z_v[p].ap.rearrange("p (h d) -> p h d", d=64), func=AF.Copy),
                  reads=[z_v[p].b], writes=[vaug[p].b])
            S.dma("sp", v_scr[t], vaug[p].ap, reads=[vaug[p].b], writes=[vscr_b[t]])
            ki_to_kiT(z_ki[p].ap, z_ki[p].b, t, kib)
            if own:
                headnorm(z_q[po], 16, g_qab, hs_q[po], sq_tmp)
                rope(z_q[po], 0, 8, cs[t % 3], sn[t % 3])
                rope(z_qi[po], 0, 8, cs[t % 3], sn[t % 3])
                S.dve(V("tensor_copy", out=qnb.ap, in_=z_q[po].ap), reads=[z_q[po].b], writes=[qnb.b])
                transposes_bf(qnb, 8, qT_sb, 0)
                S.dma("sp", q_scr[slot], qT_sb.ap, reads=[qT_sb.b], writes=[qscr_b[slot]])
                S.dve(V("tensor_copy", out=qib.ap, in_=z_qi[po].ap), reads=[z_qi[po].b], writes=[qib.b])
                transposes_bf(qib, 4, qiT_sb, 0)
                S.dma("sp", qi_scr[slot], qiT_sb.ap, reads=[qiT_sb.b], writes=[qscr_b[slot]])
                S.dma("sp", gate_scr[slot], gates.ap, reads=[gates.b], writes=[qscr_b[slot]])
                S.dma("sp", wi_scr[slot], wi_sb[po].ap, reads=[wi_sb[po].b], writes=[qscr_b[slot]])

        import os
        PH = os.environ.get("KPH", "123")
        NT1 = int(os.environ.get("KNT", NT))
        for m in (range(8) if "c" in PH or "2" in PH else []):
            p = m % 2
            tidx = 65 + m
            xb = xt[p]
            knb = knbs[p]
            kT_sb = kT_sbs[p]
            S.dma("sp", xb.ap[:, 0:512], ck_a[m * 128:(m + 1) * 128, :], writes=[xb.b])
            S.dma("sp", xb.ap[:, 512:1024], cv_a[m * 128:(m + 1) * 128, :], writes=[xb.b])
            S.dve(V("tensor_copy", out=knb.ap[:, 0:512], in_=xb.ap[:, 0:512]), reads=[xb.b], writes=[knb.b])
            S.pool(V("tensor_copy", out=vaug[p].ap[:, 0:8, 0:64], in_=xb.ap[:, 512:1024].rearrange("p (h d) -> p h d", d=64)),
                   reads=[xb.b], writes=[vaug[p].b])
            if m >= 4:
                zb = z_k[p]
                S.dma("sp", zb.ap[:, 0:512], ck_b[(m - 4) * 128:(m - 3) * 128, :], writes=[zb.b])
                S.dma("sp", zb.ap[:, 512:1024], cv_b[(m - 4) * 128:(m - 3) * 128, :], writes=[zb.b])
                S.dve(V("tensor_copy", out=knb.ap[:, 512:1024], in_=zb.ap[:, 0:512]), reads=[zb.b], writes=[knb.b])
                S.pool(V("tensor_copy", out=vaug[p].ap[:, 8:16, 0:64], in_=zb.ap[:, 512:1024].rearrange("p (h d) -> p h d", d=64)),
                       reads=[zb.b], writes=[vaug[p].b])
            nb_ = 8 if m >= 4 else 4
            transposes_bf(knb, nb_, kT_sb, 0)
            S.dma("sp", kT_scr[tidx][:, 0:nb_, :], kT_sb.ap[:, 0:nb_, :], reads=[kT_sb.b], writes=[kscr_b[tidx]])
            S.dma("sp", v_scr[tidx][:, 0:2 * nb_, :], vaug[p].ap[:, 0:2 * nb_, :], reads=[vaug[p].b], writes=[vscr_b[tidx]])
            zi = z_ki[p]
            S.dma("sp", zi.ap, ck_i[m * 128:(m + 1) * 128, :], writes=[zi.b])
            ki_to_kiT(zi.ap, zi.b, tidx, kibs[p])

        p1_load(0)
        p1_front(0)
        for t in range(NT1):
            S.begin_defer()
            p1_back(t)
            ops = S.end_defer()
            if t + 1 < NT1:
                nch = 13 if (t + 1) % 2 == 0 else 5
                per = (len(ops) + nch - 1) // nch
                p1_front(t + 1, after_chunk=lambda: S.replay(ops, per))
            S.replay(ops, len(ops))

        S.barrier()
        A2 = Arena(arena_t, PBASE, ARENA_BYTES)
        NMAX = 64 * 128
        isc = [A2.alloc([128, NMAX], F32) for _ in range(2)]
        junk = A2.alloc([128, NMAX], U8)
        mb = [A2.alloc([128, NMAX], BF16) for _ in range(2)]
        kbuf = [A2.alloc([128, 4, 4, 128], BF16) for _ in range(2)]
        vbuf = [A2.alloc([128, 4, 8, 65], BF16) for _ in range(2)]
        bandb = A2.alloc([128, 8, 768], BF16)
        bstage = A2.alloc([128, 768], F32)
        qTb = [A2.alloc([128, 16, 128], BF16) for _ in range(2)]
        qiTb = [A2.alloc([128, 4, 128], BF16) for _ in range(2)]
        wib = [A2.alloc([128, 8], F32) for _ in range(2)]
        diag = [A2.alloc([128, 8, 128], BF16) for _ in range(2)]
        Rb = [A2.alloc([128, 512], BF16) for _ in range(5)]
        pTb = [A2.alloc([128, 512], BF16) for _ in range(5)]
        LAG = 3
        SBANKS = [3, 4, 7, 0, 1]
        HBANKS = [0, 1, 3, 4, 7]
        oab = [A2.alloc([128, 1024], BF16) for _ in range(2)]
        bst = [A2.alloc([128, 4 * (KSTEPS + 2)], F32) for _ in range(2)]
        lnb = A2.alloc([128, 8], F32)
        rsb = [A2.alloc([128, 1], F32) for _ in range(8)]
        cstg = [A2.alloc([128, 1024], F32) for _ in range(2)]
        cout = [A2.alloc([128, 1024], BF16) for _ in range(2)]
        conv_i = [0]

        def conv_src(i):
            if i < 4:
                return w_oa[i * 128:(i + 1) * 128, :]
            if i < 8:
                return w_ob[(i - 4) * 128:(i - 3) * 128, :]
            if i < 16:
                return w_out[(i - 8) * 128:(i - 7) * 128, :]
            if i < 48:
                c, j = (i - 16) // 4, (i - 16) % 4
                return w_up[c * 128:(c + 1) * 128, j * 1024:(j + 1) * 1024]
            return w_down[(i - 48) * 128:(i - 47) * 128, :]

        def conv_steps(k):
            for _ in range(k):
                i = conv_i[0]
                if i >= 80:
                    return
                conv_i[0] += 1
                S.dma("pool", cstg[i % 2].ap, conv_src(i), writes=[cstg[i % 2].b])
                S.pool(V("tensor_copy", out=cout[i % 2].ap, in_=cstg[i % 2].ap), reads=[cstg[i % 2].b], writes=[cout[i % 2].b])
                S.dma("pool", wbf_scr[:, i * 1024:(i + 1) * 1024], cout[i % 2].ap, reads=[cout[i % 2].b], writes=[wscr_b])

        rot = {"sh": 0, "R": 0, "st": 0, "pT": 0, "kv": 0}
        K1 = KSTEPS + 2

        def load_band(src_d):
            for h in range(8):
                S.dma("sp", bstage.ap, src_d[h], writes=[bstage.b])
                S.act(V("activation", out=bandb.ap[:, h, :], in_=bstage.ap, func=AF.Copy, scale=8.0), reads=[bstage.b], writes=[bandb.b])

        def groups_of(tiles):
            gs = []
            for pos, t in enumerate(tiles):
                if gs and len(gs[-1]) < 4 and gs[-1][-1][1] + 1 == t:
                    gs[-1].append((pos, t))
                else:
                    gs.append([(pos, t)])
            return gs

        def p2_load_idx(s):
            b = s % 2
            S.dma("sp", qiTb[b].ap, qi_scr[s], reads=[qscr_b[s]], writes=[qiTb[b].b])
            S.dma("sp", wib[b].ap, wi_scr[s], reads=[qscr_b[s]], writes=[wib[b].b])
            for h in range(8):
                S.dve(V("tensor_scalar", out=diag[b].ap[:, h, :], in0=ident_b.ap, scalar1=wib[b].ap[:, h:h + 1], scalar2=None, op0=ALU.mult),
                      reads=[ident_b.b, wib[b].b], writes=[diag[b].b])

        def p2_load_q(s):
            b = s % 2
            S.dma("sp", qTb[b].ap[0:64, 0::2, :], q_scr[s][0:64, :, :], reads=[qscr_b[s]], writes=[qTb[b].b])
            S.dma("sp", qTb[b].ap[64:128, 1::2, :], q_scr[s][64:128, :, :], reads=[qscr_b[s]], writes=[qTb[b].b])

        def p2_index(s, tiles):
            b = s % 2
            steps = [(g, h) for g in groups_of(tiles) for h in range(8)]
            q = []

            def emit_d(pd):
                (pg, ph, pR, pn) = pd
                S.pe(V("matmul", PS[:, 2, 0:pn], lhsT=diag[b].ap[:, ph, :], rhs=pR.ap[:, 0:pn], start=(ph == 0), stop=(ph == 7)),
                     reads=[diag[b].b, pR.b], writes=[psb[2]])
                if ph == 7:
                    c0 = pg[0][0] * 128
                    S.act(V("activation", out=isc[b].ap[:, c0:c0 + pn], in_=PS[:, 2, 0:pn], func=AF.Copy),
                          reads=[psb[2]], writes=[isc[b].b])

            for (g, h) in steps:
                n = len(g) * 128
                t0 = g[0][1]
                bank = HBANKS[rot["sh"] % 5]
                rot["sh"] += 1
                base = 64 * (h % 2)
                S.pe(V("matmul", PS[:, bank, 0:n], lhsT=qiTb[b].ap[base:base + 64, h // 2, :],
                       rhs=kiT.ap[base:base + 64, t0 * 128:t0 * 128 + n], start=True, stop=True),
                     reads=[qiTb[b].b] + [kiT_b[t] for (_, t) in g], writes=[psb[bank]])
                R = Rb[rot["R"] % 5]
                rot["R"] += 1
                S.act(V("activation", out=R.ap[:, 0:n], in_=PS[:, bank, 0:n], func=AF.Relu), reads=[psb[bank]], writes=[R.b])
                q.append((g, h, R, n))
                if len(q) > LAG:
                    emit_d(q.pop(0))
            while q:
                emit_d(q.pop(0))

        def p2_bisect(s, tiles, mask_list):
            b = s % 2
            N = len(tiles) * 128
            X = isc[b]
            st_ = bst[b]
            amax = st_.ap[:, 0:1]
            wk = st_.ap[:, K1:2 * K1]
            mid = st_.ap[:, 2 * K1:3 * K1]
            cnt = st_.ap[:, 3 * K1:4 * K1]
            w2 = st_.ap[:, 1:K1]
            S.dve(V("tensor_reduce", out=amax, in_=X.ap[:, 0:N], axis=AX.X, op=ALU.max, apply_absolute_value=True),
                  reads=[X.b], writes=[st_.b])
            for (pos, mcol) in mask_list:
                S.dve(V("tensor_tensor", out=X.ap[:, pos * 128:(pos + 1) * 128], in0=X.ap[:, pos * 128:(pos + 1) * 128],
                        in1=masks.ap[:, mcol * 128:(mcol + 1) * 128], op=ALU.add), reads=[X.b, masks.b], writes=[X.b])
            S.dve(V("tensor_scalar", out=wk, in0=pow2.ap[:, 0:K1], scalar1=amax, scalar2=None, op0=ALU.mult),
                  reads=[st_.b, pow2.b], writes=[st_.b])
            S.dve(V("tensor_scalar", out=w2, in0=pow2.ap[:, K1 + 1:2 * K1], scalar1=amax, scalar2=None, op0=ALU.mult),
                  reads=[st_.b, pow2.b], writes=[st_.b])
            S.dve(V("memset", mid[:, 0:1], 0.0), writes=[st_.b])
            for k in range(KSTEPS):
                S.dve(V("tensor_scalar", out=junk.ap[:, 0:N], in0=X.ap[:, 0:N], scalar1=mid[:, k:k + 1], scalar2=None,
                        op0=ALU.is_ge, op1=ALU.add, accum_out=cnt[:, k:k + 1]), reads=[X.b, st_.b], writes=[junk.b, st_.b])
                S.dve(V("scalar_tensor_tensor", out=cnt[:, k:k + 1], in0=cnt[:, k:k + 1], scalar=255.5, in1=w2[:, k:k + 1],
                        op0=ALU.is_ge, op1=ALU.mult), reads=[st_.b], writes=[st_.b])
                S.dve(V("scalar_tensor_tensor", out=mid[:, k + 1:k + 2], in0=mid[:, k:k + 1], scalar=wk[:, k + 1:k + 2],
                        in1=cnt[:, k:k + 1], op0=ALU.subtract, op1=ALU.add), reads=[st_.b], writes=[st_.b])
            thr = cnt[:, KSTEPS:KSTEPS + 1]
            S.dve(V("tensor_tensor", out=thr, in0=mid[:, KSTEPS:KSTEPS + 1], in1=wk[:, KSTEPS:KSTEPS + 1], op=ALU.subtract),
                  reads=[st_.b], writes=[st_.b])
            S.dve(V("tensor_scalar", out=mb[b].ap[:, 0:N], in0=X.ap[:, 0:N], scalar1=thr, scalar2=NEG, op0=ALU.is_lt, op1=ALU.mult),
                  reads=[X.b, st_.b], writes=[mb[b].b])

        def attend(s, tiles, pair0, vh0, bias_mode):
            b = s % 2
            qT = qTb[b]
            steps = []
            for g in groups_of(tiles):
                kv = rot["kv"] % 2
                rot["kv"] += 1
                ng = len(g)
                t0 = g[0][1]
                steps.append(("load", g, kv))
                for j, (pos, t) in enumerate(g):
                    for hg in range(2):
                        steps.append(("qk", pos, j, hg, kv))
            first = [True, True]
            nsteps_left = [sum(1 for x in steps if x[0] == "qk" and x[3] == hg) for hg in range(2)]
            pendq = []

            def emit_pv(pd):
                (pos, j, hg, kv, pt) = pd
                nsteps_left[hg] -= 1
                for hh in range(4):
                    h = 4 * hg + hh
                    S.pe(V("matmul", PS[:, 5 + hg, hh * 65:(hh + 1) * 65], lhsT=pt.ap[:, hh * 128:(hh + 1) * 128],
                           rhs=vbuf[kv].ap[:, j, h, :], start=(first[hg] and hh == 0), stop=(nsteps_left[hg] == 0),
                           skip_group_check=True), reads=[pt.b, vbuf[kv].b], writes=[psb[5 + hg]])
                first[hg] = False

            for stp in steps:
                if stp[0] == "load":
                    (_, g, kv) = stp
                    ng = len(g)
                    t0 = g[0][1]
                    S.dma("sp", kbuf[kv].ap[:, 0:ng], kT_scr[t0:t0 + ng].rearrange("t p c k -> p t c k")[:, :, pair0:pair0 + 4, :],
                          reads=[kscr_b[t] for (_, t) in g], writes=[kbuf[kv].b])
                    S.dma("sp", vbuf[kv].ap[:, 0:ng], v_scr[t0:t0 + ng].rearrange("t p h d -> p t h d")[:, :, vh0:vh0 + 8, :],
                          reads=[vscr_b[t] for (_, t) in g], writes=[vbuf[kv].b])
                    continue
                (_, pos, j, hg, kv) = stp
                sbank = SBANKS[rot["st"] % 5]
                rot["st"] += 1
                if bias_mode is None:
                    S.pe(V("matmul", PS[:, sbank, :], lhsT=mb[b].ap[:, pos * 128:(pos + 1) * 128], rhs=I4.ap, start=True, stop=False,
                           skip_group_check=True), reads=[mb[b].b, I4.b], writes=[psb[sbank]])
                for hh in range(4):
                    h = 4 * hg + hh
                    pr = h // 2
                    base = 64 * (h % 2)
                    if bias_mode is not None:
                        m = bias_mode[pos]
                        S.pe(V("matmul", PS[:, sbank, hh * 128:(hh + 1) * 128], lhsT=bandb.ap[:, h, m * 128:(m + 1) * 128], rhs=ident_b.ap,
                               start=True, stop=False, skip_group_check=True), reads=[bandb.b, ident_b.b], writes=[psb[sbank]])
                    S.pe(V("matmul", PS[:, sbank, hh * 128:(hh + 1) * 128], lhsT=kbuf[kv].ap[:, j, pr, :],
                           rhs=qT.ap[:, 2 * pair0 + h, :], start=False, stop=True, skip_group_check=True),
                         reads=[kbuf[kv].b, qT.b], writes=[psb[sbank]])
                pt = pTb[rot["pT"] % 5]
                rot["pT"] += 1
                S.act(V("activation", out=pt.ap, in_=PS[:, sbank, :], func=AF.Exp, scale=0.125), reads=[psb[sbank]], writes=[pt.b])
                pendq.append((pos, j, hg, kv, pt))
                if len(pendq) > LAG:
                    emit_pv(pendq.pop(0))
            while pendq:
                emit_pv(pendq.pop(0))
            col0 = 0 if bias_mode is None else 512
            for hg in range(2):
                ov = PS[:, 5 + hg, 0:260].rearrange("p (h d) -> p h d", d=65)
                S.act(V("activation", out=lnb.ap[:, hg * 4:(hg + 1) * 4], in_=ov[:, :, 64], func=AF.Ln), reads=[psb[5 + hg]], writes=[lnb.b])
                for hh in range(4):
                    h = 4 * hg + hh
                    S.act(V("activation", out=rsb[h].ap, in_=lnb.ap[:, h:h + 1], func=AF.Exp, scale=-1.0), reads=[lnb.b], writes=[rsb[h].b])
                    S.act(V("activation", out=oab[b].ap[:, col0 + h * 64:col0 + (h + 1) * 64], in_=ov[:, hh, 0:64], func=AF.Copy,
                            scale=rsb[h].ap), reads=[psb[5 + hg], rsb[h].b], writes=[oab[b].b])

        slots = []
        for i in range(32):
            tiles = list(range(2 * i + 2))
            masks_l = [(2 * i, 0), (2 * i + 1, 1)]
            band = [(2 * i - 4 + m, m) for m in range(6) if 2 * i - 4 + m >= 0]
            slots.append((tiles, masks_l, band))
        slots.append((list(range(65, 73)) + [64], [(8, 0)], [(69 + m, m) for m in range(4)] + [(64, 4)]))

        NS2 = min(NSLOT, int(os.environ.get("KNS", NSLOT))) if "2" in PH else 0
        for i in range(2):
            S.pool(V("memset", qTb[i].ap, 0.0), writes=[qTb[i].b])

        def mark(n):
            if os.environ.get("KDBG"):
                print("MARK", n, S.uid, flush=True)
        if NS2:
          mark("p2start")
          load_band(bandp_d)
          mark("band")
          p2_load_idx(0)
          mark("loadidx0")
          p2_index(0, slots[0][0])
          mark("index0")
          p2_bisect(0, slots[0][0], slots[0][1])
          mark("bisect0")
          p2_load_idx(1)
          p2_index(1, slots[1][0])
          p2_load_q(0)
          mark("pre-loop")
        for s in range(NS2):
            if s + 2 < NS2:
                p2_load_idx(s + 2)
                p2_index(s + 2, slots[s + 2][0])
            attend(s, slots[s][0], 0, 0, None)
            mark("dsa%d" % s)
            if s + 1 < NS2:
                p2_load_q(s + 1)
                p2_bisect(s + 1, slots[s + 1][0], slots[s + 1][1])
            if s == NSLOT - 1:
                load_band(bands_d)
            band = slots[s][2]
            mark("bis%d" % (s + 1))
            attend(s, [t for (t, m) in band], 4, 8, [m for (t, m) in band])
            mark("band%d" % s)
            if "3" in PH:
                conv_steps(3)
            S.dma("pool", oab_scr[s], oab[s % 2].ap, reads=[oab[s % 2].b], writes=[oscr_b[s]])

        S.barrier()
        A3 = Arena(arena_t, PBASE - NTS * 256, ARENA_BYTES)
        wall = A3.alloc([128, 80 * 1024], BF16)

        def wview(c0, nchunk, ncol):
            t_ = T(wall.ap[:, c0 * 1024:c0 * 1024 + nchunk * ncol].rearrange("p (c n) -> p c n", n=ncol))
            t_.b = wall.b
            return t_
        woa = wview(0, 4, D)
        wob = wview(4, 4, D)
        wo = wview(8, 8, D)
        wu = wview(16, 8, 4096)
        wd = wview(48, 32, D)
        o_in = A3.alloc([128, 1024], BF16)
        oT = A3.alloc([128, 8, 128], BF16)
        gt = A3.alloc([128, 2048], BF16)
        xin = A3.alloc([128, D], F32)
        t1 = A3.alloc([128, D], F32)
        t2 = A3.alloc([128, D], F32)
        x2s = [A3.alloc([128, D], F32) for _ in range(2)]
        x2Ts = [A3.alloc([128, 8, 128], BF16) for _ in range(2)]
        hT = A3.alloc([128, 32, 128], BF16)
        rl = A3.alloc([128, 512], F32)
        ss3 = A3.alloc([128, 1], F32)
        r3s = [A3.alloc([128, 1], F32) for _ in range(2)]

        if "3" in PH:
            conv_steps(80)
            for k in range(10):
                S.dma("sp", wall.ap[:, k * 8192:(k + 1) * 8192], wbf_scr[:, k * 8192:(k + 1) * 8192], reads=[wscr_b], writes=[wall.b])

        def tr_bf_p3(src, dst):
            pv = PSb16(0)
            for c in range(8):
                S.pe(V("transpose", out=pv[:, c * 128:(c + 1) * 128], in_=src.ap[:, c * 128:(c + 1) * 128], identity=ident_b.ap),
                     reads=[src.b, ident_b.b], writes=[psb[0]])
            S.dve(V("tensor_copy", out=dst.ap, in_=pv.rearrange("p (c k) -> p c k", k=128)), reads=[psb[0]], writes=[dst.b])

        def p3_stage_a(s):
            x2 = x2s[s % 2]
            x2T = x2Ts[s % 2]
            r3 = r3s[s % 2]
            S.dma("sp", o_in.ap, oab_scr[s], reads=[oscr_b[s]], writes=[o_in.b])
            S.dma("sp", gt.ap, gate_scr[s], reads=[qscr_b[s]], writes=[gt.b])
            S.dma("sp", xin.ap, x_own[s * 128:(s + 1) * 128, :], writes=[xin.b])
            tr_bf_p3(o_in, oT)
            for half in range(2):
                for c in range(4):
                    S.pe(V("matmul", PS[:, 2 + half, :], lhsT=oT.ap[:, c, :], rhs=woa.ap[:, c, half * 512:(half + 1) * 512],
                           start=(c == 0), stop=(c == 3)), reads=[oT.b, woa.b], writes=[psb[2 + half]])
                S.dve(V("tensor_tensor", out=t1.ap[:, half * 512:(half + 1) * 512], in0=PS[:, 2 + half, :],
                        in1=gt.ap[:, half * 512:(half + 1) * 512], op=ALU.mult), reads=[psb[2 + half], gt.b], writes=[t1.b])
            for half in range(2):
                for c in range(4):
                    S.pe(V("matmul", PS[:, 2 + half, :], lhsT=oT.ap[:, 4 + c, :], rhs=wob.ap[:, c, half * 512:(half + 1) * 512],
                           start=(c == 0), stop=(c == 3)), reads=[oT.b, wob.b], writes=[psb[2 + half]])
                S.dve(V("tensor_tensor", out=t2.ap[:, half * 512:(half + 1) * 512], in0=PS[:, 2 + half, :],
                        in1=gt.ap[:, 1024 + half * 512:1024 + (half + 1) * 512], op=ALU.mult), reads=[psb[2 + half], gt.b], writes=[t2.b])
            S.pool(V("tensor_tensor", out=o_in.ap, in0=t1.ap, in1=t2.ap, op=ALU.add), reads=[t1.b, t2.b], writes=[o_in.b])
            tr_bf_p3(o_in, oT)
            for half in range(2):
                for c in range(8):
                    S.pe(V("matmul", PS[:, 2 + half, :], lhsT=oT.ap[:, c, :], rhs=wo.ap[:, c, half * 512:(half + 1) * 512],
                           start=(c == 0), stop=(c == 7)), reads=[oT.b, wo.b], writes=[psb[2 + half]])
                S.dve(V("tensor_tensor", out=x2.ap[:, half * 512:(half + 1) * 512], in0=PS[:, 2 + half, :],
                        in1=xin.ap[:, half * 512:(half + 1) * 512], op=ALU.add), reads=[psb[2 + half], xin.b], writes=[x2.b])
            S.act(V("activation", out=t1.ap, in_=x2.ap, func=AF.Square, accum_out=ss3.ap), reads=[x2.b], writes=[t1.b, ss3.b])
            S.dve(V("tensor_scalar", out=r3.ap, in0=ss3.ap, scalar1=1.0 / D, scalar2=EPS, op0=ALU.mult, op1=ALU.add),
                  reads=[ss3.b], writes=[r3.b])
            S.dve(V("reciprocal", out=r3.ap, in_=r3.ap), reads=[r3.b], writes=[r3.b])
            for c in range(8):
                bank = c // 4
                S.pe(V("transpose", out=PS[:, bank, (c % 4) * 128:(c % 4 + 1) * 128], in_=x2.ap[:, c * 128:(c + 1) * 128],
                       identity=ident_f.ap), reads=[x2.b, ident_f.b], writes=[psb[bank]])
            for c in range(8):
                bank = c // 4
                src = PS[:, bank, (c % 4) * 128:(c % 4 + 1) * 128]
                S.dve(V("tensor_scalar", out=x2T.ap[:, c, :], in0=src, scalar1=gffn.ap[:, c:c + 1], scalar2=None, op0=ALU.mult),
                      reads=[psb[bank], gffn.b], writes=[x2T.b])

        def p3_stage_b(s, hook):
            x2 = x2s[s % 2]
            x2T = x2Ts[s % 2]
            r3 = r3s[s % 2]
            for f4 in range(8):
                bank = 4 + (f4 % 2)
                for ff in range(4):
                    f = f4 * 4 + ff
                    for c in range(8):
                        S.pe(V("matmul", PS[:, bank, ff * 128:(ff + 1) * 128], lhsT=wu.ap[:, c, f * 128:(f + 1) * 128], rhs=x2T.ap[:, c, :],
                               start=(c == 0), stop=(c == 7), skip_group_check=True), reads=[wu.b, x2T.b], writes=[psb[bank]])
                S.act(V("activation", out=rl.ap, in_=PS[:, bank, :], func=AF.Relu), reads=[psb[bank]], writes=[rl.b])
                S.dve(V("tensor_tensor", out=hT.ap[:, f4 * 4:(f4 + 1) * 4, :].rearrange("p a b -> p (a b)"), in0=rl.ap,
                        in1=rl.ap, op=ALU.mult), reads=[rl.b], writes=[hT.b])
                hook()
            for half in range(2):
                for f in range(32):
                    S.pe(V("matmul", PS[:, 6 + half, :], lhsT=hT.ap[:, f, :], rhs=wd.ap[:, f, half * 512:(half + 1) * 512],
                           start=(f == 0), stop=(f == 31)), reads=[hT.b, wd.b], writes=[psb[6 + half]])
                S.dve(V("scalar_tensor_tensor", out=x2.ap[:, half * 512:(half + 1) * 512], in0=PS[:, 6 + half, :], scalar=r3.ap,
                        in1=x2.ap[:, half * 512:(half + 1) * 512], op0=ALU.mult, op1=ALU.add),
                      reads=[psb[6 + half], r3.b, x2.b], writes=[x2.b])
                hook()
            S.dma("sp", y_o[s * 128:(s + 1) * 128, :], x2.ap, reads=[x2.b])

        NS3 = min(NSLOT, int(os.environ.get("KNS3", NSLOT))) if "3" in PH else 0
        if NS3:
            p3_stage_a(0)
        for s in range(NS3):
            ops = []
            if s + 1 < NS3:
                S.begin_defer()
                p3_stage_a(s + 1)
                ops = S.end_defer()
            per = (len(ops) + 7) // 8
            p3_stage_b(s, lambda: S.replay(ops, per))
            S.replay(ops, len(ops))

        S.emit(nc, st)
    return nc


def _rope_tables(pos):
    half = 8
    inv = (np.float32(500000.0) ** (-np.arange(half, dtype=np.float32) / np.float32(half))).astype(np.float32)
    ang = pos.astype(np.float32)[:, None] * inv[None, :]
    return np.cos(ang).astype(np.float32), np.sin(ang).astype(np.float32)


def _band_tiles(table, true_tile_of_m, visible_fn):
    out = np.full((8, 128, 768), NEG, np.float32)
    qi = np.arange(128)[:, None]
    kj = np.arange(128)[None, :]
    for m in range(6):
        off = true_tile_of_m[m]
        if off is None:
            continue
        dist = off * 128 + qi - kj
        idx = np.clip(dist, -128, 128) + 128
        vis = visible_fn(off, qi, kj)
        for h in range(8):
            g = table[idx, h]
            out[h, :, m * 128:(m + 1) * 128] = np.where(vis, g, np.float32(NEG))
    return out


def _vis_prompt(off, qi, kj):
    dc = 2 * off + qi // 64 - kj // 64
    return (dc >= 0) & (dc <= 8)


_CACHE = {}


def kernel(x_prompt, x_sample, cache_k_a, cache_v_a, cache_k_idx, cache_k_b, cache_v_b,
           norm_mix, w_in, qnorm_a, knorm_a, knorm_idx, qnorm_b, knorm_b, rel_bias_b,
           w_o_a, w_o_b, w_out, norm_ffn, w_up, w_down):
    f = lambda a: np.ascontiguousarray(np.asarray(a, dtype=np.float32))
    x_prompt, x_sample = f(x_prompt), f(x_sample)
    w_in_ = f(w_in)[0]
    sp = np.cumsum([0, 512, 512, 512, 512, 64, 8, 512, 512, 512, 1024, 1024])
    seg = {n: (sp[i], sp[i + 1]) for i, n in enumerate(["qa", "ka", "va", "qi", "ki", "wi", "qb", "kb", "vb", "ga", "gb"])}
    order = ["ka", "kb", "va", "vb", "ki", "qa", "qb", "qi", "ga", "gb", "wi"]
    w_in_r = np.ascontiguousarray(np.concatenate([w_in_[:, seg[n][0]:seg[n][1]] for n in order], axis=1))
    rep = lambda g, n: np.tile(f(g)[0], n)
    g_kab = np.ascontiguousarray(np.broadcast_to(np.concatenate([rep(knorm_a, 8), rep(knorm_b, 8)])[None], (128, 1024)))
    g_qab = np.ascontiguousarray(np.broadcast_to(np.concatenate([rep(qnorm_a, 8), rep(qnorm_b, 8)])[None], (128, 1024)))
    g_ki = np.ascontiguousarray(np.broadcast_to(f(knorm_idx)[0][None], (128, 64)))
    gmix = np.ascontiguousarray(f(norm_mix)[0].reshape(8, 128).T)
    gffn = np.ascontiguousarray(f(norm_ffn)[0].reshape(8, 128).T)
    K1 = KSTEPS + 2
    p2 = np.concatenate([2.0 ** (-np.arange(K1)), 2.0 ** (1.0 - np.arange(K1))]).astype(np.float32)
    pow2 = np.ascontiguousarray(np.broadcast_to(p2[None], (128, 2 * K1)))
    table = f(rel_bias_b)[0]
    qi = np.arange(128)[:, None]
    kj = np.arange(128)[None, :]
    own_mask = np.where((kj // 64) <= (qi // 64), 0.0, -1e30).astype(np.float32)
    band_s = _band_tiles(table, [4, 3, 2, 1, 0, None], _vis_prompt)

    in_maps = []
    for c in range(8):
        b, hf = c // 2, c % 2
        xb = x_prompt[b].reshape(64, 128, D)
        order_t = []
        for i in range(32):
            order_t += [2 * i + hf, 2 * i + 1 - hf]
        xs_pad = np.zeros((128, D), np.float32)
        xs_pad[:64] = x_sample[c]
        x_all = np.concatenate([xb[order_t].reshape(64 * 128, D), xs_pad], axis=0)
        pos = np.concatenate([(np.array(order_t)[:, None] * 128 + np.arange(128)[None]).reshape(-1),
                              1024 + np.arange(64), np.zeros(64, np.int64)])
        cos_all, sin_all = _rope_tables(pos)
        x_own = np.concatenate([xb[hf::2].reshape(32 * 128, D), xs_pad], axis=0)
        foreign = np.full((128, 128), 0.0 if hf == 1 else -1e30, np.float32)
        masks = np.ascontiguousarray(np.concatenate([own_mask, foreign], axis=1))
        if hf == 0:
            offs = [4, 3, 2, 1, 0, None]
        else:
            offs = [None, 4, 1, 2, 0 - 0, 0]
            offs = [4, None, 2, 3, 0, 1]
        band_p = _band_tiles(table, offs, _vis_prompt)
        in_maps.append(dict(
            x_all=np.ascontiguousarray(x_all), cos_all=cos_all, sin_all=sin_all, x_own=np.ascontiguousarray(x_own),
            ck_a=f(cache_k_a)[0, c].reshape(1024, 512), cv_a=f(cache_v_a)[0, c].reshape(1024, 512),
            ck_i=f(cache_k_idx)[0, c], ck_b=f(cache_k_b)[0, c].reshape(512, 512), cv_b=f(cache_v_b)[0, c].reshape(512, 512),
            w_in=w_in_r, w_oa=f(w_o_a)[0], w_ob=f(w_o_b)[0], w_out=f(w_out)[0], w_up=f(w_up)[0], w_down=f(w_down)[0],
            gmix=gmix, gffn=gffn, g_kab=g_kab, g_qab=g_qab, g_ki=g_ki, ident=np.eye(128, dtype=np.float32),
            pow2=pow2, masks=masks, band_p=band_p, band_s=band_s))

    if "nc" not in _CACHE:
        _CACHE["nc"] = build_program()
    import os
    NCORE = int(os.environ.get("KCORES", "8"))
    res = run_bass_kernel_spmd(_CACHE["nc"], in_maps[:NCORE], core_ids=list(range(NCORE)))
    R = list(res.results) + [res.results[0]] * (8 - NCORE)

    y_p = np.zeros((4, 64, 128, D), np.float32)
    ka_p = np.zeros((4, 64, 128, 512), np.float32)
    va_p = np.zeros((4, 64, 128, 512), np.float32)
    ki_p = np.zeros((4, 64, 128, 64), np.float32)
    kb_p = np.zeros((4, 4, 128, 512), np.float32)
    vb_p = np.zeros((4, 4, 128, 512), np.float32)
    y_s = np.zeros((8, 64, D), np.float32)
    ka_s = np.zeros((8, 64, 512), np.float32)
    va_s = np.zeros((8, 64, 512), np.float32)
    ki_s = np.zeros((8, 64, 64), np.float32)
    kb_s = np.zeros((8, 64, 512), np.float32)
    vb_s = np.zeros((8, 64, 512), np.float32)
    for c in range(8):
        b, hf = c // 2, c % 2
        r = R[c]
        y_p[b, hf::2] = r["y_o"][:4096].reshape(32, 128, D)
        ka_p[b, hf::2] = r["ka_o"][:4096].reshape(32, 128, 512)
        va_p[b, hf::2] = r["va_o"][:4096].reshape(32, 128, 512)
        ki_p[b, hf::2] = r["ki_o"][:4096].reshape(32, 128, 64)
        kb_p[b, hf::2] = r["kb_o"][:256].reshape(2, 128, 512)
        vb_p[b, hf::2] = r["vb_o"][:256].reshape(2, 128, 512)
        y_s[c] = r["y_o"][4096:4160]
        ka_s[c] = r["ka_o"][4096:4160]
        va_s[c] = r["va_o"][4096:4160]
        ki_s[c] = r["ki_o"][4096:4160]
        kb_s[c] = r["kb_o"][256:320]
        vb_s[c] = r["vb_o"][256:320]
    return (y_p.reshape(4, 8192, D), y_s,
            ka_p.reshape(1, 4, 8192, 8, 64), va_p.reshape(1, 4, 8192, 8, 64), ki_p.reshape(1, 4, 8192, 64),
            kb_p.reshape(1, 4, 512, 8, 64), vb_p.reshape(1, 4, 512, 8, 64),
            ka_s.reshape(1, 8, 64, 8, 64), va_s.reshape(1, 8, 64, 8, 64), ki_s.reshape(1, 8, 64, 64),
            kb_s.reshape(1, 8, 64, 8, 64), vb_s.reshape(1, 8, 64, 8, 64))
```

```python
import numpy as np
from contextlib import ExitStack
import concourse.bass as bass
import concourse.mybir as mybir
from concourse.bass_utils import run_bass_kernel_spmd

F32 = mybir.dt.float32
BF16 = mybir.dt.bfloat16
U8 = mybir.dt.uint8
ALU = mybir.AluOpType
AF = mybir.ActivationFunctionType
AX = mybir.AxisListType

D = 1024
NT = 65
NTS = 73
NSLOT = 33
DIN = 5704
KSTEPS = 16
NEG = -30000.0
EPS = 1e-6


class Buf:
    __slots__ = ("w", "r")

    def __init__(self):
        self.w = None
        self.r = {}


class Instr:
    __slots__ = ("eng", "fn", "deps", "signal", "sem", "val", "is_dma", "uid")


ENGS = ("pe", "act", "dve", "pool", "sp")
NRING = 8


class Sched:
    def __init__(self):
        self.q = {e: [] for e in ENGS}
        self.uid = 0
        self.bar = []
        self.bar_done = set()
        self.deferred = None

    def begin_defer(self):
        self.deferred = []

    def end_defer(self):
        ops = self.deferred
        self.deferred = None
        return ops

    def replay(self, ops, k):
        for _ in range(min(k, len(ops))):
            self.add(*ops.pop(0))

    def barrier(self):
        lst = []
        for e in ENGS:
            comp = [i for i in self.q[e] if not i.is_dma]
            if comp:
                lst.append(comp[-1])
            lst += [i for i in self.q[e] if i.is_dma][-NRING:]
        self.bar = lst
        self.bar_done = set()

    def add(self, eng, fn, reads=(), writes=(), dma=False):
        import os
        if self.deferred is not None:
            self.deferred.append((eng, fn, tuple(reads), tuple(writes), dma))
            return None
        if self.uid >= int(os.environ.get("KMAX", "100000000")):
            return None
        ins = Instr()
        ins.eng = eng
        ins.fn = fn
        ins.is_dma = dma
        ins.signal = dma
        ins.uid = self.uid
        self.uid += 1
        deps = {}

        def need(d, raw):
            if d is None or d is ins:
                return
            if (not d.is_dma) and (not dma) and d.eng == eng:
                if eng == "pe":
                    return
            deps[d.uid] = d

        for b in reads:
            need(b.w, True)
        for b in writes:
            need(b.w, False)
            for rd in b.r.values():
                need(rd, False)
        if self.bar and eng not in self.bar_done:
            self.bar_done.add(eng)
            for d in self.bar:
                deps[d.uid] = d
        for b in reads:
            b.r[("dma", ins.uid) if dma else eng] = ins
        for b in writes:
            b.w = ins
            b.r = {}
        ins.deps = list(deps.values())
        for d in ins.deps:
            d.signal = True
        self.q[eng].append(ins)
        return ins

    def pe(self, fn, reads=(), writes=()):
        return self.add("pe", fn, reads, writes)

    def act(self, fn, reads=(), writes=()):
        return self.add("act", fn, reads, writes)

    def dve(self, fn, reads=(), writes=()):
        return self.add("dve", fn, reads, writes)

    def pool(self, fn, reads=(), writes=()):
        return self.add("pool", fn, reads, writes)

    def dma(self, queue, out, in_, reads=(), writes=(), **kw):
        return self.add(queue, lambda e: e.dma_start(out=out, in_=in_, **kw), reads, writes, dma=True)

    def emit(self, nc, stack):
        esem = {e: stack.enter_context(nc.semaphore("s_" + e)) for e in ENGS}
        rings = {e: [stack.enter_context(nc.semaphore("r_%s%d" % (e, i))) for i in range(NRING)]
                 for e in ("sp", "pool", "act")}
        final_ring = {}
        for e in ENGS:
            cnt = 0
            nd = 0
            for ins in self.q[e]:
                if ins.is_dma:
                    ins.sem = rings[e][nd % NRING]
                    ins.val = 16 * (nd // NRING + 1)
                    final_ring[(e, nd % NRING)] = (ins.sem, ins.val)
                    nd += 1
                elif ins.signal:
                    cnt += 1
                    ins.sem = esem[e]
                    ins.val = cnt
        block = stack.enter_context(nc.Block())
        handles = {"pe": "tensor", "act": "scalar", "dve": "vector", "pool": "gpsimd", "sp": "sync"}

        def make(e):
            def body(h):
                waited = {}

                def wait(sem, val):
                    k = id(sem)
                    if waited.get(k, 0) >= val:
                        return
                    waited[k] = val
                    h.wait_ge(sem, val)

                nd = 0
                for ins in self.q[e]:
                    for d in ins.deps:
                        wait(d.sem, d.val)
                    if ins.is_dma:
                        if nd >= NRING:
                            wait(ins.sem, ins.val - 16)
                        nd += 1
                        ins.fn(h).then_inc(ins.sem, 16)
                    else:
                        r = ins.fn(h)
                        if ins.signal:
                            r.then_inc(ins.sem, 1)
                if e == "sp":
                    for (sem, val) in final_ring.values():
                        wait(sem, val)
            return body

        for e in ENGS:
            getattr(block, handles[e])(make(e))


def V(name, *a, **k):
    return lambda e: getattr(e, name)(*a, **k)


class T:
    __slots__ = ("ap", "b")

    def __init__(self, ap):
        self.ap = ap
        self.b = Buf()


_DSZ = {F32: 4, BF16: 2, U8: 1}


class Arena:
    def __init__(self, ap, base, limit):
        self.ap = ap
        self.off = base
        self.limit = limit

    def alloc(self, shape, dt):
        n = 1
        for s in shape[1:]:
            n *= s
        nb = (n * _DSZ[dt] + 31) // 32 * 32
        assert self.off + nb <= self.limit, ("arena overflow", self.off + nb, self.limit)
        v = self.ap[:, self.off // 2:(self.off + nb) // 2]
        self.off += nb
        if dt != BF16:
            v = v.bitcast(dt)
        v = v[:, 0:n]
        if len(shape) == 3:
            v = v.rearrange("p (a b) -> p a b", b=shape[2])
        elif len(shape) == 4:
            v = v.rearrange("p (a b c) -> p a b c", b=shape[2], c=shape[3])
        if shape[0] != 128:
            v = v[0:shape[0]]
        return T(v)


def build_program():
    nc = bass.Bass("TRN2", target_bir_lowering=False)

    def din(name, shape, dt=F32):
        return nc.dram_tensor(name, list(shape), dt, kind="ExternalInput").ap()

    def dout(name, shape, dt=F32):
        return nc.dram_tensor(name, list(shape), dt, kind="ExternalOutput").ap()

    def dscr(name, shape, dt):
        return nc.dram_tensor(name, list(shape), dt, kind="Internal").ap()

    x_all = din("x_all", [NT * 128, D])
    cos_all = din("cos_all", [NT * 128, 8])
    sin_all = din("sin_all", [NT * 128, 8])
    x_own = din("x_own", [NSLOT * 128, D])
    ck_a = din("ck_a", [1024, 512])
    cv_a = din("cv_a", [1024, 512])
    ck_i = din("ck_i", [1024, 64])
    ck_b = din("ck_b", [512, 512])
    cv_b = din("cv_b", [512, 512])
    w_in = din("w_in", [D, DIN])
    w_oa = din("w_oa", [512, D])
    w_ob = din("w_ob", [512, D])
    w_out = din("w_out", [D, D])
    w_up = din("w_up", [D, 4096])
    w_down = din("w_down", [4096, D])
    gmix_d = din("gmix", [128, 8])
    gffn_d = din("gffn", [128, 8])
    g_kab_d = din("g_kab", [128, 1024])
    g_qab_d = din("g_qab", [128, 1024])
    g_ki_d = din("g_ki", [128, 64])
    ident_d = din("ident", [128, 128])
    pow2_d = din("pow2", [128, 2 * (KSTEPS + 2)])
    mask_d = din("masks", [128, 256])
    bandp_d = din("band_p", [8, 128, 768])
    bands_d = din("band_s", [8, 128, 768])

    y_o = dout("y_o", [NSLOT * 128, D])
    ka_o = dout("ka_o", [NSLOT * 128, 512])
    va_o = dout("va_o", [NSLOT * 128, 512])
    ki_o = dout("ki_o", [NSLOT * 128, 64])
    kb_o = dout("kb_o", [3 * 128, 512])
    vb_o = dout("vb_o", [3 * 128, 512])

    kT_scr = dscr("kT_scr", [NTS, 128, 8, 128], BF16)
    v_scr = dscr("v_scr", [NTS, 128, 16, 65], BF16)
    q_scr = dscr("q_scr", [NSLOT, 128, 8, 128], BF16)
    qi_scr = dscr("qi_scr", [NSLOT, 128, 4, 128], BF16)
    gate_scr = dscr("gate_scr", [NSLOT, 128, 2048], BF16)
    wi_scr = dscr("wi_scr", [NSLOT, 128, 8], F32)
    oab_scr = dscr("oab_scr", [NSLOT, 128, 1024], BF16)
    wbf_scr = dscr("wbf_scr", [128, 80 * 1024], BF16)
    wscr_b = Buf()
    kscr_b = [Buf() for _ in range(NTS)]
    vscr_b = [Buf() for _ in range(NTS)]
    qscr_b = [Buf() for _ in range(NSLOT)]
    oscr_b = [Buf() for _ in range(NSLOT)]

    S = Sched()
    with ExitStack() as st:
        ARENA_BYTES = 206 * 1024
        arena_t = st.enter_context(nc.sbuf_tensor("arena", [128, ARENA_BYTES // 2], BF16))
        PS = st.enter_context(nc.psum_tensor("ps", [128, 8, 512], F32))
        psb = [Buf() for _ in range(8)]

        def PSb16(bank):
            return PS[:, bank, :].bitcast(BF16)

        A0 = Arena(arena_t, 0, ARENA_BYTES)
        ident_f = A0.alloc([128, 128], F32)
        ident_b = A0.alloc([128, 128], BF16)
        I4 = A0.alloc([128, 512], BF16)
        gmix = A0.alloc([128, 8], F32)
        gffn = A0.alloc([128, 8], F32)
        pow2 = A0.alloc([128, 2 * (KSTEPS + 2)], F32)
        masks = A0.alloc([128, 256], F32)
        kiT = A0.alloc([128, NTS * 128], BF16)
        kiT_b = [Buf() for _ in range(NTS)]
        PBASE = A0.off

        S.dma("sp", ident_f.ap, ident_d, writes=[ident_f.b])
        S.dma("sp", gmix.ap, gmix_d, writes=[gmix.b])
        S.dma("sp", gffn.ap, gffn_d, writes=[gffn.b])
        S.dma("sp", pow2.ap, pow2_d, writes=[pow2.b])
        S.dma("sp", masks.ap, mask_d, writes=[masks.b])
        S.dve(V("tensor_copy", out=ident_b.ap, in_=ident_f.ap), reads=[ident_f.b], writes=[ident_b.b])
        for j in range(4):
            S.dve(V("tensor_copy", out=I4.ap[:, j * 128:(j + 1) * 128], in_=ident_f.ap), reads=[ident_f.b], writes=[I4.b])

        A1 = Arena(arena_t, PBASE, ARENA_BYTES)
        w1 = A1.alloc([128, 8, DIN], BF16)
        g_kab = A1.alloc([128, 1024], F32)
        g_qab = A1.alloc([128, 1024], F32)
        g_ki = A1.alloc([128, 64], F32)
        xt = [A1.alloc([128, D], F32) for _ in range(2)]
        xT = [A1.alloc([128, 8, 128], BF16) for _ in range(2)]
        xT_b = [[Buf() for _ in range(8)] for _ in range(2)]
        cs = [A1.alloc([128, 8], F32) for _ in range(3)]
        sn = [A1.alloc([128, 8], F32) for _ in range(3)]
        ssx = [A1.alloc([128, 1], F32) for _ in range(2)]
        rstd = [A1.alloc([128, 1], F32) for _ in range(2)]
        z_k = [A1.alloc([128, 1024], F32) for _ in range(2)]
        z_v = [A1.alloc([128, 1024], F32) for _ in range(2)]
        z_q = [A1.alloc([128, 1024], F32) for _ in range(2)]
        z_qi = [A1.alloc([128, 512], F32) for _ in range(2)]
        z_ki = [A1.alloc([128, 64], F32) for _ in range(2)]
        wi_sb = [A1.alloc([128, 8], F32) for _ in range(2)]
        sq_tmps = [A1.alloc([128, 1024], F32) for _ in range(2)]
        hs_k = [A1.alloc([128, 16], F32) for _ in range(2)]
        hs_q = [A1.alloc([128, 16], F32) for _ in range(2)]
        hs_i = [A1.alloc([128, 1], F32) for _ in range(2)]
        knbs = [A1.alloc([128, 1024], BF16) for _ in range(2)]
        qnb = A1.alloc([128, 1024], BF16)
        qib = A1.alloc([128, 512], BF16)
        kibs = [A1.alloc([128, 128], BF16) for _ in range(2)]
        kT_sbs = [A1.alloc([128, 8, 128], BF16) for _ in range(2)]
        qT_sb = A1.alloc([128, 8, 128], BF16)
        qiT_sb = A1.alloc([128, 4, 128], BF16)
        vaug = [A1.alloc([128, 16, 65], BF16) for _ in range(2)]
        gates = A1.alloc([128, 2048], BF16)
        rt = [A1.alloc([128, 16, 8], F32) for _ in range(4)]

        w1_b = [Buf() for _ in range(3)]
        wstg = [A1.alloc([128, 1024], F32) for _ in range(2)]
        wj = 0
        for c0 in range(0, DIN, 1024):
            c1 = min(DIN, c0 + 1024)
            n = c1 - c0
            for c in range(8):
                stg = wstg[wj % 2]
                S.dma("sp", stg.ap[:, 0:n], w_in[c * 128:(c + 1) * 128, c0:c1], writes=[stg.b])
                if wj % 2 == 0:
                    S.dve(V("tensor_copy", out=w1.ap[:, c, c0:c1], in_=stg.ap[:, 0:n]), reads=[stg.b], writes=[w1_b[c0 // 2048]])
                else:
                    S.act(V("activation", out=w1.ap[:, c, c0:c1], in_=stg.ap[:, 0:n], func=AF.Copy), reads=[stg.b], writes=[w1_b[c0 // 2048]])
                wj += 1
        S.dma("sp", g_kab.ap, g_kab_d, writes=[g_kab.b])
        S.dma("sp", g_qab.ap, g_qab_d, writes=[g_qab.b])
        S.dma("sp", g_ki.ap, g_ki_d, writes=[g_ki.b])
        for i in range(2):
            S.pool(V("memset", vaug[i].ap[:, :, 64:65], 1.0), writes=[vaug[i].b])

        zrot = [0]
        trot = [0]

        def headnorm(z, nh, gains, hs, sq_tmp):
            n = nh * 64
            S.act(V("activation", out=sq_tmp.ap[:, 0:n], in_=z.ap[:, 0:n], func=AF.Square), reads=[z.b], writes=[sq_tmp.b])
            S.dve(V("tensor_reduce", out=hs.ap[:, 0:nh], in_=sq_tmp.ap[:, 0:n].rearrange("p (h d) -> p h d", d=64),
                    axis=AX.X, op=ALU.add), reads=[sq_tmp.b], writes=[hs.b])
            S.dve(V("tensor_scalar", out=hs.ap[:, 0:nh], in0=hs.ap[:, 0:nh], scalar1=1.0 / 64, scalar2=EPS,
                    op0=ALU.mult, op1=ALU.add), reads=[hs.b], writes=[hs.b])
            S.act(V("activation", out=hs.ap[:, 0:nh], in_=hs.ap[:, 0:nh], func=AF.Sqrt), reads=[hs.b], writes=[hs.b])
            S.dve(V("reciprocal", out=hs.ap[:, 0:nh], in_=hs.ap[:, 0:nh]), reads=[hs.b], writes=[hs.b])
            zv = z.ap[:, 0:n].rearrange("p (h d) -> p h d", d=64)
            S.dve(V("tensor_tensor", out=zv, in0=zv, in1=hs.ap[:, 0:nh].unsqueeze(2).to_broadcast([128, nh, 64]),
                    op=ALU.mult), reads=[z.b, hs.b], writes=[z.b])
            S.dve(V("tensor_tensor", out=z.ap[:, 0:n], in0=z.ap[:, 0:n], in1=gains.ap, op=ALU.mult),
                  reads=[z.b, gains.b], writes=[z.b])

        def rope(z, col0, nh, cs_t, sn_t):
            v = z.ap[:, col0:col0 + nh * 64].rearrange("p (h d) -> p h d", d=64)
            x1 = v[:, :, 0:8]
            x2 = v[:, :, 8:16]
            cb = cs_t.ap.unsqueeze(1).to_broadcast([128, nh, 8])
            sb_ = sn_t.ap.unsqueeze(1).to_broadcast([128, nh, 8])
            t = [r.ap[:, 0:nh, :] for r in rt]
            rd = [z.b, cs_t.b, sn_t.b]
            S.pool(V("tensor_tensor", out=t[0], in0=x1, in1=cb, op=ALU.mult), reads=rd, writes=[rt[0].b])
            S.pool(V("tensor_tensor", out=t[1], in0=x2, in1=sb_, op=ALU.mult), reads=rd, writes=[rt[1].b])
            S.pool(V("tensor_tensor", out=t[2], in0=x2, in1=cb, op=ALU.mult), reads=rd, writes=[rt[2].b])
            S.pool(V("tensor_tensor", out=t[3], in0=x1, in1=sb_, op=ALU.mult), reads=rd, writes=[rt[3].b])
            S.pool(V("tensor_tensor", out=x1, in0=t[0], in1=t[1], op=ALU.subtract), reads=[rt[0].b, rt[1].b], writes=[z.b])
            S.pool(V("tensor_tensor", out=x2, in0=t[2], in1=t[3], op=ALU.add), reads=[rt[2].b, rt[3].b], writes=[z.b])

        def transposes_bf(src, ncol_blocks, dst, dst_c0):
            bank = 6 + (trot[0] % 2)
            trot[0] += 1
            pv = PSb16(bank)
            for c in range(ncol_blocks):
                S.pe(V("transpose", out=pv[:, c * 128:(c + 1) * 128], in_=src.ap[:, c * 128:(c + 1) * 128],
                       identity=ident_b.ap), reads=[src.b, ident_b.b], writes=[psb[bank]])
            S.dve(V("tensor_copy", out=dst.ap[:, dst_c0:dst_c0 + ncol_blocks, :],
                    in_=pv[:, 0:ncol_blocks * 128].rearrange("p (c k) -> p c k", k=128)),
                  reads=[psb[bank]], writes=[dst.b])

        def ki_to_kiT(src_f32_ap, src_b, tidx, kib):
            S.pool(V("tensor_copy", out=kib.ap[:, 0:64], in_=src_f32_ap), reads=[src_b], writes=[kib.b])
            S.pool(V("tensor_copy", out=kib.ap[:, 64:128], in_=src_f32_ap), reads=[src_b], writes=[kib.b])
            bank = 6 + (trot[0] % 2)
            trot[0] += 1
            pv = PSb16(bank)
            S.pe(V("transpose", out=pv[:, 0:128], in_=kib.ap, identity=ident_b.ap), reads=[kib.b, ident_b.b], writes=[psb[bank]])
            S.dve(V("tensor_copy", out=kiT.ap[:, tidx * 128:(tidx + 1) * 128], in_=pv[:, 0:128]),
                  reads=[psb[bank]], writes=[kiT_b[tidx]])

        KCH = [(0, 512, "ka"), (512, 1024, "kb"), (1024, 1536, "va"), (1536, 2048, "vb"), (2048, 2112, "ki")]
        QCH = [(2112, 2624, "qa"), (2624, 3136, "qb"), (3136, 3648, "qi"), (3648, 4160, "g0"), (4160, 4672, "g1"),
               (4672, 5184, "g2"), (5184, 5696, "g3"), (5696, 5704, "wi")]

        def p1_load(t):
            p = t % 2
            S.dma("sp", xt[p].ap, x_all[t * 128:(t + 1) * 128, :], writes=[xt[p].b])
            S.dma("sp", cs[t % 3].ap, cos_all[t * 128:(t + 1) * 128, :], writes=[cs[t % 3].b])
            S.dma("sp", sn[t % 3].ap, sin_all[t * 128:(t + 1) * 128, :], writes=[sn[t % 3].b])

        def p1_front(t, after_chunk=None):
            own = (t % 2 == 0)
            slot = t // 2
            p = t % 2
            po = slot % 2
            xb = xt[p]
            sq_tmp = sq_tmps[p]
            S.act(V("activation", out=sq_tmp.ap, in_=xb.ap, func=AF.Square, accum_out=ssx[p].ap),
                  reads=[xb.b], writes=[sq_tmp.b, ssx[p].b])
            S.dve(V("tensor_scalar", out=rstd[p].ap, in0=ssx[p].ap, scalar1=1.0 / D, scalar2=EPS, op0=ALU.mult, op1=ALU.add),
                  reads=[ssx[p].b], writes=[rstd[p].b])
            S.act(V("activation", out=rstd[p].ap, in_=rstd[p].ap, func=AF.Sqrt), reads=[rstd[p].b], writes=[rstd[p].b])
            S.dve(V("reciprocal", out=rstd[p].ap, in_=rstd[p].ap), reads=[rstd[p].b], writes=[rstd[p].b])
            for c in range(8):
                bank = c // 4
                S.pe(V("transpose", out=PS[:, bank, (c % 4) * 128:(c % 4 + 1) * 128], in_=xb.ap[:, c * 128:(c + 1) * 128],
                       identity=ident_f.ap), reads=[xb.b, ident_f.b], writes=[psb[bank]])
            for c in range(8):
                bank = c // 4
                src = PS[:, bank, (c % 4) * 128:(c % 4 + 1) * 128]
                S.dve(V("tensor_scalar", out=xT[p].ap[:, c, :], in0=src, scalar1=gmix.ap[:, c:c + 1], scalar2=None,
                        op0=ALU.mult), reads=[psb[bank], gmix.b], writes=[xT_b[p][c]])
            if t + 1 < NT1:
                p1_load(t + 1)
            chunks = KCH + (QCH if own else [])
            for (c0, c1, kind) in chunks:
                n = c1 - c0
                bank = 2 + (zrot[0] % 4)
                zrot[0] += 1
                for c in range(8):
                    S.pe(V("matmul", PS[:, bank, 0:n], lhsT=xT[p].ap[:, c, :], rhs=w1.ap[:, c, c0:c1], start=(c == 0), stop=(c == 7)),
                         reads=[xT_b[p][c], w1_b[c0 // 2048], w1_b[(c1 - 1) // 2048]], writes=[psb[bank]])
                src = PS[:, bank, 0:n]
                rd = [psb[bank], rstd[p].b]
                sc = rstd[p].ap
                if kind == "ka":
                    S.act(V("activation", out=z_k[p].ap[:, 0:512], in_=src, func=AF.Copy, scale=sc), reads=rd, writes=[z_k[p].b])
                elif kind == "kb":
                    S.act(V("activation", out=z_k[p].ap[:, 512:1024], in_=src, func=AF.Copy, scale=sc), reads=rd, writes=[z_k[p].b])
                elif kind == "va":
                    S.act(V("activation", out=z_v[p].ap[:, 0:512], in_=src, func=AF.Copy, scale=sc), reads=rd, writes=[z_v[p].b])
                elif kind == "vb":
                    S.act(V("activation", out=z_v[p].ap[:, 512:1024], in_=src, func=AF.Copy, scale=sc), reads=rd, writes=[z_v[p].b])
                elif kind == "ki":
                    S.act(V("activation", out=z_ki[p].ap, in_=src, func=AF.Copy, scale=sc), reads=rd, writes=[z_ki[p].b])
                elif kind == "qa":
                    S.act(V("activation", out=z_q[po].ap[:, 0:512], in_=src, func=AF.Copy, scale=sc), reads=rd, writes=[z_q[po].b])
                elif kind == "qb":
                    S.act(V("activation", out=z_q[po].ap[:, 512:1024], in_=src, func=AF.Copy, scale=sc), reads=rd, writes=[z_q[po].b])
                elif kind == "qi":
                    S.act(V("activation", out=z_qi[po].ap, in_=src, func=AF.Copy, scale=sc), reads=rd, writes=[z_qi[po].b])
                elif kind[0] == "g":
                    j = int(kind[1])
                    S.act(V("activation", out=gates.ap[:, j * 512:(j + 1) * 512], in_=src, func=AF.Sigmoid, scale=sc), reads=rd, writes=[gates.b])
                elif kind == "wi":
                    S.act(V("activation", out=wi_sb[po].ap, in_=src, func=AF.Copy, scale=sc), reads=rd, writes=[wi_sb[po].b])
                if after_chunk is not None:
                    after_chunk()
        def p1_back(t):
            own = (t % 2 == 0)
            slot = t // 2
            p = t % 2
            po = slot % 2
            sq_tmp = sq_tmps[p]
            knb = knbs[p]
            kib = kibs[p]
            kT_sb = kT_sbs[p]
            headnorm(z_k[p], 16, g_kab, hs_k[p], sq_tmp)
            rope(z_k[p], 0, 8, cs[t % 3], sn[t % 3])
            headnorm(z_ki[p], 1, g_ki, hs_i[p], sq_tmp)
            rope(z_ki[p], 0, 1, cs[t % 3], sn[t % 3])
            if own:
                r0 = slot * 128
                S.dma("sp", ka_o[r0:r0 + 128, :], z_k[p].ap[:, 0:512], reads=[z_k[p].b])
                S.dma("sp", va_o[r0:r0 + 128, :], z_v[p].ap[:, 0:512], reads=[z_v[p].b])
                S.dma("sp", ki_o[r0:r0 + 128, :], z_ki[p].ap, reads=[z_ki[p].b])
                if slot >= 30:
                    rb = (slot - 30) * 128
                    S.dma("sp", kb_o[rb:rb + 128, :], z_k[p].ap[:, 512:1024], reads=[z_k[p].b])
                    S.dma("sp", vb_o[rb:rb + 128, :], z_v[p].ap[:, 512:1024], reads=[z_v[p].b])
            S.dve(V("tensor_copy", out=knb.ap, in_=z_k[p].ap), reads=[z_k[p].b], writes=[knb.b])
            transposes_bf(knb, 8, kT_sb, 0)
            S.dma("sp", kT_scr[t], kT_sb.ap, reads=[kT_sb.b], writes=[kscr_b[t]])
            S.act(V("activation", out=vaug[p].ap[:, :, 0:64], in_=z_v[p].ap.rearrange("p (h d) -> p h d", d=64), func=AF.Copy),
                  reads=[z_v[p].b], writes=[vaug[p].b])
            S.dma("sp", v_scr[t], vaug[p].ap, reads=[vaug[p].b], writes=[vscr_b[t]])
            ki_to_kiT(z_ki[p].ap, z_ki[p].b, t, kib)
            if own:
                headnorm(z_q[po], 16, g_qab, hs_q[po], sq_tmp)
                rope(z_q[po], 0, 8, cs[t % 3], sn[t % 3])
                rope(z_qi[po], 0, 8, cs[t % 3], sn[t % 3])
                S.dve(V("tensor_copy", out=qnb.ap, in_=z_q[po].ap), reads=[z_q[po].b], writes=[qnb.b])
                transposes_bf(qnb, 8, qT_sb, 0)
                S.dma("sp", q_scr[slot], qT_sb.ap, reads=[qT_sb.b], writes=[qscr_b[slot]])
                S.dve(V("tensor_copy", out=qib.ap, in_=z_qi[po].ap), reads=[z_qi[po].b], writes=[qib.b])
                transposes_bf(qib, 4, qiT_sb, 0)
                S.dma("sp", qi_scr[slot], qiT_sb.ap, reads=[qiT_sb.b], writes=[qscr_b[slot]])
                S.dma("sp", gate_scr[slot], gates.ap, reads=[gates.b], writes=[qscr_b[slot]])
                S.dma("sp", wi_scr[slot], wi_sb[po].ap, reads=[wi_sb[po].b], writes=[qscr_b[slot]])

        import os
        PH = os.environ.get("KPH", "123")
        NT1 = int(os.environ.get("KNT", NT))
        for m in (range(8) if "c" in PH or "2" in PH else []):
            p = m % 2
            tidx = 65 + m
            xb = xt[p]
            knb = knbs[p]
            kT_sb = kT_sbs[p]
            S.dma("sp", xb.ap[:, 0:512], ck_a[m * 128:(m + 1) * 128, :], writes=[xb.b])
            S.dma("sp", xb.ap[:, 512:1024], cv_a[m * 128:(m + 1) * 128, :], writes=[xb.b])
            S.dve(V("tensor_copy", out=knb.ap[:, 0:512], in_=xb.ap[:, 0:512]), reads=[xb.b], writes=[knb.b])
            S.pool(V("tensor_copy", out=vaug[p].ap[:, 0:8, 0:64], in_=xb.ap[:, 512:1024].rearrange("p (h d) -> p h d", d=64)),
                   reads=[xb.b], writes=[vaug[p].b])
            if m >= 4:
                zb = z_k[p]
                S.dma("sp", zb.ap[:, 0:512], ck_b[(m - 4) * 128:(m - 3) * 128, :], writes=[zb.b])
                S.dma("sp", zb.ap[:, 512:1024], cv_b[(m - 4) * 128:(m - 3) * 128, :], writes=[zb.b])
                S.dve(V("tensor_copy", out=knb.ap[:, 512:1024], in_=zb.ap[:, 0:512]), reads=[zb.b], writes=[knb.b])
                S.pool(V("tensor_copy", out=vaug[p].ap[:, 8:16, 0:64], in_=zb.ap[:, 512:1024].rearrange("p (h d) -> p h d", d=64)),
                       reads=[zb.b], writes=[vaug[p].b])
            nb_ = 8 if m >= 4 else 4
            transposes_bf(knb, nb_, kT_sb, 0)
            S.dma("sp", kT_scr[tidx][:, 0:nb_, :], kT_sb.ap[:, 0:nb_, :], reads=[kT_sb.b], writes=[kscr_b[tidx]])
            S.dma("sp", v_scr[tidx][:, 0:2 * nb_, :], vaug[p].ap[:, 0:2 * nb_, :], reads=[vaug[p].b], writes=[vscr_b[tidx]])
            zi = z_ki[p]
            S.dma("sp", zi.ap, ck_i[m * 128:(m + 1) * 128, :], writes=[zi.b])
            ki_to_kiT(zi.ap, zi.b, tidx, kibs[p])

        p1_load(0)
        p1_front(0)
        for t in range(NT1):
            S.begin_defer()
            p1_back(t)
            ops = S.end_defer()
            if t + 1 < NT1:
                nch = 13 if (t + 1) % 2 == 0 else 5
                per = (len(ops) + nch - 1) // nch
                p1_front(t + 1, after_chunk=lambda: S.replay(ops, per))
            S.replay(ops, len(ops))

        S.barrier()
        A2 = Arena(arena_t, PBASE, ARENA_BYTES)
        NMAX = 64 * 128
        isc = [A2.alloc([128, NMAX], F32) for _ in range(2)]
        junk = A2.alloc([128, NMAX], U8)
        mb = [A2.alloc([128, NMAX], BF16) for _ in range(2)]
        kbuf = [A2.alloc([128, 4, 4, 128], BF16) for _ in range(2)]
        vbuf = [A2.alloc([128, 4, 8, 65], BF16) for _ in range(2)]
        bandb = A2.alloc([128, 8, 768], BF16)
        bstage = A2.alloc([128, 768], F32)
        qTb = [A2.alloc([128, 16, 128], BF16) for _ in range(2)]
        qiTb = [A2.alloc([128, 4, 128], BF16) for _ in range(2)]
        wib = [A2.alloc([128, 8], F32) for _ in range(2)]
        diag = [A2.alloc([128, 8, 128], BF16) for _ in range(2)]
        Rb = [A2.alloc([128, 512], BF16) for _ in range(5)]
        pTb = [A2.alloc([128, 512], BF16) for _ in range(5)]
        LAG = 3
        SBANKS = [3, 4, 7, 0, 1]
        HBANKS = [0, 1, 3, 4, 7]
        oab = [A2.alloc([128, 1024], BF16) for _ in range(2)]
        bst = [A2.alloc([128, 4 * (KSTEPS + 2)], F32) for _ in range(2)]
        lnb = A2.alloc([128, 8], F32)
        rsb = [A2.alloc([128, 1], F32) for _ in range(8)]
        cstg = [A2.alloc([128, 1024], F32) for _ in range(2)]
        cout = [A2.alloc([128, 1024], BF16) for _ in range(2)]
        conv_i = [0]

        def conv_src(i):
            if i < 4:
                return w_oa[i * 128:(i + 1) * 128, :]
            if i < 8:
                return w_ob[(i - 4) * 128:(i - 3) * 128, :]
            if i < 16:
                return w_out[(i - 8) * 128:(i - 7) * 128, :]
            if i < 48:
                c, j = (i - 16) // 4, (i - 16) % 4
                return w_up[c * 128:(c + 1) * 128, j * 1024:(j + 1) * 1024]
            return w_down[(i - 48) * 128:(i - 47) * 128, :]

        def conv_steps(k):
            for _ in range(k):
                i = conv_i[0]
                if i >= 80:
                    return
                conv_i[0] += 1
                S.dma("pool", cstg[i % 2].ap, conv_src(i), writes=[cstg[i % 2].b])
                S.pool(V("tensor_copy", out=cout[i % 2].ap, in_=cstg[i % 2].ap), reads=[cstg[i % 2].b], writes=[cout[i % 2].b])
                S.dma("pool", wbf_scr[:, i * 1024:(i + 1) * 1024], cout[i % 2].ap, reads=[cout[i % 2].b], writes=[wscr_b])

        rot = {"sh": 0, "R": 0, "st": 0, "pT": 0, "kv": 0}
        K1 = KSTEPS + 2

        def load_band(src_d):
            for h in range(8):
                S.dma("sp", bstage.ap, src_d[h], writes=[bstage.b])
                S.act(V("activation", out=bandb.ap[:, h, :], in_=bstage.ap, func=AF.Copy, scale=8.0), reads=[bstage.b], writes=[bandb.b])

        def groups_of(tiles):
            gs = []
            for pos, t in enumerate(tiles):
                if gs and len(gs[-1]) < 4 and gs[-1][-1][1] + 1 == t:
                    gs[-1].append((pos, t))
                else:
                    gs.append([(pos, t)])
            return gs

        def p2_load_idx(s):
            b = s % 2
            S.dma("sp", qiTb[b].ap, qi_scr[s], reads=[qscr_b[s]], writes=[qiTb[b].b])
            S.dma("sp", wib[b].ap, wi_scr[s], reads=[qscr_b[s]], writes=[wib[b].b])
            for h in range(8):
                S.dve(V("tensor_scalar", out=diag[b].ap[:, h, :], in0=ident_b.ap, scalar1=wib[b].ap[:, h:h + 1], scalar2=None, op0=ALU.mult),
                      reads=[ident_b.b, wib[b].b], writes=[diag[b].b])

        def p2_load_q(s):
            b = s % 2
            S.dma("sp", qTb[b].ap[0:64, 0::2, :], q_scr[s][0:64, :, :], reads=[qscr_b[s]], writes=[qTb[b].b])
            S.dma("sp", qTb[b].ap[64:128, 1::2, :], q_scr[s][64:128, :, :], reads=[qscr_b[s]], writes=[qTb[b].b])

        def p2_index(s, tiles):
            b = s % 2
            steps = [(g, h) for g in groups_of(tiles) for h in range(8)]
            q = []

            def emit_d(pd):
                (pg, ph, pR, pn) = pd
                S.pe(V("matmul", PS[:, 2, 0:pn], lhsT=diag[b].ap[:, ph, :], rhs=pR.ap[:, 0:pn], start=(ph == 0), stop=(ph == 7)),
                     reads=[diag[b].b, pR.b], writes=[psb[2]])
                if ph == 7:
                    c0 = pg[0][0] * 128
                    S.act(V("activation", out=isc[b].ap[:, c0:c0 + pn], in_=PS[:, 2, 0:pn], func=AF.Copy),
                          reads=[psb[2]], writes=[isc[b].b])

            for (g, h) in steps:
                n = len(g) * 128
                t0 = g[0][1]
                bank = HBANKS[rot["sh"] % 5]
                rot["sh"] += 1
                base = 64 * (h % 2)
                S.pe(V("matmul", PS[:, bank, 0:n], lhsT=qiTb[b].ap[base:base + 64, h // 2, :],
                       rhs=kiT.ap[base:base + 64, t0 * 128:t0 * 128 + n], start=True, stop=True),
                     reads=[qiTb[b].b] + [kiT_b[t] for (_, t) in g], writes=[psb[bank]])
                R = Rb[rot["R"] % 5]
                rot["R"] += 1
                S.act(V("activation", out=R.ap[:, 0:n], in_=PS[:, bank, 0:n], func=AF.Relu), reads=[psb[bank]], writes=[R.b])
                q.append((g, h, R, n))
                if len(q) > LAG:
                    emit_d(q.pop(0))
            while q:
                emit_d(q.pop(0))

        def p2_bisect(s, tiles, mask_list):
            b = s % 2
            N = len(tiles) * 128
            X = isc[b]
            st_ = bst[b]
            amax = st_.ap[:, 0:1]
            wk = st_.ap[:, K1:2 * K1]
            mid = st_.ap[:, 2 * K1:3 * K1]
            cnt = st_.ap[:, 3 * K1:4 * K1]
            w2 = st_.ap[:, 1:K1]
            S.dve(V("tensor_reduce", out=amax, in_=X.ap[:, 0:N], axis=AX.X, op=ALU.max, apply_absolute_value=True),
                  reads=[X.b], writes=[st_.b])
            for (pos, mcol) in mask_list:
                S.dve(V("tensor_tensor", out=X.ap[:, pos * 128:(pos + 1) * 128], in0=X.ap[:, pos * 128:(pos + 1) * 128],
                        in1=masks.ap[:, mcol * 128:(mcol + 1) * 128], op=ALU.add), reads=[X.b, masks.b], writes=[X.b])
            S.dve(V("tensor_scalar", out=wk, in0=pow2.ap[:, 0:K1], scalar1=amax, scalar2=None, op0=ALU.mult),
                  reads=[st_.b, pow2.b], writes=[st_.b])
            S.dve(V("tensor_scalar", out=w2, in0=pow2.ap[:, K1 + 1:2 * K1], scalar1=amax, scalar2=None, op0=ALU.mult),
                  reads=[st_.b, pow2.b], writes=[st_.b])
            S.dve(V("memset", mid[:, 0:1], 0.0), writes=[st_.b])
            for k in range(KSTEPS):
                S.dve(V("tensor_scalar", out=junk.ap[:, 0:N], in0=X.ap[:, 0:N], scalar1=mid[:, k:k + 1], scalar2=None,
                        op0=ALU.is_ge, op1=ALU.add, accum_out=cnt[:, k:k + 1]), reads=[X.b, st_.b], writes=[junk.b, st_.b])
                S.dve(V("scalar_tensor_tensor", out=cnt[:, k:k + 1], in0=cnt[:, k:k + 1], scalar=255.5, in1=w2[:, k:k + 1],
                        op0=ALU.is_ge, op1=ALU.mult), reads=[st_.b], writes=[st_.b])
                S.dve(V("scalar_tensor_tensor", out=mid[:, k + 1:k + 2], in0=mid[:, k:k + 1], scalar=wk[:, k + 1:k + 2],
                        in1=cnt[:, k:k + 1], op0=ALU.subtract, op1=ALU.add), reads=[st_.b], writes=[st_.b])
            thr = cnt[:, KSTEPS:KSTEPS + 1]
            S.dve(V("tensor_tensor", out=thr, in0=mid[:, KSTEPS:KSTEPS + 1], in1=wk[:, KSTEPS:KSTEPS + 1], op=ALU.subtract),
                  reads=[st_.b], writes=[st_.b])
            S.dve(V("tensor_scalar", out=mb[b].ap[:, 0:N], in0=X.ap[:, 0:N], scalar1=thr, scalar2=NEG, op0=ALU.is_lt, op1=ALU.mult),
                  reads=[X.b, st_.b], writes=[mb[b].b])

        def attend(s, tiles, pair0, vh0, bias_mode):
            b = s % 2
            qT = qTb[b]
            steps = []
            for g in groups_of(tiles):
                kv = rot["kv"] % 2
                rot["kv"] += 1
                ng = len(g)
                t0 = g[0][1]
                steps.append(("load", g, kv))
                for j, (pos, t) in enumerate(g):
                    for hg in range(2):
                        steps.append(("qk", pos, j, hg, kv))
            first = [True, True]
            nsteps_left = [sum(1 for x in steps if x[0] == "qk" and x[3] == hg) for hg in range(2)]
            pendq = []

            def emit_pv(pd):
                (pos, j, hg, kv, pt) = pd
                nsteps_left[hg] -= 1
                for hh in range(4):
                    h = 4 * hg + hh
                    S.pe(V("matmul", PS[:, 5 + hg, hh * 65:(hh + 1) * 65], lhsT=pt.ap[:, hh * 128:(hh + 1) * 128],
                           rhs=vbuf[kv].ap[:, j, h, :], start=(first[hg] and hh == 0), stop=(nsteps_left[hg] == 0),
                           skip_group_check=True), reads=[pt.b, vbuf[kv].b], writes=[psb[5 + hg]])
                first[hg] = False

            for stp in steps:
                if stp[0] == "load":
                    (_, g, kv) = stp
                    ng = len(g)
                    t0 = g[0][1]
                    S.dma("sp", kbuf[kv].ap[:, 0:ng], kT_scr[t0:t0 + ng].rearrange("t p c k -> p t c k")[:, :, pair0:pair0 + 4, :],
                          reads=[kscr_b[t] for (_, t) in g], writes=[kbuf[kv].b])
                    S.dma("sp", vbuf[kv].ap[:, 0:ng], v_scr[t0:t0 + ng].rearrange("t p h d -> p t h d")[:, :, vh0:vh0 + 8, :],
                          reads=[vscr_b[t] for (_, t) in g], writes=[vbuf[kv].b])
                    continue
                (_, pos, j, hg, kv) = stp
                sbank = SBANKS[rot["st"] % 5]
                rot["st"] += 1
                if bias_mode is None:
                    S.pe(V("matmul", PS[:, sbank, :], lhsT=mb[b].ap[:, pos * 128:(pos + 1) * 128], rhs=I4.ap, start=True, stop=False,
                           skip_group_check=True), reads=[mb[b].b, I4.b], writes=[psb[sbank]])
                for hh in range(4):
                    h = 4 * hg + hh
                    pr = h // 2
                    base = 64 * (h % 2)
                    if bias_mode is not None:
                        m = bias_mode[pos]
                        S.pe(V("matmul", PS[:, sbank, hh * 128:(hh + 1) * 128], lhsT=bandb.ap[:, h, m * 128:(m + 1) * 128], rhs=ident_b.ap,
                               start=True, stop=False, skip_group_check=True), reads=[bandb.b, ident_b.b], writes=[psb[sbank]])
                    S.pe(V("matmul", PS[:, sbank, hh * 128:(hh + 1) * 128], lhsT=kbuf[kv].ap[:, j, pr, :],
                           rhs=qT.ap[:, 2 * pair0 + h, :], start=False, stop=True, skip_group_check=True),
                         reads=[kbuf[kv].b, qT.b], writes=[psb[sbank]])
                pt = pTb[rot["pT"] % 5]
                rot["pT"] += 1
                S.act(V("activation", out=pt.ap, in_=PS[:, sbank, :], func=AF.Exp, scale=0.125), reads=[psb[sbank]], writes=[pt.b])
                pendq.append((pos, j, hg, kv, pt))
                if len(pendq) > LAG:
                    emit_pv(pendq.pop(0))
            while pendq:
                emit_pv(pendq.pop(0))
            col0 = 0 if bias_mode is None else 512
            for hg in range(2):
                ov = PS[:, 5 + hg, 0:260].rearrange("p (h d) -> p h d", d=65)
                S.act(V("activation", out=lnb.ap[:, hg * 4:(hg + 1) * 4], in_=ov[:, :, 64], func=AF.Ln), reads=[psb[5 + hg]], writes=[lnb.b])
                for hh in range(4):
                    h = 4 * hg + hh
                    S.act(V("activation", out=rsb[h].ap, in_=lnb.ap[:, h:h + 1], func=AF.Exp, scale=-1.0), reads=[lnb.b], writes=[rsb[h].b])
                    S.act(V("activation", out=oab[b].ap[:, col0 + h * 64:col0 + (h + 1) * 64], in_=ov[:, hh, 0:64], func=AF.Copy,
                            scale=rsb[h].ap), reads=[psb[5 + hg], rsb[h].b], writes=[oab[b].b])

        slots = []
        for i in range(32):
            tiles = list(range(2 * i + 2))
            masks_l = [(2 * i, 0), (2 * i + 1, 1)]
            band = [(2 * i - 4 + m, m) for m in range(6) if 2 * i - 4 + m >= 0]
            slots.append((tiles, masks_l, band))
        slots.append((list(range(65, 73)) + [64], [(8, 0)], [(69 + m, m) for m in range(4)] + [(64, 4)]))

        NS2 = min(NSLOT, int(os.environ.get("KNS", NSLOT))) if "2" in PH else 0
        for i in range(2):
            S.pool(V("memset", qTb[i].ap, 0.0), writes=[qTb[i].b])

        def mark(n):
            if os.environ.get("KDBG"):
                print("MARK", n, S.uid, flush=True)
        if NS2:
          mark("p2start")
          load_band(bandp_d)
          mark("band")
          p2_load_idx(0)
          mark("loadidx0")
          p2_index(0, slots[0][0])
          mark("index0")
          p2_bisect(0, slots[0][0], slots[0][1])
          mark("bisect0")
          p2_load_idx(1)
          p2_index(1, slots[1][0])
          p2_load_q(0)
          mark("pre-loop")
        for s in range(NS2):
            if s + 2 < NS2:
                p2_load_idx(s + 2)
                p2_index(s + 2, slots[s + 2][0])
            attend(s, slots[s][0], 0, 0, None)
            mark("dsa%d" % s)
            if s + 1 < NS2:
                p2_load_q(s + 1)
                p2_bisect(s + 1, slots[s + 1][0], slots[s + 1][1])
            if s == NSLOT - 1:
                load_band(bands_d)
            band = slots[s][2]
            mark("bis%d" % (s + 1))
            attend(s, [t for (t, m) in band], 4, 8, [m for (t, m) in band])
            mark("band%d" % s)
            if "3" in PH:
                conv_steps(3)
            S.dma("pool", oab_scr[s], oab[s % 2].ap, reads=[oab[s % 2].b], writes=[oscr_b[s]])

        S.barrier()
        A3 = Arena(arena_t, PBASE - NTS * 256, ARENA_BYTES)
        wall = A3.alloc([128, 80 * 1024], BF16)

        def wview(c0, nchunk, ncol):
            t_ = T(wall.ap[:, c0 * 1024:c0 * 1024 + nchunk * ncol].rearrange("p (c n) -> p c n", n=ncol))
            t_.b = wall.b
            return t_
        woa = wview(0, 4, D)
        wob = wview(4, 4, D)
        wo = wview(8, 8, D)
        wu = wview(16, 8, 4096)
        wd = wview(48, 32, D)
        o_in = A3.alloc([128, 1024], BF16)
        oT = A3.alloc([128, 8, 128], BF16)
        gt = A3.alloc([128, 2048], BF16)
        xin = A3.alloc([128, D], F32)
        t1 = A3.alloc([128, D], F32)
        t2 = A3.alloc([128, D], F32)
        x2s = [A3.alloc([128, D], F32) for _ in range(2)]
        x2Ts = [A3.alloc([128, 8, 128], BF16) for _ in range(2)]
        hT = A3.alloc([128, 32, 128], BF16)
        rl = A3.alloc([128, 512], F32)
        ss3 = A3.alloc([128, 1], F32)
        r3s = [A3.alloc([128, 1], F32) for _ in range(2)]

        if "3" in PH:
            conv_steps(80)
            for k in range(10):
                S.dma("sp", wall.ap[:, k * 8192:(k + 1) * 8192], wbf_scr[:, k * 8192:(k + 1) * 8192], reads=[wscr_b], writes=[wall.b])

        def tr_bf_p3(src, dst):
            pv = PSb16(0)
            for c in range(8):
                S.pe(V("transpose", out=pv[:, c * 128:(c + 1) * 128], in_=src.ap[:, c * 128:(c + 1) * 128], identity=ident_b.ap),
                     reads=[src.b, ident_b.b], writes=[psb[0]])
            S.dve(V("tensor_copy", out=dst.ap, in_=pv.rearrange("p (c k) -> p c k", k=128)), reads=[psb[0]], writes=[dst.b])

        def p3_stage_a(s):
            x2 = x2s[s % 2]
            x2T = x2Ts[s % 2]
            r3 = r3s[s % 2]
            S.dma("sp", o_in.ap, oab_scr[s], reads=[oscr_b[s]], writes=[o_in.b])
            S.dma("sp", gt.ap, gate_scr[s], reads=[qscr_b[s]], writes=[gt.b])
            S.dma("sp", xin.ap, x_own[s * 128:(s + 1) * 128, :], writes=[xin.b])
            tr_bf_p3(o_in, oT)
            for half in range(2):
                for c in range(4):
                    S.pe(V("matmul", PS[:, 2 + half, :], lhsT=oT.ap[:, c, :], rhs=woa.ap[:, c, half * 512:(half + 1) * 512],
                           start=(c == 0), stop=(c == 3)), reads=[oT.b, woa.b], writes=[psb[2 + half]])
                S.dve(V("tensor_tensor", out=t1.ap[:, half * 512:(half + 1) * 512], in0=PS[:, 2 + half, :],
                        in1=gt.ap[:, half * 512:(half + 1) * 512], op=ALU.mult), reads=[psb[2 + half], gt.b], writes=[t1.b])
            for half in range(2):
                for c in range(4):
                    S.pe(V("matmul", PS[:, 2 + half, :], lhsT=oT.ap[:, 4 + c, :], rhs=wob.ap[:, c, half * 512:(half + 1) * 512],
                           start=(c == 0), stop=(c == 3)), reads=[oT.b, wob.b], writes=[psb[2 + half]])
                S.dve(V("tensor_tensor", out=t2.ap[:, half * 512:(half + 1) * 512], in0=PS[:, 2 + half, :],
                        in1=gt.ap[:, 1024 + half * 512:1024 + (half + 1) * 512], op=ALU.mult), reads=[psb[2 + half], gt.b], writes=[t2.b])
            S.pool(V("tensor_tensor", out=o_in.ap, in0=t1.ap, in1=t2.ap, op=ALU.add), reads=[t1.b, t2.b], writes=[o_in.b])
            tr_bf_p3(o_in, oT)
            for half in range(2):
                for c in range(8):
                    S.pe(V("matmul", PS[:, 2 + half, :], lhsT=oT.ap[:, c, :], rhs=wo.ap[:, c, half * 512:(half + 1) * 512],
                           start=(c == 0), stop=(c == 7)), reads=[oT.b, wo.b], writes=[psb[2 + half]])
                S.dve(V("tensor_tensor", out=x2.ap[:, half * 512:(half + 1) * 512], in0=PS[:, 2 + half, :],
                        in1=xin.ap[:, half * 512:(half + 1) * 512], op=ALU.add), reads=[psb[2 + half], xin.b], writes=[x2.b])
            S.act(V("activation", out=t1.ap, in_=x2.ap, func=AF.Square, accum_out=ss3.ap), reads=[x2.b], writes=[t1.b, ss3.b])
            S.dve(V("tensor_scalar", out=r3.ap, in0=ss3.ap, scalar1=1.0 / D, scalar2=EPS, op0=ALU.mult, op1=ALU.add),
                  reads=[ss3.b], writes=[r3.b])
            S.dve(V("reciprocal", out=r3.ap, in_=r3.ap), reads=[r3.b], writes=[r3.b])
            for c in range(8):
                bank = c // 4
                S.pe(V("transpose", out=PS[:, bank, (c % 4) * 128:(c % 4 + 1) * 128], in_=x2.ap[:, c * 128:(c + 1) * 128],
                       identity=ident_f.ap), reads=[x2.b, ident_f.b], writes=[psb[bank]])
            for c in range(8):
                bank = c // 4
                src = PS[:, bank, (c % 4) * 128:(c % 4 + 1) * 128]
                S.dve(V("tensor_scalar", out=x2T.ap[:, c, :], in0=src, scalar1=gffn.ap[:, c:c + 1], scalar2=None, op0=ALU.mult),
                      reads=[psb[bank], gffn.b], writes=[x2T.b])

        def p3_stage_b(s, hook):
            x2 = x2s[s % 2]
            x2T = x2Ts[s % 2]
            r3 = r3s[s % 2]
            for f4 in range(8):
                bank = 4 + (f4 % 2)
                for ff in range(4):
                    f = f4 * 4 + ff
                    for c in range(8):
                        S.pe(V("matmul", PS[:, bank, ff * 128:(ff + 1) * 128], lhsT=wu.ap[:, c, f * 128:(f + 1) * 128], rhs=x2T.ap[:, c, :],
                               start=(c == 0), stop=(c == 7), skip_group_check=True), reads=[wu.b, x2T.b], writes=[psb[bank]])
                S.act(V("activation", out=rl.ap, in_=PS[:, bank, :], func=AF.Relu), reads=[psb[bank]], writes=[rl.b])
                S.dve(V("tensor_tensor", out=hT.ap[:, f4 * 4:(f4 + 1) * 4, :].rearrange("p a b -> p (a b)"), in0=rl.ap,
                        in1=rl.ap, op=ALU.mult), reads=[rl.b], writes=[hT.b])
                hook()
            for half in range(2):
                for f in range(32):
                    S.pe(V("matmul", PS[:, 6 + half, :], lhsT=hT.ap[:, f, :], rhs=wd.ap[:, f, half * 512:(half + 1) * 512],
                           start=(f == 0), stop=(f == 31)), reads=[hT.b, wd.b], writes=[psb[6 + half]])
                    if f % 8 == 7:
                        hook()
                S.dve(V("scalar_tensor_tensor", out=x2.ap[:, half * 512:(half + 1) * 512], in0=PS[:, 6 + half, :], scalar=r3.ap,
                        in1=x2.ap[:, half * 512:(half + 1) * 512], op0=ALU.mult, op1=ALU.add),
                      reads=[psb[6 + half], r3.b, x2.b], writes=[x2.b])
            S.dma("sp", y_o[s * 128:(s + 1) * 128, :], x2.ap, reads=[x2.b])

        NS3 = min(NSLOT, int(os.environ.get("KNS3", NSLOT))) if "3" in PH else 0
        if NS3:
            p3_stage_a(0)
        for s in range(NS3):
            ops = []
            if s + 1 < NS3:
                S.begin_defer()
                p3_stage_a(s + 1)
                ops = S.end_defer()
            per = (len(ops) + 14) // 15
            p3_stage_b(s, lambda: S.replay(ops, per))
            S.replay(ops, len(ops))

        S.emit(nc, st)
    return nc


def _rope_tables(pos):
    half = 8
    inv = (np.float32(500000.0) ** (-np.arange(half, dtype=np.float32) / np.float32(half))).astype(np.float32)
    ang = pos.astype(np.float32)[:, None] * inv[None, :]
    return np.cos(ang).astype(np.float32), np.sin(ang).astype(np.float32)


def _band_tiles(table, true_tile_of_m, visible_fn):
    out = np.full((8, 128, 768), NEG, np.float32)
    qi = np.arange(128)[:, None]
    kj = np.arange(128)[None, :]
    for m in range(6):
        off = true_tile_of_m[m]
        if off is None:
            continue
        dist = off * 128 + qi - kj
        idx = np.clip(dist, -128, 128) + 128
        vis = visible_fn(off, qi, kj)
        for h in range(8):
            g = table[idx, h]
            out[h, :, m * 128:(m + 1) * 128] = np.where(vis, g, np.float32(NEG))
    return out


def _vis_prompt(off, qi, kj):
    dc = 2 * off + qi // 64 - kj // 64
    return (dc >= 0) & (dc <= 8)


_CACHE = {}


def kernel(x_prompt, x_sample, cache_k_a, cache_v_a, cache_k_idx, cache_k_b, cache_v_b,
           norm_mix, w_in, qnorm_a, knorm_a, knorm_idx, qnorm_b, knorm_b, rel_bias_b,
           w_o_a, w_o_b, w_out, norm_ffn, w_up, w_down):
    f = lambda a: np.ascontiguousarray(np.asarray(a, dtype=np.float32))
    x_prompt, x_sample = f(x_prompt), f(x_sample)
    w_in_ = f(w_in)[0]
    sp = np.cumsum([0, 512, 512, 512, 512, 64, 8, 512, 512, 512, 1024, 1024])
    seg = {n: (sp[i], sp[i + 1]) for i, n in enumerate(["qa", "ka", "va", "qi", "ki", "wi", "qb", "kb", "vb", "ga", "gb"])}
    order = ["ka", "kb", "va", "vb", "ki", "qa", "qb", "qi", "ga", "gb", "wi"]
    w_in_r = np.ascontiguousarray(np.concatenate([w_in_[:, seg[n][0]:seg[n][1]] for n in order], axis=1))
    rep = lambda g, n: np.tile(f(g)[0], n)
    g_kab = np.ascontiguousarray(np.broadcast_to(np.concatenate([rep(knorm_a, 8), rep(knorm_b, 8)])[None], (128, 1024)))
    g_qab = np.ascontiguousarray(np.broadcast_to(np.concatenate([rep(qnorm_a, 8), rep(qnorm_b, 8)])[None], (128, 1024)))
    g_ki = np.ascontiguousarray(np.broadcast_to(f(knorm_idx)[0][None], (128, 64)))
    gmix = np.ascontiguousarray(f(norm_mix)[0].reshape(8, 128).T)
    gffn = np.ascontiguousarray(f(norm_ffn)[0].reshape(8, 128).T)
    K1 = KSTEPS + 2
    p2 = np.concatenate([2.0 ** (-np.arange(K1)), 2.0 ** (1.0 - np.arange(K1))]).astype(np.float32)
    pow2 = np.ascontiguousarray(np.broadcast_to(p2[None], (128, 2 * K1)))
    table = f(rel_bias_b)[0]
    qi = np.arange(128)[:, None]
    kj = np.arange(128)[None, :]
    own_mask = np.where((kj // 64) <= (qi // 64), 0.0, -1e30).astype(np.float32)
    band_s = _band_tiles(table, [4, 3, 2, 1, 0, None], _vis_prompt)

    in_maps = []
    for c in range(8):
        b, hf = c // 2, c % 2
        xb = x_prompt[b].reshape(64, 128, D)
        order_t = []
        for i in range(32):
            order_t += [2 * i + hf, 2 * i + 1 - hf]
        xs_pad = np.zeros((128, D), np.float32)
        xs_pad[:64] = x_sample[c]
        x_all = np.concatenate([xb[order_t].reshape(64 * 128, D), xs_pad], axis=0)
        pos = np.concatenate([(np.array(order_t)[:, None] * 128 + np.arange(128)[None]).reshape(-1),
                              1024 + np.arange(64), np.zeros(64, np.int64)])
        cos_all, sin_all = _rope_tables(pos)
        x_own = np.concatenate([xb[hf::2].reshape(32 * 128, D), xs_pad], axis=0)
        foreign = np.full((128, 128), 0.0 if hf == 1 else -1e30, np.float32)
        masks = np.ascontiguousarray(np.concatenate([own_mask, foreign], axis=1))
        if hf == 0:
            offs = [4, 3, 2, 1, 0, None]
        else:
            offs = [None, 4, 1, 2, 0 - 0, 0]
            offs = [4, None, 2, 3, 0, 1]
        band_p = _band_tiles(table, offs, _vis_prompt)
        in_maps.append(dict(
            x_all=np.ascontiguousarray(x_all), cos_all=cos_all, sin_all=sin_all, x_own=np.ascontiguousarray(x_own),
            ck_a=f(cache_k_a)[0, c].reshape(1024, 512), cv_a=f(cache_v_a)[0, c].reshape(1024, 512),
            ck_i=f(cache_k_idx)[0, c], ck_b=f(cache_k_b)[0, c].reshape(512, 512), cv_b=f(cache_v_b)[0, c].reshape(512, 512),
            w_in=w_in_r, w_oa=f(w_o_a)[0], w_ob=f(w_o_b)[0], w_out=f(w_out)[0], w_up=f(w_up)[0], w_down=f(w_down)[0],
            gmix=gmix, gffn=gffn, g_kab=g_kab, g_qab=g_qab, g_ki=g_ki, ident=np.eye(128, dtype=np.float32),
            pow2=pow2, masks=masks, band_p=band_p, band_s=band_s))

    if "nc" not in _CACHE:
        _CACHE["nc"] = build_program()
    import os
    NCORE = int(os.environ.get("KCORES", "8"))
    res = run_bass_kernel_spmd(_CACHE["nc"], in_maps[:NCORE], core_ids=list(range(NCORE)))
    R = list(res.results) + [res.results[0]] * (8 - NCORE)

    y_p = np.zeros((4, 64, 128, D), np.float32)
    ka_p = np.zeros((4, 64, 128, 512), np.float32)
    va_p = np.zeros((4, 64, 128, 512), np.float32)
    ki_p = np.zeros((4, 64, 128, 64), np.float32)
    kb_p = np.zeros((4, 4, 128, 512), np.float32)
    vb_p = np.zeros((4, 4, 128, 512), np.float32)
    y_s = np.zeros((8, 64, D), np.float32)
    ka_s = np.zeros((8, 64, 512), np.float32)
    va_s = np.zeros((8, 64, 512), np.float32)
    ki_s = np.zeros((8, 64, 64), np.float32)
    kb_s = np.zeros((8, 64, 512), np.float32)
    vb_s = np.zeros((8, 64, 512), np.float32)
    for c in range(8):
        b, hf = c // 2, c % 2
        r = R[c]
        y_p[b, hf::2] = r["y_o"][:4096].reshape(32, 128, D)
        ka_p[b, hf::2] = r["ka_o"][:4096].reshape(32, 128, 512)
        va_p[b, hf::2] = r["va_o"][:4096].reshape(32, 128, 512)
        ki_p[b, hf::2] = r["ki_o"][:4096].reshape(32, 128, 64)
        kb_p[b, hf::2] = r["kb_o"][:256].reshape(2, 128, 512)
        vb_p[b, hf::2] = r["vb_o"][:256].reshape(2, 128, 512)
        y_s[c] = r["y_o"][4096:4160]
        ka_s[c] = r["ka_o"][4096:4160]
        va_s[c] = r["va_o"][4096:4160]
        ki_s[c] = r["ki_o"][4096:4160]
        kb_s[c] = r["kb_o"][256:320]
        vb_s[c] = r["vb_o"][256:320]
    return (y_p.reshape(4, 8192, D), y_s,
            ka_p.reshape(1, 4, 8192, 8, 64), va_p.reshape(1, 4, 8192, 8, 64), ki_p.reshape(1, 4, 8192, 64),
            kb_p.reshape(1, 4, 512, 8, 64), vb_p.reshape(1, 4, 512, 8, 64),
            ka_s.reshape(1, 8, 64, 8, 64), va_s.reshape(1, 8, 64, 8, 64), ki_s.reshape(1, 8, 64, 64),
            kb_s.reshape(1, 8, 64, 8, 64), vb_s.reshape(1, 8, 64, 8, 64))
```

```python
import numpy as np
from contextlib import ExitStack
import concourse.bass as bass
import concourse.mybir as mybir
from concourse.bass_utils import run_bass_kernel_spmd

F32 = mybir.dt.float32
BF16 = mybir.dt.bfloat16
U8 = mybir.dt.uint8
ALU = mybir.AluOpType
AF = mybir.ActivationFunctionType
AX = mybir.AxisListType

D = 1024
NT = 65
NTS = 73
NSLOT = 33
DIN = 5704
KSTEPS = 16
NEG = -30000.0
EPS = 1e-6


class Buf:
    __slots__ = ("w", "r")

    def __init__(self):
        self.w = None
        self.r = {}


class Instr:
    __slots__ = ("eng", "fn", "deps", "signal", "sem", "val", "is_dma", "uid")


ENGS = ("pe", "act", "dve", "pool", "sp")
NRING = 8


class Sched:
    def __init__(self):
        self.q = {e: [] for e in ENGS}
        self.uid = 0
        self.bar = []
        self.bar_done = set()
        self.deferred = None

    def begin_defer(self):
        self.deferred = []

    def end_defer(self):
        ops = self.deferred
        self.deferred = None
        return ops

    def replay(self, ops, k):
        for _ in range(min(k, len(ops))):
            self.add(*ops.pop(0))

    def barrier(self):
        lst = []
        for e in ENGS:
            comp = [i for i in self.q[e] if not i.is_dma]
            if comp:
                lst.append(comp[-1])
            lst += [i for i in self.q[e] if i.is_dma][-NRING:]
        self.bar = lst
        self.bar_done = set()

    def add(self, eng, fn, reads=(), writes=(), dma=False):
        import os
        if self.deferred is not None:
            self.deferred.append((eng, fn, tuple(reads), tuple(writes), dma))
            return None
        if self.uid >= int(os.environ.get("KMAX", "100000000")):
            return None
        ins = Instr()
        ins.eng = eng
        ins.fn = fn
        ins.is_dma = dma
        ins.signal = dma
        ins.uid = self.uid
        self.uid += 1
        deps = {}

        def need(d, raw):
            if d is None or d is ins:
                return
            if (not d.is_dma) and (not dma) and d.eng == eng:
                if eng == "pe":
                    return
            deps[d.uid] = d

        for b in reads:
            need(b.w, True)
        for b in writes:
            need(b.w, False)
            for rd in b.r.values():
                need(rd, False)
        if self.bar and eng not in self.bar_done:
            self.bar_done.add(eng)
            for d in self.bar:
                deps[d.uid] = d
        for b in reads:
            b.r[("dma", ins.uid) if dma else eng] = ins
        for b in writes:
            b.w = ins
            b.r = {}
        ins.deps = list(deps.values())
        for d in ins.deps:
            d.signal = True
        self.q[eng].append(ins)
        return ins

    def pe(self, fn, reads=(), writes=()):
        return self.add("pe", fn, reads, writes)

    def act(self, fn, reads=(), writes=()):
        return self.add("act", fn, reads, writes)

    def dve(self, fn, reads=(), writes=()):
        return self.add("dve", fn, reads, writes)

    def pool(self, fn, reads=(), writes=()):
        return self.add("pool", fn, reads, writes)

    def dma(self, queue, out, in_, reads=(), writes=(), **kw):
        return self.add(queue, lambda e: e.dma_start(out=out, in_=in_, **kw), reads, writes, dma=True)

    def emit(self, nc, stack):
        esem = {e: stack.enter_context(nc.semaphore("s_" + e)) for e in ENGS}
        rings = {e: [stack.enter_context(nc.semaphore("r_%s%d" % (e, i))) for i in range(NRING)]
                 for e in ("sp", "pool", "act")}
        final_ring = {}
        for e in ENGS:
            cnt = 0
            nd = 0
            for ins in self.q[e]:
                if ins.is_dma:
                    ins.sem = rings[e][nd % NRING]
                    ins.val = 16 * (nd // NRING + 1)
                    final_ring[(e, nd % NRING)] = (ins.sem, ins.val)
                    nd += 1
                elif ins.signal:
                    cnt += 1
                    ins.sem = esem[e]
                    ins.val = cnt
        block = stack.enter_context(nc.Block())
        handles = {"pe": "tensor", "act": "scalar", "dve": "vector", "pool": "gpsimd", "sp": "sync"}

        def make(e):
            def body(h):
                waited = {}

                def wait(sem, val):
                    k = id(sem)
                    if waited.get(k, 0) >= val:
                        return
                    waited[k] = val
                    h.wait_ge(sem, val)

                nd = 0
                for ins in self.q[e]:
                    for d in ins.deps:
                        wait(d.sem, d.val)
                    if ins.is_dma:
                        if nd >= NRING:
                            wait(ins.sem, ins.val - 16)
                        nd += 1
                        ins.fn(h).then_inc(ins.sem, 16)
                    else:
                        r = ins.fn(h)
                        if ins.signal:
                            r.then_inc(ins.sem, 1)
                if e == "sp":
                    for (sem, val) in final_ring.values():
                        wait(sem, val)
            return body

        for e in ENGS:
            getattr(block, handles[e])(make(e))


def V(name, *a, **k):
    return lambda e: getattr(e, name)(*a, **k)


class T:
    __slots__ = ("ap", "b")

    def __init__(self, ap):
        self.ap = ap
        self.b = Buf()


_DSZ = {F32: 4, BF16: 2, U8: 1}


class Arena:
    def __init__(self, ap, base, limit):
        self.ap = ap
        self.off = base
        self.limit = limit

    def alloc(self, shape, dt):
        n = 1
        for s in shape[1:]:
            n *= s
        nb = (n * _DSZ[dt] + 31) // 32 * 32
        assert self.off + nb <= self.limit, ("arena overflow", self.off + nb, self.limit)
        v = self.ap[:, self.off // 2:(self.off + nb) // 2]
        self.off += nb
        if dt != BF16:
            v = v.bitcast(dt)
        v = v[:, 0:n]
        if len(shape) == 3:
            v = v.rearrange("p (a b) -> p a b", b=shape[2])
        elif len(shape) == 4:
            v = v.rearrange("p (a b c) -> p a b c", b=shape[2], c=shape[3])
        if shape[0] != 128:
            v = v[0:shape[0]]
        return T(v)


def build_program():
    nc = bass.Bass("TRN2", target_bir_lowering=False)

    def din(name, shape, dt=F32):
        return nc.dram_tensor(name, list(shape), dt, kind="ExternalInput").ap()

    def dout(name, shape, dt=F32):
        return nc.dram_tensor(name, list(shape), dt, kind="ExternalOutput").ap()

    def dscr(name, shape, dt):
        return nc.dram_tensor(name, list(shape), dt, kind="Internal").ap()

    x_all = din("x_all", [NT * 128, D])
    cos_all = din("cos_all", [NT * 128, 8])
    sin_all = din("sin_all", [NT * 128, 8])
    x_own = din("x_own", [NSLOT * 128, D])
    ck_a = din("ck_a", [1024, 512])
    cv_a = din("cv_a", [1024, 512])
    ck_i = din("ck_i", [1024, 64])
    ck_b = din("ck_b", [512, 512])
    cv_b = din("cv_b", [512, 512])
    w_in = din("w_in", [D, DIN])
    w_oa = din("w_oa", [512, D])
    w_ob = din("w_ob", [512, D])
    w_out = din("w_out", [D, D])
    w_up = din("w_up", [D, 4096])
    w_down = din("w_down", [4096, D])
    gmix_d = din("gmix", [128, 8])
    gffn_d = din("gffn", [128, 8])
    g_kab_d = din("g_kab", [128, 1024])
    g_qab_d = din("g_qab", [128, 1024])
    g_ki_d = din("g_ki", [128, 64])
    ident_d = din("ident", [128, 128])
    pow2_d = din("pow2", [128, 2 * (KSTEPS + 2)])
    mask_d = din("masks", [128, 256])
    bandp_d = din("band_p", [8, 128, 768])
    bands_d = din("band_s", [8, 128, 768])

    y_o = dout("y_o", [NSLOT * 128, D])
    ka_o = dout("ka_o", [NSLOT * 128, 512])
    va_o = dout("va_o", [NSLOT * 128, 512])
    ki_o = dout("ki_o", [NSLOT * 128, 64])
    kb_o = dout("kb_o", [3 * 128, 512])
    vb_o = dout("vb_o", [3 * 128, 512])

    kT_scr = dscr("kT_scr", [NTS, 128, 8, 128], BF16)
    v_scr = dscr("v_scr", [NTS, 128, 16, 65], BF16)
    q_scr = dscr("q_scr", [NSLOT, 128, 8, 128], BF16)
    qi_scr = dscr("qi_scr", [NSLOT, 128, 4, 128], BF16)
    gate_scr = dscr("gate_scr", [NSLOT, 128, 2048], BF16)
    wi_scr = dscr("wi_scr", [NSLOT, 128, 8], F32)
    oab_scr = dscr("oab_scr", [NSLOT, 128, 1024], BF16)
    wbf_scr = dscr("wbf_scr", [128, 80 * 1024], BF16)
    wscr_b = Buf()
    kscr_b = [Buf() for _ in range(NTS)]
    vscr_b = [Buf() for _ in range(NTS)]
    qscr_b = [Buf() for _ in range(NSLOT)]
    oscr_b = [Buf() for _ in range(NSLOT)]

    S = Sched()
    with ExitStack() as st:
        ARENA_BYTES = 206 * 1024
        arena_t = st.enter_context(nc.sbuf_tensor("arena", [128, ARENA_BYTES // 2], BF16))
        PS = st.enter_context(nc.psum_tensor("ps", [128, 8, 512], F32))
        psb = [Buf() for _ in range(8)]

        def PSb16(bank):
            return PS[:, bank, :].bitcast(BF16)

        A0 = Arena(arena_t, 0, ARENA_BYTES)
        ident_f = A0.alloc([128, 128], F32)
        ident_b = A0.alloc([128, 128], BF16)
        I4 = A0.alloc([128, 512], BF16)
        gmix = A0.alloc([128, 8], F32)
        gffn = A0.alloc([128, 8], F32)
        pow2 = A0.alloc([128, 2 * (KSTEPS + 2)], F32)
        masks = A0.alloc([128, 256], F32)
        kiT = A0.alloc([128, NTS * 128], BF16)
        kiT_b = [Buf() for _ in range(NTS)]
        PBASE = A0.off

        S.dma("sp", ident_f.ap, ident_d, writes=[ident_f.b])
        S.dma("sp", gmix.ap, gmix_d, writes=[gmix.b])
        S.dma("sp", gffn.ap, gffn_d, writes=[gffn.b])
        S.dma("sp", pow2.ap, pow2_d, writes=[pow2.b])
        S.dma("sp", masks.ap, mask_d, writes=[masks.b])
        S.dve(V("tensor_copy", out=ident_b.ap, in_=ident_f.ap), reads=[ident_f.b], writes=[ident_b.b])
        for j in range(4):
            S.dve(V("tensor_copy", out=I4.ap[:, j * 128:(j + 1) * 128], in_=ident_f.ap), reads=[ident_f.b], writes=[I4.b])

        A1 = Arena(arena_t, PBASE, ARENA_BYTES)
        w1 = A1.alloc([128, 8, DIN], BF16)
        g_kab = A1.alloc([128, 1024], F32)
        g_qab = A1.alloc([128, 1024], F32)
        g_ki = A1.alloc([128, 64], F32)
        xt = [A1.alloc([128, D], F32) for _ in range(2)]
        xT = [A1.alloc([128, 8, 128], BF16) for _ in range(2)]
        xT_b = [[Buf() for _ in range(8)] for _ in range(2)]
        cs = [A1.alloc([128, 8], F32) for _ in range(4)]
        sn = [A1.alloc([128, 8], F32) for _ in range(4)]
        ssx = [A1.alloc([128, 1], F32) for _ in range(2)]
        rstd = [A1.alloc([128, 1], F32) for _ in range(2)]
        z_k = [A1.alloc([128, 1024], F32) for _ in range(2)]
        z_v = [A1.alloc([128, 1024], F32) for _ in range(2)]
        z_q = [A1.alloc([128, 1024], F32) for _ in range(2)]
        z_qi = [A1.alloc([128, 512], F32) for _ in range(2)]
        z_ki = [A1.alloc([128, 64], F32) for _ in range(2)]
        wi_sb = [A1.alloc([128, 8], F32) for _ in range(2)]
        sq_tmps = [A1.alloc([128, 1024], F32) for _ in range(2)]
        hs_k = [A1.alloc([128, 16], F32) for _ in range(2)]
        hs_q = [A1.alloc([128, 16], F32) for _ in range(2)]
        hs_i = [A1.alloc([128, 1], F32) for _ in range(2)]
        knbs = [A1.alloc([128, 1024], BF16) for _ in range(2)]
        qnb = A1.alloc([128, 1024], BF16)
        qib = A1.alloc([128, 512], BF16)
        kibs = [A1.alloc([128, 128], BF16) for _ in range(2)]
        kT_sbs = [A1.alloc([128, 8, 128], BF16) for _ in range(2)]
        qT_sb = A1.alloc([128, 8, 128], BF16)
        qiT_sb = A1.alloc([128, 4, 128], BF16)
        vaug = [A1.alloc([128, 16, 65], BF16) for _ in range(2)]
        gates = A1.alloc([128, 2048], BF16)
        rt_default = [A1.alloc([128, 16, 8], F32) for _ in range(4)]

        w1_b = [Buf() for _ in range(3)]
        wstg = [A1.alloc([128, 1024], F32) for _ in range(2)]
        rtq = []
        for i_ in range(4):
            t_ = T(wstg[0].ap[:, i_ * 128:(i_ + 1) * 128].rearrange("p (h e) -> p h e", e=8))
            t_.b = wstg[0].b
            rtq.append(t_)
        wj = 0
        for c0 in range(0, DIN, 1024):
            c1 = min(DIN, c0 + 1024)
            n = c1 - c0
            for c in range(8):
                stg = wstg[wj % 2]
                S.dma("sp", stg.ap[:, 0:n], w_in[c * 128:(c + 1) * 128, c0:c1], writes=[stg.b])
                if wj % 2 == 0:
                    S.dve(V("tensor_copy", out=w1.ap[:, c, c0:c1], in_=stg.ap[:, 0:n]), reads=[stg.b], writes=[w1_b[c0 // 2048]])
                else:
                    S.act(V("activation", out=w1.ap[:, c, c0:c1], in_=stg.ap[:, 0:n], func=AF.Copy), reads=[stg.b], writes=[w1_b[c0 // 2048]])
                wj += 1
        S.dma("sp", g_kab.ap, g_kab_d, writes=[g_kab.b])
        S.dma("sp", g_qab.ap, g_qab_d, writes=[g_qab.b])
        S.dma("sp", g_ki.ap, g_ki_d, writes=[g_ki.b])
        for i in range(2):
            S.pool(V("memset", vaug[i].ap[:, :, 64:65], 1.0), writes=[vaug[i].b])

        zrot = [0]
        trot = [0]

        def headnorm(z, nh, gains, hs, sq_tmp):
            n = nh * 64
            S.act(V("activation", out=sq_tmp.ap[:, 0:n], in_=z.ap[:, 0:n], func=AF.Square), reads=[z.b], writes=[sq_tmp.b])
            S.dve(V("tensor_reduce", out=hs.ap[:, 0:nh], in_=sq_tmp.ap[:, 0:n].rearrange("p (h d) -> p h d", d=64),
                    axis=AX.X, op=ALU.add), reads=[sq_tmp.b], writes=[hs.b])
            S.dve(V("tensor_scalar", out=hs.ap[:, 0:nh], in0=hs.ap[:, 0:nh], scalar1=1.0 / 64, scalar2=EPS,
                    op0=ALU.mult, op1=ALU.add), reads=[hs.b], writes=[hs.b])
            S.act(V("activation", out=hs.ap[:, 0:nh], in_=hs.ap[:, 0:nh], func=AF.Sqrt), reads=[hs.b], writes=[hs.b])
            S.dve(V("reciprocal", out=hs.ap[:, 0:nh], in_=hs.ap[:, 0:nh]), reads=[hs.b], writes=[hs.b])
            zv = z.ap[:, 0:n].rearrange("p (h d) -> p h d", d=64)
            S.dve(V("tensor_tensor", out=zv, in0=zv, in1=hs.ap[:, 0:nh].unsqueeze(2).to_broadcast([128, nh, 64]),
                    op=ALU.mult), reads=[z.b, hs.b], writes=[z.b])
            S.dve(V("tensor_tensor", out=z.ap[:, 0:n], in0=z.ap[:, 0:n], in1=gains.ap, op=ALU.mult),
                  reads=[z.b, gains.b], writes=[z.b])

        def rope(z, col0, nh, cs_t, sn_t, rt=None):
            rt = rt_default if rt is None else rt
            v = z.ap[:, col0:col0 + nh * 64].rearrange("p (h d) -> p h d", d=64)
            x1 = v[:, :, 0:8]
            x2 = v[:, :, 8:16]
            cb = cs_t.ap.unsqueeze(1).to_broadcast([128, nh, 8])
            sb_ = sn_t.ap.unsqueeze(1).to_broadcast([128, nh, 8])
            t = [r.ap[:, 0:nh, :] for r in rt]
            rd = [z.b, cs_t.b, sn_t.b]
            S.pool(V("tensor_tensor", out=t[0], in0=x1, in1=cb, op=ALU.mult), reads=rd, writes=[rt[0].b])
            S.pool(V("tensor_tensor", out=t[1], in0=x2, in1=sb_, op=ALU.mult), reads=rd, writes=[rt[1].b])
            S.pool(V("tensor_tensor", out=t[2], in0=x2, in1=cb, op=ALU.mult), reads=rd, writes=[rt[2].b])
            S.pool(V("tensor_tensor", out=t[3], in0=x1, in1=sb_, op=ALU.mult), reads=rd, writes=[rt[3].b])
            S.pool(V("tensor_tensor", out=x1, in0=t[0], in1=t[1], op=ALU.subtract), reads=[rt[0].b, rt[1].b], writes=[z.b])
            S.pool(V("tensor_tensor", out=x2, in0=t[2], in1=t[3], op=ALU.add), reads=[rt[2].b, rt[3].b], writes=[z.b])

        def transposes_bf(src, ncol_blocks, dst, dst_c0, bank=6):
            pv = PSb16(bank)
            for c in range(ncol_blocks):
                S.pe(V("transpose", out=pv[:, c * 128:(c + 1) * 128], in_=src.ap[:, c * 128:(c + 1) * 128],
                       identity=ident_b.ap), reads=[src.b, ident_b.b], writes=[psb[bank]])
            S.dve(V("tensor_copy", out=dst.ap[:, dst_c0:dst_c0 + ncol_blocks, :],
                    in_=pv[:, 0:ncol_blocks * 128].rearrange("p (c k) -> p c k", k=128)),
                  reads=[psb[bank]], writes=[dst.b])

        def ki_to_kiT(src_f32_ap, src_b, tidx, kib):
            S.pool(V("tensor_copy", out=kib.ap[:, 0:64], in_=src_f32_ap), reads=[src_b], writes=[kib.b])
            S.pool(V("tensor_copy", out=kib.ap[:, 64:128], in_=src_f32_ap), reads=[src_b], writes=[kib.b])
            bank = 6
            pv = PSb16(bank)
            S.pe(V("transpose", out=pv[:, 0:128], in_=kib.ap, identity=ident_b.ap), reads=[kib.b, ident_b.b], writes=[psb[bank]])
            S.dve(V("tensor_copy", out=kiT.ap[:, tidx * 128:(tidx + 1) * 128], in_=pv[:, 0:128]),
                  reads=[psb[bank]], writes=[kiT_b[tidx]])

        KCH = [(0, 512, "ka"), (512, 1024, "kb"), (1024, 1536, "va"), (1536, 2048, "vb"), (2048, 2112, "ki")]
        QCH = [(2112, 2624, "qa"), (2624, 3136, "qb"), (3136, 3648, "qi"), (3648, 4160, "g0"), (4160, 4672, "g1"),
               (4672, 5184, "g2"), (5184, 5696, "g3"), (5696, 5704, "wi")]

        def p1_load(t):
            p = t % 2
            S.dma("sp", xt[p].ap, x_all[t * 128:(t + 1) * 128, :], writes=[xt[p].b])
            S.dma("sp", cs[t % 4].ap, cos_all[t * 128:(t + 1) * 128, :], writes=[cs[t % 4].b])
            S.dma("sp", sn[t % 4].ap, sin_all[t * 128:(t + 1) * 128, :], writes=[sn[t % 4].b])

        def p1_front(t, after_chunk=None):
            own = (t % 2 == 0)
            slot = t // 2
            p = t % 2
            po = slot % 2
            xb = xt[p]
            sq_tmp = sq_tmps[p]
            S.act(V("activation", out=sq_tmp.ap, in_=xb.ap, func=AF.Square, accum_out=ssx[p].ap),
                  reads=[xb.b], writes=[sq_tmp.b, ssx[p].b])
            S.dve(V("tensor_scalar", out=rstd[p].ap, in0=ssx[p].ap, scalar1=1.0 / D, scalar2=EPS, op0=ALU.mult, op1=ALU.add),
                  reads=[ssx[p].b], writes=[rstd[p].b])
            S.act(V("activation", out=rstd[p].ap, in_=rstd[p].ap, func=AF.Sqrt), reads=[rstd[p].b], writes=[rstd[p].b])
            S.dve(V("reciprocal", out=rstd[p].ap, in_=rstd[p].ap), reads=[rstd[p].b], writes=[rstd[p].b])
            for c in range(8):
                bank = c // 4
                S.pe(V("transpose", out=PS[:, bank, (c % 4) * 128:(c % 4 + 1) * 128], in_=xb.ap[:, c * 128:(c + 1) * 128],
                       identity=ident_f.ap), reads=[xb.b, ident_f.b], writes=[psb[bank]])
            for c in range(8):
                bank = c // 4
                src = PS[:, bank, (c % 4) * 128:(c % 4 + 1) * 128]
                S.dve(V("tensor_scalar", out=xT[p].ap[:, c, :], in0=src, scalar1=gmix.ap[:, c:c + 1], scalar2=None,
                        op0=ALU.mult), reads=[psb[bank], gmix.b], writes=[xT_b[p][c]])
            if t + 1 < NT1:
                p1_load(t + 1)
            chunks = KCH + (QCH if own else [])
            for (c0, c1, kind) in chunks:
                n = c1 - c0
                bank = 2 + (zrot[0] % 4)
                zrot[0] += 1
                for c in range(8):
                    S.pe(V("matmul", PS[:, bank, 0:n], lhsT=xT[p].ap[:, c, :], rhs=w1.ap[:, c, c0:c1], start=(c == 0), stop=(c == 7)),
                         reads=[xT_b[p][c], w1_b[c0 // 2048], w1_b[(c1 - 1) // 2048]], writes=[psb[bank]])
                src = PS[:, bank, 0:n]
                rd = [psb[bank], rstd[p].b]
                sc = rstd[p].ap
                if kind == "ka":
                    S.act(V("activation", out=z_k[p].ap[:, 0:512], in_=src, func=AF.Copy, scale=sc), reads=rd, writes=[z_k[p].b])
                elif kind == "kb":
                    S.act(V("activation", out=z_k[p].ap[:, 512:1024], in_=src, func=AF.Copy, scale=sc), reads=rd, writes=[z_k[p].b])
                elif kind == "va":
                    S.act(V("activation", out=z_v[p].ap[:, 0:512], in_=src, func=AF.Copy, scale=sc), reads=rd, writes=[z_v[p].b])
                elif kind == "vb":
                    S.act(V("activation", out=z_v[p].ap[:, 512:1024], in_=src, func=AF.Copy, scale=sc), reads=rd, writes=[z_v[p].b])
                elif kind == "ki":
                    S.act(V("activation", out=z_ki[p].ap, in_=src, func=AF.Copy, scale=sc), reads=rd, writes=[z_ki[p].b])
                elif kind == "qa":
                    S.act(V("activation", out=z_q[po].ap[:, 0:512], in_=src, func=AF.Copy, scale=sc), reads=rd, writes=[z_q[po].b])
                elif kind == "qb":
                    S.act(V("activation", out=z_q[po].ap[:, 512:1024], in_=src, func=AF.Copy, scale=sc), reads=rd, writes=[z_q[po].b])
                elif kind == "qi":
                    S.act(V("activation", out=z_qi[po].ap, in_=src, func=AF.Copy, scale=sc), reads=rd, writes=[z_qi[po].b])
                elif kind[0] == "g":
                    j = int(kind[1])
                    S.act(V("activation", out=gates.ap[:, j * 512:(j + 1) * 512], in_=src, func=AF.Sigmoid, scale=sc), reads=rd, writes=[gates.b])
                elif kind == "wi":
                    S.act(V("activation", out=wi_sb[po].ap, in_=src, func=AF.Copy, scale=sc), reads=rd, writes=[wi_sb[po].b])
                if after_chunk is not None:
                    after_chunk()
        def p1_back_k(t):
            own = (t % 2 == 0)
            slot = t // 2
            p = t % 2
            po = slot % 2
            sq_tmp = sq_tmps[p]
            knb = knbs[p]
            kib = kibs[p]
            kT_sb = kT_sbs[p]
            headnorm(z_k[p], 16, g_kab, hs_k[p], sq_tmp)
            rope(z_k[p], 0, 8, cs[t % 4], sn[t % 4])
            headnorm(z_ki[p], 1, g_ki, hs_i[p], sq_tmp)
            rope(z_ki[p], 0, 1, cs[t % 4], sn[t % 4])
            if own:
                r0 = slot * 128
                S.dma("sp", ka_o[r0:r0 + 128, :], z_k[p].ap[:, 0:512], reads=[z_k[p].b])
                S.dma("sp", va_o[r0:r0 + 128, :], z_v[p].ap[:, 0:512], reads=[z_v[p].b])
                S.dma("sp", ki_o[r0:r0 + 128, :], z_ki[p].ap, reads=[z_ki[p].b])
                if slot >= 30:
                    rb = (slot - 30) * 128
                    S.dma("sp", kb_o[rb:rb + 128, :], z_k[p].ap[:, 512:1024], reads=[z_k[p].b])
                    S.dma("sp", vb_o[rb:rb + 128, :], z_v[p].ap[:, 512:1024], reads=[z_v[p].b])
            S.dve(V("tensor_copy", out=knb.ap, in_=z_k[p].ap), reads=[z_k[p].b], writes=[knb.b])
            transposes_bf(knb, 8, kT_sb, 0)
            S.dma("sp", kT_scr[t], kT_sb.ap, reads=[kT_sb.b], writes=[kscr_b[t]])
            S.act(V("activation", out=vaug[p].ap[:, :, 0:64], in_=z_v[p].ap.rearrange("p (h d) -> p h d", d=64), func=AF.Copy),
                  reads=[z_v[p].b], writes=[vaug[p].b])
            S.dma("sp", v_scr[t], vaug[p].ap, reads=[vaug[p].b], writes=[vscr_b[t]])
            ki_to_kiT(z_ki[p].ap, z_ki[p].b, t, kib)
            if own:
                S.dma("sp", gate_scr[slot], gates.ap, reads=[gates.b], writes=[qscr_b[slot]])
                S.dma("sp", wi_scr[slot], wi_sb[po].ap, reads=[wi_sb[po].b], writes=[qscr_b[slot]])

        def p1_back_q(t):
            slot = t // 2
            p = t % 2
            po = slot % 2
            sq_tmp = sq_tmps[p]
            headnorm(z_q[po], 16, g_qab, hs_q[po], sq_tmp)
            rope(z_q[po], 0, 8, cs[t % 4], sn[t % 4], rtq)
            rope(z_qi[po], 0, 8, cs[t % 4], sn[t % 4], rtq)
            S.dve(V("tensor_copy", out=qnb.ap, in_=z_q[po].ap), reads=[z_q[po].b], writes=[qnb.b])
            transposes_bf(qnb, 8, qT_sb, 0, bank=7)
            S.dma("sp", q_scr[slot], qT_sb.ap, reads=[qT_sb.b], writes=[qscr_b[slot]])
            S.dve(V("tensor_copy", out=qib.ap, in_=z_qi[po].ap), reads=[z_qi[po].b], writes=[qib.b])
            transposes_bf(qib, 4, qiT_sb, 0, bank=7)
            S.dma("sp", qi_scr[slot], qiT_sb.ap, reads=[qiT_sb.b], writes=[qscr_b[slot]])

        import os
        PH = os.environ.get("KPH", "123")
        NT1 = int(os.environ.get("KNT", NT))
        for m in (range(8) if "c" in PH or "2" in PH else []):
            p = m % 2
            tidx = 65 + m
            xb = xt[p]
            knb = knbs[p]
            kT_sb = kT_sbs[p]
            S.dma("sp", xb.ap[:, 0:512], ck_a[m * 128:(m + 1) * 128, :], writes=[xb.b])
            S.dma("sp", xb.ap[:, 512:1024], cv_a[m * 128:(m + 1) * 128, :], writes=[xb.b])
            S.dve(V("tensor_copy", out=knb.ap[:, 0:512], in_=xb.ap[:, 0:512]), reads=[xb.b], writes=[knb.b])
            S.pool(V("tensor_copy", out=vaug[p].ap[:, 0:8, 0:64], in_=xb.ap[:, 512:1024].rearrange("p (h d) -> p h d", d=64)),
                   reads=[xb.b], writes=[vaug[p].b])
            if m >= 4:
                zb = z_k[p]
                S.dma("sp", zb.ap[:, 0:512], ck_b[(m - 4) * 128:(m - 3) * 128, :], writes=[zb.b])
                S.dma("sp", zb.ap[:, 512:1024], cv_b[(m - 4) * 128:(m - 3) * 128, :], writes=[zb.b])
                S.dve(V("tensor_copy", out=knb.ap[:, 512:1024], in_=zb.ap[:, 0:512]), reads=[zb.b], writes=[knb.b])
                S.pool(V("tensor_copy", out=vaug[p].ap[:, 8:16, 0:64], in_=zb.ap[:, 512:1024].rearrange("p (h d) -> p h d", d=64)),
                       reads=[zb.b], writes=[vaug[p].b])
            nb_ = 8 if m >= 4 else 4
            transposes_bf(knb, nb_, kT_sb, 0)
            S.dma("sp", kT_scr[tidx][:, 0:nb_, :], kT_sb.ap[:, 0:nb_, :], reads=[kT_sb.b], writes=[kscr_b[tidx]])
            S.dma("sp", v_scr[tidx][:, 0:2 * nb_, :], vaug[p].ap[:, 0:2 * nb_, :], reads=[vaug[p].b], writes=[vscr_b[tidx]])
            zi = z_ki[p]
            S.dma("sp", zi.ap, ck_i[m * 128:(m + 1) * 128, :], writes=[zi.b])
            ki_to_kiT(zi.ap, zi.b, tidx, kibs[p])

        def merge_ops(a_, b_):
            out = []
            i = j = 0
            while i < len(a_) or j < len(b_):
                if j >= len(b_) or (i < len(a_) and i * max(1, len(b_)) <= j * max(1, len(a_))):
                    out.append(a_[i])
                    i += 1
                else:
                    out.append(b_[j])
                    j += 1
            return out

        p1_load(0)
        p1_front(0)
        pend_q = []
        for t in range(NT1):
            S.begin_defer()
            p1_back_k(t)
            ops_k = S.end_defer()
            ops = merge_ops(ops_k, pend_q)
            pend_q = []
            if t % 2 == 0:
                S.begin_defer()
                p1_back_q(t)
                pend_q = S.end_defer()
            if t + 1 < NT1:
                nch = 13 if (t + 1) % 2 == 0 else 5
                per = (len(ops) + nch - 1) // nch
                p1_front(t + 1, after_chunk=lambda: S.replay(ops, per))
            S.replay(ops, len(ops))
        S.replay(pend_q, len(pend_q))

        S.barrier()
        A2 = Arena(arena_t, PBASE, ARENA_BYTES)
        NMAX = 64 * 128
        isc = [A2.alloc([128, NMAX], F32) for _ in range(2)]
        junk = A2.alloc([128, NMAX], U8)
        mb = [A2.alloc([128, NMAX], BF16) for _ in range(2)]
        kbuf = [A2.alloc([128, 4, 4, 128], BF16) for _ in range(2)]
        vbuf = [A2.alloc([128, 4, 8, 65], BF16) for _ in range(2)]
        bandb = A2.alloc([128, 8, 768], BF16)
        bstage = A2.alloc([128, 768], F32)
        qTb = [A2.alloc([128, 16, 128], BF16) for _ in range(2)]
        qiTb = [A2.alloc([128, 4, 128], BF16) for _ in range(2)]
        wib = [A2.alloc([128, 8], F32) for _ in range(2)]
        diag = [A2.alloc([128, 8, 128], BF16) for _ in range(2)]
        Rb = [A2.alloc([128, 512], BF16) for _ in range(5)]
        pTb = [A2.alloc([128, 512], BF16) for _ in range(5)]
        LAG = 3
        SBANKS = [3, 4, 7, 0, 1]
        HBANKS = [0, 1, 3, 4, 7]
        oab = [A2.alloc([128, 1024], BF16) for _ in range(2)]
        bst = [A2.alloc([128, 4 * (KSTEPS + 2)], F32) for _ in range(2)]
        lnb = A2.alloc([128, 8], F32)
        rsb = [A2.alloc([128, 1], F32) for _ in range(8)]
        cstg = [A2.alloc([128, 1024], F32) for _ in range(2)]
        cout = [A2.alloc([128, 1024], BF16) for _ in range(2)]
        conv_i = [0]

        def conv_src(i):
            if i < 4:
                return w_oa[i * 128:(i + 1) * 128, :]
            if i < 8:
                return w_ob[(i - 4) * 128:(i - 3) * 128, :]
            if i < 16:
                return w_out[(i - 8) * 128:(i - 7) * 128, :]
            if i < 48:
                c, j = (i - 16) // 4, (i - 16) % 4
                return w_up[c * 128:(c + 1) * 128, j * 1024:(j + 1) * 1024]
            return w_down[(i - 48) * 128:(i - 47) * 128, :]

        def conv_steps(k):
            for _ in range(k):
                i = conv_i[0]
                if i >= 80:
                    return
                conv_i[0] += 1
                S.dma("pool", cstg[i % 2].ap, conv_src(i), writes=[cstg[i % 2].b])
                S.pool(V("tensor_copy", out=cout[i % 2].ap, in_=cstg[i % 2].ap), reads=[cstg[i % 2].b], writes=[cout[i % 2].b])
                S.dma("pool", wbf_scr[:, i * 1024:(i + 1) * 1024], cout[i % 2].ap, reads=[cout[i % 2].b], writes=[wscr_b])

        rot = {"sh": 0, "R": 0, "st": 0, "pT": 0, "kv": 0}
        K1 = KSTEPS + 2

        def load_band(src_d):
            for h in range(8):
                S.dma("sp", bstage.ap, src_d[h], writes=[bstage.b])
                S.act(V("activation", out=bandb.ap[:, h, :], in_=bstage.ap, func=AF.Copy, scale=8.0), reads=[bstage.b], writes=[bandb.b])

        def groups_of(tiles):
            gs = []
            for pos, t in enumerate(tiles):
                if gs and len(gs[-1]) < 4 and gs[-1][-1][1] + 1 == t:
                    gs[-1].append((pos, t))
                else:
                    gs.append([(pos, t)])
            return gs

        def p2_load_idx(s):
            b = s % 2
            S.dma("sp", qiTb[b].ap, qi_scr[s], reads=[qscr_b[s]], writes=[qiTb[b].b])
            S.dma("sp", wib[b].ap, wi_scr[s], reads=[qscr_b[s]], writes=[wib[b].b])
            for h in range(8):
                S.dve(V("tensor_scalar", out=diag[b].ap[:, h, :], in0=ident_b.ap, scalar1=wib[b].ap[:, h:h + 1], scalar2=None, op0=ALU.mult),
                      reads=[ident_b.b, wib[b].b], writes=[diag[b].b])

        def p2_load_q(s):
            b = s % 2
            S.dma("sp", qTb[b].ap[0:64, 0::2, :], q_scr[s][0:64, :, :], reads=[qscr_b[s]], writes=[qTb[b].b])
            S.dma("sp", qTb[b].ap[64:128, 1::2, :], q_scr[s][64:128, :, :], reads=[qscr_b[s]], writes=[qTb[b].b])

        def p2_index(s, tiles):
            b = s % 2
            steps = [(g, h) for g in groups_of(tiles) for h in range(8)]
            q = []

            def emit_d(pd):
                (pg, ph, pR, pn) = pd
                S.pe(V("matmul", PS[:, 2, 0:pn], lhsT=diag[b].ap[:, ph, :], rhs=pR.ap[:, 0:pn], start=(ph == 0), stop=(ph == 7)),
                     reads=[diag[b].b, pR.b], writes=[psb[2]])
                if ph == 7:
                    c0 = pg[0][0] * 128
                    S.act(V("activation", out=isc[b].ap[:, c0:c0 + pn], in_=PS[:, 2, 0:pn], func=AF.Copy),
                          reads=[psb[2]], writes=[isc[b].b])

            for (g, h) in steps:
                n = len(g) * 128
                t0 = g[0][1]
                bank = HBANKS[rot["sh"] % 5]
                rot["sh"] += 1
                base = 64 * (h % 2)
                S.pe(V("matmul", PS[:, bank, 0:n], lhsT=qiTb[b].ap[base:base + 64, h // 2, :],
                       rhs=kiT.ap[base:base + 64, t0 * 128:t0 * 128 + n], start=True, stop=True),
                     reads=[qiTb[b].b] + [kiT_b[t] for (_, t) in g], writes=[psb[bank]])
                R = Rb[rot["R"] % 5]
                rot["R"] += 1
                S.act(V("activation", out=R.ap[:, 0:n], in_=PS[:, bank, 0:n], func=AF.Relu), reads=[psb[bank]], writes=[R.b])
                q.append((g, h, R, n))
                if len(q) > LAG:
                    emit_d(q.pop(0))
            while q:
                emit_d(q.pop(0))

        def p2_bisect(s, tiles, mask_list):
            b = s % 2
            N = len(tiles) * 128
            X = isc[b]
            st_ = bst[b]
            amax = st_.ap[:, 0:1]
            wk = st_.ap[:, K1:2 * K1]
            mid = st_.ap[:, 2 * K1:3 * K1]
            cnt = st_.ap[:, 3 * K1:4 * K1]
            w2 = st_.ap[:, 1:K1]
            S.dve(V("tensor_reduce", out=amax, in_=X.ap[:, 0:N], axis=AX.X, op=ALU.max, apply_absolute_value=True),
                  reads=[X.b], writes=[st_.b])
            for (pos, mcol) in mask_list:
                S.dve(V("tensor_tensor", out=X.ap[:, pos * 128:(pos + 1) * 128], in0=X.ap[:, pos * 128:(pos + 1) * 128],
                        in1=masks.ap[:, mcol * 128:(mcol + 1) * 128], op=ALU.add), reads=[X.b, masks.b], writes=[X.b])
            S.dve(V("tensor_scalar", out=wk, in0=pow2.ap[:, 0:K1], scalar1=amax, scalar2=None, op0=ALU.mult),
                  reads=[st_.b, pow2.b], writes=[st_.b])
            S.dve(V("tensor_scalar", out=w2, in0=pow2.ap[:, K1 + 1:2 * K1], scalar1=amax, scalar2=None, op0=ALU.mult),
                  reads=[st_.b, pow2.b], writes=[st_.b])
            S.dve(V("memset", mid[:, 0:1], 0.0), writes=[st_.b])
            for k in range(KSTEPS):
                S.dve(V("tensor_scalar", out=junk.ap[:, 0:N], in0=X.ap[:, 0:N], scalar1=mid[:, k:k + 1], scalar2=None,
                        op0=ALU.is_ge, op1=ALU.add, accum_out=cnt[:, k:k + 1]), reads=[X.b, st_.b], writes=[junk.b, st_.b])
                S.dve(V("scalar_tensor_tensor", out=cnt[:, k:k + 1], in0=cnt[:, k:k + 1], scalar=255.5, in1=w2[:, k:k + 1],
                        op0=ALU.is_ge, op1=ALU.mult), reads=[st_.b], writes=[st_.b])
                S.dve(V("scalar_tensor_tensor", out=mid[:, k + 1:k + 2], in0=mid[:, k:k + 1], scalar=wk[:, k + 1:k + 2],
                        in1=cnt[:, k:k + 1], op0=ALU.subtract, op1=ALU.add), reads=[st_.b], writes=[st_.b])
            thr = cnt[:, KSTEPS:KSTEPS + 1]
            S.dve(V("tensor_tensor", out=thr, in0=mid[:, KSTEPS:KSTEPS + 1], in1=wk[:, KSTEPS:KSTEPS + 1], op=ALU.subtract),
                  reads=[st_.b], writes=[st_.b])
            S.dve(V("tensor_scalar", out=mb[b].ap[:, 0:N], in0=X.ap[:, 0:N], scalar1=thr, scalar2=NEG, op0=ALU.is_lt, op1=ALU.mult),
                  reads=[X.b, st_.b], writes=[mb[b].b])

        def attend(s, tiles, pair0, vh0, bias_mode):
            b = s % 2
            qT = qTb[b]
            steps = []
            for g in groups_of(tiles):
                kv = rot["kv"] % 2
                rot["kv"] += 1
                ng = len(g)
                t0 = g[0][1]
                steps.append(("load", g, kv))
                for j, (pos, t) in enumerate(g):
                    for hg in range(2):
                        steps.append(("qk", pos, j, hg, kv))
            first = [True, True]
            nsteps_left = [sum(1 for x in steps if x[0] == "qk" and x[3] == hg) for hg in range(2)]
            pendq = []

            def emit_pv(pd):
                (pos, j, hg, kv, pt) = pd
                nsteps_left[hg] -= 1
                for hh in range(4):
                    h = 4 * hg + hh
                    S.pe(V("matmul", PS[:, 5 + hg, hh * 65:(hh + 1) * 65], lhsT=pt.ap[:, hh * 128:(hh + 1) * 128],
                           rhs=vbuf[kv].ap[:, j, h, :], start=(first[hg] and hh == 0), stop=(nsteps_left[hg] == 0),
                           skip_group_check=True), reads=[pt.b, vbuf[kv].b], writes=[psb[5 + hg]])
                first[hg] = False

            for stp in steps:
                if stp[0] == "load":
                    (_, g, kv) = stp
                    ng = len(g)
                    t0 = g[0][1]
                    S.dma("sp", kbuf[kv].ap[:, 0:ng], kT_scr[t0:t0 + ng].rearrange("t p c k -> p t c k")[:, :, pair0:pair0 + 4, :],
                          reads=[kscr_b[t] for (_, t) in g], writes=[kbuf[kv].b])
                    S.dma("sp", vbuf[kv].ap[:, 0:ng], v_scr[t0:t0 + ng].rearrange("t p h d -> p t h d")[:, :, vh0:vh0 + 8, :],
                          reads=[vscr_b[t] for (_, t) in g], writes=[vbuf[kv].b])
                    continue
                (_, pos, j, hg, kv) = stp
                sbank = SBANKS[rot["st"] % 5]
                rot["st"] += 1
                if bias_mode is None:
                    S.pe(V("matmul", PS[:, sbank, :], lhsT=mb[b].ap[:, pos * 128:(pos + 1) * 128], rhs=I4.ap, start=True, stop=False,
                           skip_group_check=True), reads=[mb[b].b, I4.b], writes=[psb[sbank]])
                for hh in range(4):
                    h = 4 * hg + hh
                    pr = h // 2
                    base = 64 * (h % 2)
                    if bias_mode is not None:
                        m = bias_mode[pos]
                        S.pe(V("matmul", PS[:, sbank, hh * 128:(hh + 1) * 128], lhsT=bandb.ap[:, h, m * 128:(m + 1) * 128], rhs=ident_b.ap,
                               start=True, stop=False, skip_group_check=True), reads=[bandb.b, ident_b.b], writes=[psb[sbank]])
                    S.pe(V("matmul", PS[:, sbank, hh * 128:(hh + 1) * 128], lhsT=kbuf[kv].ap[:, j, pr, :],
                           rhs=qT.ap[:, 2 * pair0 + h, :], start=False, stop=True, skip_group_check=True),
                         reads=[kbuf[kv].b, qT.b], writes=[psb[sbank]])
                pt = pTb[rot["pT"] % 5]
                rot["pT"] += 1
                S.act(V("activation", out=pt.ap, in_=PS[:, sbank, :], func=AF.Exp, scale=0.125), reads=[psb[sbank]], writes=[pt.b])
                pendq.append((pos, j, hg, kv, pt))
                if len(pendq) > LAG:
                    emit_pv(pendq.pop(0))
            while pendq:
                emit_pv(pendq.pop(0))
            col0 = 0 if bias_mode is None else 512
            for hg in range(2):
                ov = PS[:, 5 + hg, 0:260].rearrange("p (h d) -> p h d", d=65)
                S.act(V("activation", out=lnb.ap[:, hg * 4:(hg + 1) * 4], in_=ov[:, :, 64], func=AF.Ln), reads=[psb[5 + hg]], writes=[lnb.b])
                for hh in range(4):
                    h = 4 * hg + hh
                    S.act(V("activation", out=rsb[h].ap, in_=lnb.ap[:, h:h + 1], func=AF.Exp, scale=-1.0), reads=[lnb.b], writes=[rsb[h].b])
                    S.act(V("activation", out=oab[b].ap[:, col0 + h * 64:col0 + (h + 1) * 64], in_=ov[:, hh, 0:64], func=AF.Copy,
                            scale=rsb[h].ap), reads=[psb[5 + hg], rsb[h].b], writes=[oab[b].b])

        slots = []
        for i in range(32):
            tiles = list(range(2 * i + 2))
            masks_l = [(2 * i, 0), (2 * i + 1, 1)]
            band = [(2 * i - 4 + m, m) for m in range(6) if 2 * i - 4 + m >= 0]
            slots.append((tiles, masks_l, band))
        slots.append((list(range(65, 73)) + [64], [(8, 0)], [(69 + m, m) for m in range(4)] + [(64, 4)]))

        NS2 = min(NSLOT, int(os.environ.get("KNS", NSLOT))) if "2" in PH else 0
        for i in range(2):
            S.pool(V("memset", qTb[i].ap, 0.0), writes=[qTb[i].b])

        def mark(n):
            if os.environ.get("KDBG"):
                print("MARK", n, S.uid, flush=True)
        if NS2:
          mark("p2start")
          load_band(bandp_d)
          mark("band")
          p2_load_idx(0)
          mark("loadidx0")
          p2_index(0, slots[0][0])
          mark("index0")
          p2_bisect(0, slots[0][0], slots[0][1])
          mark("bisect0")
          p2_load_idx(1)
          p2_index(1, slots[1][0])
          p2_load_q(0)
          mark("pre-loop")
        for s in range(NS2):
            if s + 2 < NS2:
                p2_load_idx(s + 2)
                p2_index(s + 2, slots[s + 2][0])
            attend(s, slots[s][0], 0, 0, None)
            mark("dsa%d" % s)
            if s + 1 < NS2:
                p2_load_q(s + 1)
                p2_bisect(s + 1, slots[s + 1][0], slots[s + 1][1])
            if s == NSLOT - 1:
                load_band(bands_d)
            band = slots[s][2]
            mark("bis%d" % (s + 1))
            attend(s, [t for (t, m) in band], 4, 8, [m for (t, m) in band])
            mark("band%d" % s)
            if "3" in PH:
                conv_steps(3)
            S.dma("pool", oab_scr[s], oab[s % 2].ap, reads=[oab[s % 2].b], writes=[oscr_b[s]])

        S.barrier()
        A3 = Arena(arena_t, PBASE - NTS * 256, ARENA_BYTES)
        wall = A3.alloc([128, 80 * 1024], BF16)

        def wview(c0, nchunk, ncol):
            t_ = T(wall.ap[:, c0 * 1024:c0 * 1024 + nchunk * ncol].rearrange("p (c n) -> p c n", n=ncol))
            t_.b = wall.b
            return t_
        woa = wview(0, 4, D)
        wob = wview(4, 4, D)
        wo = wview(8, 8, D)
        wu = wview(16, 8, 4096)
        wd = wview(48, 32, D)
        o_in = A3.alloc([128, 1024], BF16)
        oT = A3.alloc([128, 8, 128], BF16)
        gt = A3.alloc([128, 2048], BF16)
        xin = A3.alloc([128, D], F32)
        t1 = A3.alloc([128, D], F32)
        t2 = A3.alloc([128, D], F32)
        x2s = [A3.alloc([128, D], F32) for _ in range(2)]
        x2Ts = [A3.alloc([128, 8, 128], BF16) for _ in range(2)]
        hT = A3.alloc([128, 32, 128], BF16)
        rl = A3.alloc([128, 512], F32)
        ss3 = A3.alloc([128, 1], F32)
        r3s = [A3.alloc([128, 1], F32) for _ in range(2)]

        if "3" in PH:
            conv_steps(80)
            for k in range(10):
                S.dma("sp", wall.ap[:, k * 8192:(k + 1) * 8192], wbf_scr[:, k * 8192:(k + 1) * 8192], reads=[wscr_b], writes=[wall.b])

        def tr_bf_p3(src, dst):
            pv = PSb16(0)
            for c in range(8):
                S.pe(V("transpose", out=pv[:, c * 128:(c + 1) * 128], in_=src.ap[:, c * 128:(c + 1) * 128], identity=ident_b.ap),
                     reads=[src.b, ident_b.b], writes=[psb[0]])
            S.dve(V("tensor_copy", out=dst.ap, in_=pv.rearrange("p (c k) -> p c k", k=128)), reads=[psb[0]], writes=[dst.b])

        def p3_stage_a(s):
            x2 = x2s[s % 2]
            x2T = x2Ts[s % 2]
            r3 = r3s[s % 2]
            S.dma("sp", o_in.ap, oab_scr[s], reads=[oscr_b[s]], writes=[o_in.b])
            S.dma("sp", gt.ap, gate_scr[s], reads=[qscr_b[s]], writes=[gt.b])
            S.dma("sp", xin.ap, x_own[s * 128:(s + 1) * 128, :], writes=[xin.b])
            tr_bf_p3(o_in, oT)
            for half in range(2):
                for c in range(4):
                    S.pe(V("matmul", PS[:, 2 + half, :], lhsT=oT.ap[:, c, :], rhs=woa.ap[:, c, half * 512:(half + 1) * 512],
                           start=(c == 0), stop=(c == 3)), reads=[oT.b, woa.b], writes=[psb[2 + half]])
                S.dve(V("tensor_tensor", out=t1.ap[:, half * 512:(half + 1) * 512], in0=PS[:, 2 + half, :],
                        in1=gt.ap[:, half * 512:(half + 1) * 512], op=ALU.mult), reads=[psb[2 + half], gt.b], writes=[t1.b])
            for half in range(2):
                for c in range(4):
                    S.pe(V("matmul", PS[:, 2 + half, :], lhsT=oT.ap[:, 4 + c, :], rhs=wob.ap[:, c, half * 512:(half + 1) * 512],
                           start=(c == 0), stop=(c == 3)), reads=[oT.b, wob.b], writes=[psb[2 + half]])
                S.dve(V("tensor_tensor", out=t2.ap[:, half * 512:(half + 1) * 512], in0=PS[:, 2 + half, :],
                        in1=gt.ap[:, 1024 + half * 512:1024 + (half + 1) * 512], op=ALU.mult), reads=[psb[2 + half], gt.b], writes=[t2.b])
            S.pool(V("tensor_tensor", out=o_in.ap, in0=t1.ap, in1=t2.ap, op=ALU.add), reads=[t1.b, t2.b], writes=[o_in.b])
            tr_bf_p3(o_in, oT)
            for half in range(2):
                for c in range(8):
                    S.pe(V("matmul", PS[:, 2 + half, :], lhsT=oT.ap[:, c, :], rhs=wo.ap[:, c, half * 512:(half + 1) * 512],
                           start=(c == 0), stop=(c == 7)), reads=[oT.b, wo.b], writes=[psb[2 + half]])
                S.dve(V("tensor_tensor", out=x2.ap[:, half * 512:(half + 1) * 512], in0=PS[:, 2 + half, :],
                        in1=xin.ap[:, half * 512:(half + 1) * 512], op=ALU.add), reads=[psb[2 + half], xin.b], writes=[x2.b])
            S.act(V("activation", out=t1.ap, in_=x2.ap, func=AF.Square, accum_out=ss3.ap), reads=[x2.b], writes=[t1.b, ss3.b])
            S.dve(V("tensor_scalar", out=r3.ap, in0=ss3.ap, scalar1=1.0 / D, scalar2=EPS, op0=ALU.mult, op1=ALU.add),
                  reads=[ss3.b], writes=[r3.b])
            S.dve(V("reciprocal", out=r3.ap, in_=r3.ap), reads=[r3.b], writes=[r3.b])
            for c in range(8):
                bank = c // 4
                S.pe(V("transpose", out=PS[:, bank, (c % 4) * 128:(c % 4 + 1) * 128], in_=x2.ap[:, c * 128:(c + 1) * 128],
                       identity=ident_f.ap), reads=[x2.b, ident_f.b], writes=[psb[bank]])
            for c in range(8):
                bank = c // 4
                src = PS[:, bank, (c % 4) * 128:(c % 4 + 1) * 128]
                S.dve(V("tensor_scalar", out=x2T.ap[:, c, :], in0=src, scalar1=gffn.ap[:, c:c + 1], scalar2=None, op0=ALU.mult),
                      reads=[psb[bank], gffn.b], writes=[x2T.b])

        def p3_stage_b(s, hook):
            x2 = x2s[s % 2]
            x2T = x2Ts[s % 2]
            r3 = r3s[s % 2]
            for f4 in range(8):
                bank = 4 + (f4 % 2)
                for ff in range(4):
                    f = f4 * 4 + ff
                    for c in range(8):
                        S.pe(V("matmul", PS[:, bank, ff * 128:(ff + 1) * 128], lhsT=wu.ap[:, c, f * 128:(f + 1) * 128], rhs=x2T.ap[:, c, :],
                               start=(c == 0), stop=(c == 7), skip_group_check=True), reads=[wu.b, x2T.b], writes=[psb[bank]])
                S.act(V("activation", out=rl.ap, in_=PS[:, bank, :], func=AF.Relu), reads=[psb[bank]], writes=[rl.b])
                S.dve(V("tensor_tensor", out=hT.ap[:, f4 * 4:(f4 + 1) * 4, :].rearrange("p a b -> p (a b)"), in0=rl.ap,
                        in1=rl.ap, op=ALU.mult), reads=[rl.b], writes=[hT.b])
                hook()
            for half in range(2):
                for f in range(32):
                    S.pe(V("matmul", PS[:, 6 + half, :], lhsT=hT.ap[:, f, :], rhs=wd.ap[:, f, half * 512:(half + 1) * 512],
                           start=(f == 0), stop=(f == 31)), reads=[hT.b, wd.b], writes=[psb[6 + half]])
                    if f % 8 == 7:
                        hook()
                S.dve(V("scalar_tensor_tensor", out=x2.ap[:, half * 512:(half + 1) * 512], in0=PS[:, 6 + half, :], scalar=r3.ap,
                        in1=x2.ap[:, half * 512:(half + 1) * 512], op0=ALU.mult, op1=ALU.add),
                      reads=[psb[6 + half], r3.b, x2.b], writes=[x2.b])
            S.dma("sp", y_o[s * 128:(s + 1) * 128, :], x2.ap, reads=[x2.b])

        NS3 = min(NSLOT, int(os.environ.get("KNS3", NSLOT))) if "3" in PH else 0
        if NS3:
            p3_stage_a(0)
        for s in range(NS3):
            ops = []
            if s + 1 < NS3:
                S.begin_defer()
                p3_stage_a(s + 1)
                ops = S.end_defer()
            per = (len(ops) + 14) // 15
            p3_stage_b(s, lambda: S.replay(ops, per))
            S.replay(ops, len(ops))

        S.emit(nc, st)
    return nc


def _rope_tables(pos):
    half = 8
    inv = (np.float32(500000.0) ** (-np.arange(half, dtype=np.float32) / np.float32(half))).astype(np.float32)
    ang = pos.astype(np.float32)[:, None] * inv[None, :]
    return np.cos(ang).astype(np.float32), np.sin(ang).astype(np.float32)


def _band_tiles(table, true_tile_of_m, visible_fn):
    out = np.full((8, 128, 768), NEG, np.float32)
    qi = np.arange(128)[:, None]
    kj = np.arange(128)[None, :]
    for m in range(6):
        off = true_tile_of_m[m]
        if off is None:
            continue
        dist = off * 128 + qi - kj
        idx = np.clip(dist, -128, 128) + 128
        vis = visible_fn(off, qi, kj)
        for h in range(8):
            g = table[idx, h]
            out[h, :, m * 128:(m + 1) * 128] = np.where(vis, g, np.float32(NEG))
    return out


def _vis_prompt(off, qi, kj):
    dc = 2 * off + qi // 64 - kj // 64
    return (dc >= 0) & (dc <= 8)


_CACHE = {}


def kernel(x_prompt, x_sample, cache_k_a, cache_v_a, cache_k_idx, cache_k_b, cache_v_b,
           norm_mix, w_in, qnorm_a, knorm_a, knorm_idx, qnorm_b, knorm_b, rel_bias_b,
           w_o_a, w_o_b, w_out, norm_ffn, w_up, w_down):
    f = lambda a: np.ascontiguousarray(np.asarray(a, dtype=np.float32))
    x_prompt, x_sample = f(x_prompt), f(x_sample)
    w_in_ = f(w_in)[0]
    sp = np.cumsum([0, 512, 512, 512, 512, 64, 8, 512, 512, 512, 1024, 1024])
    seg = {n: (sp[i], sp[i + 1]) for i, n in enumerate(["qa", "ka", "va", "qi", "ki", "wi", "qb", "kb", "vb", "ga", "gb"])}
    order = ["ka", "kb", "va", "vb", "ki", "qa", "qb", "qi", "ga", "gb", "wi"]
    w_in_r = np.ascontiguousarray(np.concatenate([w_in_[:, seg[n][0]:seg[n][1]] for n in order], axis=1))
    rep = lambda g, n: np.tile(f(g)[0], n)
    g_kab = np.ascontiguousarray(np.broadcast_to(np.concatenate([rep(knorm_a, 8), rep(knorm_b, 8)])[None], (128, 1024)))
    g_qab = np.ascontiguousarray(np.broadcast_to(np.concatenate([rep(qnorm_a, 8), rep(qnorm_b, 8)])[None], (128, 1024)))
    g_ki = np.ascontiguousarray(np.broadcast_to(f(knorm_idx)[0][None], (128, 64)))
    gmix = np.ascontiguousarray(f(norm_mix)[0].reshape(8, 128).T)
    gffn = np.ascontiguousarray(f(norm_ffn)[0].reshape(8, 128).T)
    K1 = KSTEPS + 2
    p2 = np.concatenate([2.0 ** (-np.arange(K1)), 2.0 ** (1.0 - np.arange(K1))]).astype(np.float32)
    pow2 = np.ascontiguousarray(np.broadcast_to(p2[None], (128, 2 * K1)))
    table = f(rel_bias_b)[0]
    qi = np.arange(128)[:, None]
    kj = np.arange(128)[None, :]
    own_mask = np.where((kj // 64) <= (qi // 64), 0.0, -1e30).astype(np.float32)
    band_s = _band_tiles(table, [4, 3, 2, 1, 0, None], _vis_prompt)

    in_maps = []
    for c in range(8):
        b, hf = c // 2, c % 2
        xb = x_prompt[b].reshape(64, 128, D)
        order_t = []
        for i in range(32):
            order_t += [2 * i + hf, 2 * i + 1 - hf]
        xs_pad = np.zeros((128, D), np.float32)
        xs_pad[:64] = x_sample[c]
        x_all = np.concatenate([xb[order_t].reshape(64 * 128, D), xs_pad], axis=0)
        pos = np.concatenate([(np.array(order_t)[:, None] * 128 + np.arange(128)[None]).reshape(-1),
                              1024 + np.arange(64), np.zeros(64, np.int64)])
        cos_all, sin_all = _rope_tables(pos)
        x_own = np.concatenate([xb[hf::2].reshape(32 * 128, D), xs_pad], axis=0)
        foreign = np.full((128, 128), 0.0 if hf == 1 else -1e30, np.float32)
        masks = np.ascontiguousarray(np.concatenate([own_mask, foreign], axis=1))
        if hf == 0:
            offs = [4, 3, 2, 1, 0, None]
        else:
            offs = [None, 4, 1, 2, 0 - 0, 0]
            offs = [4, None, 2, 3, 0, 1]
        band_p = _band_tiles(table, offs, _vis_prompt)
        in_maps.append(dict(
            x_all=np.ascontiguousarray(x_all), cos_all=cos_all, sin_all=sin_all, x_own=np.ascontiguousarray(x_own),
            ck_a=f(cache_k_a)[0, c].reshape(1024, 512), cv_a=f(cache_v_a)[0, c].reshape(1024, 512),
            ck_i=f(cache_k_idx)[0, c], ck_b=f(cache_k_b)[0, c].reshape(512, 512), cv_b=f(cache_v_b)[0, c].reshape(512, 512),
            w_in=w_in_r, w_oa=f(w_o_a)[0], w_ob=f(w_o_b)[0], w_out=f(w_out)[0], w_up=f(w_up)[0], w_down=f(w_down)[0],
            gmix=gmix, gffn=gffn, g_kab=g_kab, g_qab=g_qab, g_ki=g_ki, ident=np.eye(128, dtype=np.float32),
            pow2=pow2, masks=masks, band_p=band_p, band_s=band_s))

    if "nc" not in _CACHE:
        _CACHE["nc"] = build_program()
    import os
    NCORE = int(os.environ.get("KCORES", "8"))
    res = run_bass_kernel_spmd(_CACHE["nc"], in_maps[:NCORE], core_ids=list(range(NCORE)))
    R = list(res.results) + [res.results[0]] * (8 - NCORE)

    y_p = np.zeros((4, 64, 128, D), np.float32)
    ka_p = np.zeros((4, 64, 128, 512), np.float32)
    va_p = np.zeros((4, 64, 128, 512), np.float32)
    ki_p = np.zeros((4, 64, 128, 64), np.float32)
    kb_p = np.zeros((4, 4, 128, 512), np.float32)
    vb_p = np.zeros((4, 4, 128, 512), np.float32)
    y_s = np.zeros((8, 64, D), np.float32)
    ka_s = np.zeros((8, 64, 512), np.float32)
    va_s = np.zeros((8, 64, 512), np.float32)
    ki_s = np.zeros((8, 64, 64), np.float32)
    kb_s = np.zeros((8, 64, 512), np.float32)
    vb_s = np.zeros((8, 64, 512), np.float32)
    for c in range(8):
        b, hf = c // 2, c % 2
        r = R[c]
        y_p[b, hf::2] = r["y_o"][:4096].reshape(32, 128, D)
        ka_p[b, hf::2] = r["ka_o"][:4096].reshape(32, 128, 512)
        va_p[b, hf::2] = r["va_o"][:4096].reshape(32, 128, 512)
        ki_p[b, hf::2] = r["ki_o"][:4096].reshape(32, 128, 64)
        kb_p[b, hf::2] = r["kb_o"][:256].reshape(2, 128, 512)
        vb_p[b, hf::2] = r["vb_o"][:256].reshape(2, 128, 512)
        y_s[c] = r["y_o"][4096:4160]
        ka_s[c] = r["ka_o"][4096:4160]
        va_s[c] = r["va_o"][4096:4160]
        ki_s[c] = r["ki_o"][4096:4160]
        kb_s[c] = r["kb_o"][256:320]
        vb_s[c] = r["vb_o"][256:320]
    return (y_p.reshape(4, 8192, D), y_s,
            ka_p.reshape(1, 4, 8192, 8, 64), va_p.reshape(1, 4, 8192, 8, 64), ki_p.reshape(1, 4, 8192, 64),
            kb_p.reshape(1, 4, 512, 8, 64), vb_p.reshape(1, 4, 512, 8, 64),
            ka_s.reshape(1, 8, 64, 8, 64), va_s.reshape(1, 8, 64, 8, 64), ki_s.reshape(1, 8, 64, 64),
            kb_s.reshape(1, 8, 64, 8, 64), vb_s.reshape(1, 8, 64, 8, 64))
```

```python
import numpy as np
from contextlib import ExitStack
import concourse.bass as bass
import concourse.mybir as mybir
from concourse.bass_utils import run_bass_kernel_spmd

F32 = mybir.dt.float32
BF16 = mybir.dt.bfloat16
U8 = mybir.dt.uint8
ALU = mybir.AluOpType
AF = mybir.ActivationFunctionType
AX = mybir.AxisListType

D = 1024
NT = 65
NTS = 73
NSLOT = 33
DIN = 5704
KSTEPS = 16
NEG = -30000.0
EPS = 1e-6


class Buf:
    __slots__ = ("w", "r")

    def __init__(self):
        self.w = None
        self.r = {}


class Instr:
    __slots__ = ("eng", "fn", "deps", "signal", "sem", "val", "is_dma", "uid")


ENGS = ("pe", "act", "dve", "pool", "sp")
NRING = 8


class Sched:
    def __init__(self):
        self.q = {e: [] for e in ENGS}
        self.uid = 0
        self.bar = []
        self.bar_done = set()
        self.deferred = None

    def begin_defer(self):
        self.deferred = []

    def end_defer(self):
        ops = self.deferred
        self.deferred = None
        return ops

    def replay(self, ops, k):
        for _ in range(min(k, len(ops))):
            self.add(*ops.pop(0))

    def barrier(self):
        lst = []
        for e in ENGS:
            comp = [i for i in self.q[e] if not i.is_dma]
            if comp:
                lst.append(comp[-1])
            lst += [i for i in self.q[e] if i.is_dma][-NRING:]
        self.bar = lst
        self.bar_done = set()

    def add(self, eng, fn, reads=(), writes=(), dma=False):
        import os
        if self.deferred is not None:
            self.deferred.append((eng, fn, tuple(reads), tuple(writes), dma))
            return None
        if self.uid >= int(os.environ.get("KMAX", "100000000")):
            return None
        ins = Instr()
        ins.eng = eng
        ins.fn = fn
        ins.is_dma = dma
        ins.signal = dma
        ins.uid = self.uid
        self.uid += 1
        deps = {}

        def need(d, raw):
            if d is None or d is ins:
                return
            if (not d.is_dma) and (not dma) and d.eng == eng:
                if eng == "pe":
                    return
            deps[d.uid] = d

        for b in reads:
            need(b.w, True)
        for b in writes:
            need(b.w, False)
            for rd in b.r.values():
                need(rd, False)
        if self.bar and eng not in self.bar_done:
            self.bar_done.add(eng)
            for d in self.bar:
                deps[d.uid] = d
        for b in reads:
            b.r[("dma", ins.uid) if dma else eng] = ins
        for b in writes:
            b.w = ins
            b.r = {}
        ins.deps = list(deps.values())
        for d in ins.deps:
            d.signal = True
        self.q[eng].append(ins)
        return ins

    def pe(self, fn, reads=(), writes=()):
        return self.add("pe", fn, reads, writes)

    def act(self, fn, reads=(), writes=()):
        return self.add("act", fn, reads, writes)

    def dve(self, fn, reads=(), writes=()):
        return self.add("dve", fn, reads, writes)

    def pool(self, fn, reads=(), writes=()):
        return self.add("pool", fn, reads, writes)

    def dma(self, queue, out, in_, reads=(), writes=(), **kw):
        return self.add(queue, lambda e: e.dma_start(out=out, in_=in_, **kw), reads, writes, dma=True)

    def emit(self, nc, stack):
        esem = {e: stack.enter_context(nc.semaphore("s_" + e)) for e in ENGS}
        rings = {e: [stack.enter_context(nc.semaphore("r_%s%d" % (e, i))) for i in range(NRING)]
                 for e in ("sp", "pool", "act")}
        final_ring = {}
        for e in ENGS:
            cnt = 0
            nd = 0
            for ins in self.q[e]:
                if ins.is_dma:
                    ins.sem = rings[e][nd % NRING]
                    ins.val = 16 * (nd // NRING + 1)
                    final_ring[(e, nd % NRING)] = (ins.sem, ins.val)
                    nd += 1
                elif ins.signal:
                    cnt += 1
                    ins.sem = esem[e]
                    ins.val = cnt
        block = stack.enter_context(nc.Block())
        handles = {"pe": "tensor", "act": "scalar", "dve": "vector", "pool": "gpsimd", "sp": "sync"}

        def make(e):
            def body(h):
                waited = {}

                def wait(sem, val):
                    k = id(sem)
                    if waited.get(k, 0) >= val:
                        return
                    waited[k] = val
                    h.wait_ge(sem, val)

                nd = 0
                for ins in self.q[e]:
                    for d in ins.deps:
                        wait(d.sem, d.val)
                    if ins.is_dma:
                        if nd >= NRING:
                            wait(ins.sem, ins.val - 16)
                        nd += 1
                        ins.fn(h).then_inc(ins.sem, 16)
                    else:
                        r = ins.fn(h)
                        if ins.signal:
                            r.then_inc(ins.sem, 1)
                if e == "sp":
                    for (sem, val) in final_ring.values():
                        wait(sem, val)
            return body

        for e in ENGS:
            getattr(block, handles[e])(make(e))


def V(name, *a, **k):
    return lambda e: getattr(e, name)(*a, **k)


class T:
    __slots__ = ("ap", "b")

    def __init__(self, ap):
        self.ap = ap
        self.b = Buf()


_DSZ = {F32: 4, BF16: 2, U8: 1}


class Arena:
    def __init__(self, ap, base, limit):
        self.ap = ap
        self.off = base
        self.limit = limit

    def alloc(self, shape, dt):
        n = 1
        for s in shape[1:]:
            n *= s
        nb = (n * _DSZ[dt] + 31) // 32 * 32
        assert self.off + nb <= self.limit, ("arena overflow", self.off + nb, self.limit)
        v = self.ap[:, self.off // 2:(self.off + nb) // 2]
        self.off += nb
        if dt != BF16:
            v = v.bitcast(dt)
        v = v[:, 0:n]
        if len(shape) == 3:
            v = v.rearrange("p (a b) -> p a b", b=shape[2])
        elif len(shape) == 4:
            v = v.rearrange("p (a b c) -> p a b c", b=shape[2], c=shape[3])
        if shape[0] != 128:
            v = v[0:shape[0]]
        return T(v)


def build_program():
    nc = bass.Bass("TRN2", target_bir_lowering=False)

    def din(name, shape, dt=F32):
        return nc.dram_tensor(name, list(shape), dt, kind="ExternalInput").ap()

    def dout(name, shape, dt=F32):
        return nc.dram_tensor(name, list(shape), dt, kind="ExternalOutput").ap()

    def dscr(name, shape, dt):
        return nc.dram_tensor(name, list(shape), dt, kind="Internal").ap()

    x_all = din("x_all", [NT * 128, D])
    cos_all = din("cos_all", [NT * 128, 8])
    sin_all = din("sin_all", [NT * 128, 8])
    x_own = din("x_own", [NSLOT * 128, D])
    ck_a = din("ck_a", [1024, 512])
    cv_a = din("cv_a", [1024, 512])
    ck_i = din("ck_i", [1024, 64])
    ck_b = din("ck_b", [512, 512])
    cv_b = din("cv_b", [512, 512])
    w_in = din("w_in", [D, DIN])
    w_oa = din("w_oa", [512, D])
    w_ob = din("w_ob", [512, D])
    w_out = din("w_out", [D, D])
    w_up = din("w_up", [D, 4096])
    w_down = din("w_down", [4096, D])
    gmix_d = din("gmix", [128, 8])
    gffn_d = din("gffn", [128, 8])
    g_kab_d = din("g_kab", [128, 1024])
    g_qab_d = din("g_qab", [128, 1024])
    g_ki_d = din("g_ki", [128, 64])
    ident_d = din("ident", [128, 128])
    pow2_d = din("pow2", [128, 2 * (KSTEPS + 2)])
    mask_d = din("masks", [128, 256])
    bandp_d = din("band_p", [8, 128, 768])
    bands_d = din("band_s", [8, 128, 768])

    y_o = dout("y_o", [NSLOT * 128, D])
    ka_o = dout("ka_o", [NSLOT * 128, 512])
    va_o = dout("va_o", [NSLOT * 128, 512])
    ki_o = dout("ki_o", [NSLOT * 128, 64])
    kb_o = dout("kb_o", [3 * 128, 512])
    vb_o = dout("vb_o", [3 * 128, 512])

    kT_scr = dscr("kT_scr", [NTS, 128, 8, 128], BF16)
    v_scr = dscr("v_scr", [NTS, 128, 16, 65], BF16)
    q_scr = dscr("q_scr", [NSLOT, 128, 8, 128], BF16)
    qi_scr = dscr("qi_scr", [NSLOT, 128, 4, 128], BF16)
    gate_scr = dscr("gate_scr", [NSLOT, 128, 2048], BF16)
    wi_scr = dscr("wi_scr", [NSLOT, 128, 8], F32)
    oab_scr = dscr("oab_scr", [NSLOT, 128, 1024], BF16)
    wbf_scr = dscr("wbf_scr", [128, 80 * 1024], BF16)
    wscr_b = Buf()
    kscr_b = [Buf() for _ in range(NTS)]
    vscr_b = [Buf() for _ in range(NTS)]
    qscr_b = [Buf() for _ in range(NSLOT)]
    oscr_b = [Buf() for _ in range(NSLOT)]

    S = Sched()
    with ExitStack() as st:
        ARENA_BYTES = 206 * 1024
        arena_t = st.enter_context(nc.sbuf_tensor("arena", [128, ARENA_BYTES // 2], BF16))
        PS = st.enter_context(nc.psum_tensor("ps", [128, 8, 512], F32))
        psb = [Buf() for _ in range(8)]

        def PSb16(bank):
            return PS[:, bank, :].bitcast(BF16)

        A0 = Arena(arena_t, 0, ARENA_BYTES)
        ident_f = A0.alloc([128, 128], F32)
        ident_b = A0.alloc([128, 128], BF16)
        I4 = A0.alloc([128, 512], BF16)
        gmix = A0.alloc([128, 8], F32)
        gffn = A0.alloc([128, 8], F32)
        pow2 = A0.alloc([128, 2 * (KSTEPS + 2)], F32)
        masks = A0.alloc([128, 256], F32)
        kiT = A0.alloc([128, NTS * 128], BF16)
        kiT_b = [Buf() for _ in range(NTS)]
        PBASE = A0.off

        S.dma("sp", ident_f.ap, ident_d, writes=[ident_f.b])
        S.dma("sp", gmix.ap, gmix_d, writes=[gmix.b])
        S.dma("sp", gffn.ap, gffn_d, writes=[gffn.b])
        S.dma("sp", pow2.ap, pow2_d, writes=[pow2.b])
        S.dma("sp", masks.ap, mask_d, writes=[masks.b])
        S.dve(V("tensor_copy", out=ident_b.ap, in_=ident_f.ap), reads=[ident_f.b], writes=[ident_b.b])
        for j in range(4):
            S.dve(V("tensor_copy", out=I4.ap[:, j * 128:(j + 1) * 128], in_=ident_f.ap), reads=[ident_f.b], writes=[I4.b])

        A1 = Arena(arena_t, PBASE, ARENA_BYTES)
        w1 = A1.alloc([128, 8, DIN], BF16)
        g_kab = A1.alloc([128, 1024], F32)
        g_qab = A1.alloc([128, 1024], F32)
        g_ki = A1.alloc([128, 64], F32)
        xt = [A1.alloc([128, D], F32) for _ in range(2)]
        xT = [A1.alloc([128, 8, 128], BF16) for _ in range(2)]
        xT_b = [[Buf() for _ in range(8)] for _ in range(2)]
        cs = [A1.alloc([128, 8], F32) for _ in range(4)]
        sn = [A1.alloc([128, 8], F32) for _ in range(4)]
        ssx = [A1.alloc([128, 1], F32) for _ in range(2)]
        rstd = [A1.alloc([128, 1], F32) for _ in range(2)]
        z_k = [A1.alloc([128, 1024], F32) for _ in range(2)]
        z_v = [A1.alloc([128, 1024], F32) for _ in range(2)]
        z_q = [A1.alloc([128, 1024], F32) for _ in range(2)]
        z_qi = [A1.alloc([128, 512], F32) for _ in range(2)]
        z_ki = [A1.alloc([128, 64], F32) for _ in range(2)]
        wi_sb = [A1.alloc([128, 8], F32) for _ in range(2)]
        sq_tmps = [A1.alloc([128, 1024], F32) for _ in range(2)]
        hs_k = [A1.alloc([128, 16], F32) for _ in range(2)]
        hs_q = [A1.alloc([128, 16], F32) for _ in range(2)]
        hs_i = [A1.alloc([128, 1], F32) for _ in range(2)]
        knbs = [A1.alloc([128, 1024], BF16) for _ in range(2)]
        qnb = A1.alloc([128, 1024], BF16)
        qib = A1.alloc([128, 512], BF16)
        kibs = [A1.alloc([128, 128], BF16) for _ in range(2)]
        kT_sbs = [A1.alloc([128, 8, 128], BF16) for _ in range(2)]
        qT_sb = A1.alloc([128, 8, 128], BF16)
        qiT_sb = A1.alloc([128, 4, 128], BF16)
        vaug = [A1.alloc([128, 16, 65], BF16) for _ in range(2)]
        gates = A1.alloc([128, 2048], BF16)
        rt_default = [A1.alloc([128, 16, 8], F32) for _ in range(4)]

        w1_b = [Buf() for _ in range(3)]
        wstg = [A1.alloc([128, 1024], F32) for _ in range(2)]
        rtq = []
        for i_ in range(4):
            t_ = T(wstg[0].ap[:, i_ * 128:(i_ + 1) * 128].rearrange("p (h e) -> p h e", e=8))
            t_.b = wstg[0].b
            rtq.append(t_)
        wj = 0
        for c0 in range(0, DIN, 1024):
            c1 = min(DIN, c0 + 1024)
            n = c1 - c0
            for c in range(8):
                stg = wstg[wj % 2]
                S.dma("sp", stg.ap[:, 0:n], w_in[c * 128:(c + 1) * 128, c0:c1], writes=[stg.b])
                if wj % 2 == 0:
                    S.dve(V("tensor_copy", out=w1.ap[:, c, c0:c1], in_=stg.ap[:, 0:n]), reads=[stg.b], writes=[w1_b[c0 // 2048]])
                else:
                    S.act(V("activation", out=w1.ap[:, c, c0:c1], in_=stg.ap[:, 0:n], func=AF.Copy), reads=[stg.b], writes=[w1_b[c0 // 2048]])
                wj += 1
        S.dma("sp", g_kab.ap, g_kab_d, writes=[g_kab.b])
        S.dma("sp", g_qab.ap, g_qab_d, writes=[g_qab.b])
        S.dma("sp", g_ki.ap, g_ki_d, writes=[g_ki.b])
        for i in range(2):
            S.pool(V("memset", vaug[i].ap[:, :, 64:65], 1.0), writes=[vaug[i].b])

        zrot = [0]
        trot = [0]

        def headnorm(z, nh, gains, hs, sq_tmp):
            n = nh * 64
            S.act(V("activation", out=sq_tmp.ap[:, 0:n], in_=z.ap[:, 0:n], func=AF.Square), reads=[z.b], writes=[sq_tmp.b])
            S.dve(V("tensor_reduce", out=hs.ap[:, 0:nh], in_=sq_tmp.ap[:, 0:n].rearrange("p (h d) -> p h d", d=64),
                    axis=AX.X, op=ALU.add), reads=[sq_tmp.b], writes=[hs.b])
            S.dve(V("tensor_scalar", out=hs.ap[:, 0:nh], in0=hs.ap[:, 0:nh], scalar1=1.0 / 64, scalar2=EPS,
                    op0=ALU.mult, op1=ALU.add), reads=[hs.b], writes=[hs.b])
            S.act(V("activation", out=hs.ap[:, 0:nh], in_=hs.ap[:, 0:nh], func=AF.Sqrt), reads=[hs.b], writes=[hs.b])
            S.dve(V("reciprocal", out=hs.ap[:, 0:nh], in_=hs.ap[:, 0:nh]), reads=[hs.b], writes=[hs.b])
            zv = z.ap[:, 0:n].rearrange("p (h d) -> p h d", d=64)
            S.dve(V("tensor_tensor", out=zv, in0=zv, in1=hs.ap[:, 0:nh].unsqueeze(2).to_broadcast([128, nh, 64]),
                    op=ALU.mult), reads=[z.b, hs.b], writes=[z.b])
            S.dve(V("tensor_tensor", out=z.ap[:, 0:n], in0=z.ap[:, 0:n], in1=gains.ap, op=ALU.mult),
                  reads=[z.b, gains.b], writes=[z.b])

        def rope(z, col0, nh, cs_t, sn_t, rt=None):
            rt = rt_default if rt is None else rt
            v = z.ap[:, col0:col0 + nh * 64].rearrange("p (h d) -> p h d", d=64)
            x1 = v[:, :, 0:8]
            x2 = v[:, :, 8:16]
            cb = cs_t.ap.unsqueeze(1).to_broadcast([128, nh, 8])
            sb_ = sn_t.ap.unsqueeze(1).to_broadcast([128, nh, 8])
            t = [r.ap[:, 0:nh, :] for r in rt]
            rd = [z.b, cs_t.b, sn_t.b]
            S.pool(V("tensor_tensor", out=t[0], in0=x1, in1=cb, op=ALU.mult), reads=rd, writes=[rt[0].b])
            S.pool(V("tensor_tensor", out=t[1], in0=x2, in1=sb_, op=ALU.mult), reads=rd, writes=[rt[1].b])
            S.pool(V("tensor_tensor", out=t[2], in0=x2, in1=cb, op=ALU.mult), reads=rd, writes=[rt[2].b])
            S.pool(V("tensor_tensor", out=t[3], in0=x1, in1=sb_, op=ALU.mult), reads=rd, writes=[rt[3].b])
            S.pool(V("tensor_tensor", out=x1, in0=t[0], in1=t[1], op=ALU.subtract), reads=[rt[0].b, rt[1].b], writes=[z.b])
            S.pool(V("tensor_tensor", out=x2, in0=t[2], in1=t[3], op=ALU.add), reads=[rt[2].b, rt[3].b], writes=[z.b])

        def transposes_bf(src, ncol_blocks, dst, dst_c0, bank=6):
            pv = PSb16(bank)
            for c in range(ncol_blocks):
                S.pe(V("transpose", out=pv[:, c * 128:(c + 1) * 128], in_=src.ap[:, c * 128:(c + 1) * 128],
                       identity=ident_b.ap), reads=[src.b, ident_b.b], writes=[psb[bank]])
            S.dve(V("tensor_copy", out=dst.ap[:, dst_c0:dst_c0 + ncol_blocks, :],
                    in_=pv[:, 0:ncol_blocks * 128].rearrange("p (c k) -> p c k", k=128)),
                  reads=[psb[bank]], writes=[dst.b])

        def ki_to_kiT(src_f32_ap, src_b, tidx, kib):
            S.pool(V("tensor_copy", out=kib.ap[:, 0:64], in_=src_f32_ap), reads=[src_b], writes=[kib.b])
            S.pool(V("tensor_copy", out=kib.ap[:, 64:128], in_=src_f32_ap), reads=[src_b], writes=[kib.b])
            bank = 6
            pv = PSb16(bank)
            S.pe(V("transpose", out=pv[:, 0:128], in_=kib.ap, identity=ident_b.ap), reads=[kib.b, ident_b.b], writes=[psb[bank]])
            S.dve(V("tensor_copy", out=kiT.ap[:, tidx * 128:(tidx + 1) * 128], in_=pv[:, 0:128]),
                  reads=[psb[bank]], writes=[kiT_b[tidx]])

        KCH = [(0, 512, "ka"), (512, 1024, "kb"), (1024, 1536, "va"), (1536, 2048, "vb"), (2048, 2112, "ki")]
        QCH = [(2112, 2624, "qa"), (2624, 3136, "qb"), (3136, 3648, "qi"), (3648, 4160, "g0"), (4160, 4672, "g1"),
               (4672, 5184, "g2"), (5184, 5696, "g3"), (5696, 5704, "wi")]

        def p1_load(t):
            p = t % 2
            S.dma("sp", xt[p].ap, x_all[t * 128:(t + 1) * 128, :], writes=[xt[p].b])
            S.dma("sp", cs[t % 4].ap, cos_all[t * 128:(t + 1) * 128, :], writes=[cs[t % 4].b])
            S.dma("sp", sn[t % 4].ap, sin_all[t * 128:(t + 1) * 128, :], writes=[sn[t % 4].b])

        def p1_front(t, after_chunk=None):
            own = (t % 2 == 0)
            slot = t // 2
            p = t % 2
            po = slot % 2
            xb = xt[p]
            sq_tmp = sq_tmps[p]
            S.act(V("activation", out=sq_tmp.ap, in_=xb.ap, func=AF.Square, accum_out=ssx[p].ap),
                  reads=[xb.b], writes=[sq_tmp.b, ssx[p].b])
            S.dve(V("tensor_scalar", out=rstd[p].ap, in0=ssx[p].ap, scalar1=1.0 / D, scalar2=EPS, op0=ALU.mult, op1=ALU.add),
                  reads=[ssx[p].b], writes=[rstd[p].b])
            S.act(V("activation", out=rstd[p].ap, in_=rstd[p].ap, func=AF.Sqrt), reads=[rstd[p].b], writes=[rstd[p].b])
            S.dve(V("reciprocal", out=rstd[p].ap, in_=rstd[p].ap), reads=[rstd[p].b], writes=[rstd[p].b])
            for c in range(8):
                bank = c // 4
                S.pe(V("transpose", out=PS[:, bank, (c % 4) * 128:(c % 4 + 1) * 128], in_=xb.ap[:, c * 128:(c + 1) * 128],
                       identity=ident_f.ap), reads=[xb.b, ident_f.b], writes=[psb[bank]])
            for c in range(8):
                bank = c // 4
                src = PS[:, bank, (c % 4) * 128:(c % 4 + 1) * 128]
                S.dve(V("tensor_scalar", out=xT[p].ap[:, c, :], in0=src, scalar1=gmix.ap[:, c:c + 1], scalar2=None,
                        op0=ALU.mult), reads=[psb[bank], gmix.b], writes=[xT_b[p][c]])
            if t + 1 < NT1:
                p1_load(t + 1)
            chunks = KCH + (QCH if own else [])
            for (c0, c1, kind) in chunks:
                n = c1 - c0
                bank = 2 + (zrot[0] % 4)
                zrot[0] += 1
                for c in range(8):
                    S.pe(V("matmul", PS[:, bank, 0:n], lhsT=xT[p].ap[:, c, :], rhs=w1.ap[:, c, c0:c1], start=(c == 0), stop=(c == 7)),
                         reads=[xT_b[p][c], w1_b[c0 // 2048], w1_b[(c1 - 1) // 2048]], writes=[psb[bank]])
                src = PS[:, bank, 0:n]
                rd = [psb[bank], rstd[p].b]
                sc = rstd[p].ap
                if kind == "ka":
                    S.act(V("activation", out=z_k[p].ap[:, 0:512], in_=src, func=AF.Copy, scale=sc), reads=rd, writes=[z_k[p].b])
                elif kind == "kb":
                    S.act(V("activation", out=z_k[p].ap[:, 512:1024], in_=src, func=AF.Copy, scale=sc), reads=rd, writes=[z_k[p].b])
                elif kind == "va":
                    S.act(V("activation", out=z_v[p].ap[:, 0:512], in_=src, func=AF.Copy, scale=sc), reads=rd, writes=[z_v[p].b])
                elif kind == "vb":
                    S.act(V("activation", out=z_v[p].ap[:, 512:1024], in_=src, func=AF.Copy, scale=sc), reads=rd, writes=[z_v[p].b])
                elif kind == "ki":
                    S.act(V("activation", out=z_ki[p].ap, in_=src, func=AF.Copy, scale=sc), reads=rd, writes=[z_ki[p].b])
                elif kind == "qa":
                    S.act(V("activation", out=z_q[po].ap[:, 0:512], in_=src, func=AF.Copy, scale=sc), reads=rd, writes=[z_q[po].b])
                elif kind == "qb":
                    S.act(V("activation", out=z_q[po].ap[:, 512:1024], in_=src, func=AF.Copy, scale=sc), reads=rd, writes=[z_q[po].b])
                elif kind == "qi":
                    S.act(V("activation", out=z_qi[po].ap, in_=src, func=AF.Copy, scale=sc), reads=rd, writes=[z_qi[po].b])
                elif kind[0] == "g":
                    j = int(kind[1])
                    S.act(V("activation", out=gates.ap[:, j * 512:(j + 1) * 512], in_=src, func=AF.Sigmoid, scale=sc), reads=rd, writes=[gates.b])
                elif kind == "wi":
                    S.act(V("activation", out=wi_sb[po].ap, in_=src, func=AF.Copy, scale=sc), reads=rd, writes=[wi_sb[po].b])
                if after_chunk is not None:
                    after_chunk()
        def p1_back_k(t):
            own = (t % 2 == 0)
            slot = t // 2
            p = t % 2
            po = slot % 2
            sq_tmp = sq_tmps[p]
            knb = knbs[p]
            kib = kibs[p]
            kT_sb = kT_sbs[p]
            headnorm(z_k[p], 16, g_kab, hs_k[p], sq_tmp)
            rope(z_k[p], 0, 8, cs[t % 4], sn[t % 4])
            headnorm(z_ki[p], 1, g_ki, hs_i[p], sq_tmp)
            rope(z_ki[p], 0, 1, cs[t % 4], sn[t % 4])
            if own:
                r0 = slot * 128
                S.dma("sp", ka_o[r0:r0 + 128, :], z_k[p].ap[:, 0:512], reads=[z_k[p].b])
                S.dma("sp", va_o[r0:r0 + 128, :], z_v[p].ap[:, 0:512], reads=[z_v[p].b])
                S.dma("sp", ki_o[r0:r0 + 128, :], z_ki[p].ap, reads=[z_ki[p].b])
                if slot >= 30:
                    rb = (slot - 30) * 128
                    S.dma("sp", kb_o[rb:rb + 128, :], z_k[p].ap[:, 512:1024], reads=[z_k[p].b])
                    S.dma("sp", vb_o[rb:rb + 128, :], z_v[p].ap[:, 512:1024], reads=[z_v[p].b])
            S.dve(V("tensor_copy", out=knb.ap, in_=z_k[p].ap), reads=[z_k[p].b], writes=[knb.b])
            transposes_bf(knb, 8, kT_sb, 0)
            S.dma("sp", kT_scr[t], kT_sb.ap, reads=[kT_sb.b], writes=[kscr_b[t]])
            S.act(V("activation", out=vaug[p].ap[:, :, 0:64], in_=z_v[p].ap.rearrange("p (h d) -> p h d", d=64), func=AF.Copy),
                  reads=[z_v[p].b], writes=[vaug[p].b])
            S.dma("sp", v_scr[t], vaug[p].ap, reads=[vaug[p].b], writes=[vscr_b[t]])
            ki_to_kiT(z_ki[p].ap, z_ki[p].b, t, kib)
            if own:
                S.dma("sp", gate_scr[slot], gates.ap, reads=[gates.b], writes=[qscr_b[slot]])
                S.dma("sp", wi_scr[slot], wi_sb[po].ap, reads=[wi_sb[po].b], writes=[qscr_b[slot]])

        def p1_back_q(t):
            slot = t // 2
            p = t % 2
            po = slot % 2
            sq_tmp = sq_tmps[p]
            headnorm(z_q[po], 16, g_qab, hs_q[po], sq_tmp)
            rope(z_q[po], 0, 8, cs[t % 4], sn[t % 4], rtq)
            rope(z_qi[po], 0, 8, cs[t % 4], sn[t % 4], rtq)
            S.dve(V("tensor_copy", out=qnb.ap, in_=z_q[po].ap), reads=[z_q[po].b], writes=[qnb.b])
            transposes_bf(qnb, 8, qT_sb, 0, bank=7)
            S.dma("sp", q_scr[slot], qT_sb.ap, reads=[qT_sb.b], writes=[qscr_b[slot]])
            S.dve(V("tensor_copy", out=qib.ap, in_=z_qi[po].ap), reads=[z_qi[po].b], writes=[qib.b])
            transposes_bf(qib, 4, qiT_sb, 0, bank=7)
            S.dma("sp", qi_scr[slot], qiT_sb.ap, reads=[qiT_sb.b], writes=[qscr_b[slot]])

        import os
        PH = os.environ.get("KPH", "123")
        NT1 = int(os.environ.get("KNT", NT))
        for m in (range(8) if "c" in PH or "2" in PH else []):
            p = m % 2
            tidx = 65 + m
            xb = xt[p]
            knb = knbs[p]
            kT_sb = kT_sbs[p]
            S.dma("sp", xb.ap[:, 0:512], ck_a[m * 128:(m + 1) * 128, :], writes=[xb.b])
            S.dma("sp", xb.ap[:, 512:1024], cv_a[m * 128:(m + 1) * 128, :], writes=[xb.b])
            S.dve(V("tensor_copy", out=knb.ap[:, 0:512], in_=xb.ap[:, 0:512]), reads=[xb.b], writes=[knb.b])
            S.pool(V("tensor_copy", out=vaug[p].ap[:, 0:8, 0:64], in_=xb.ap[:, 512:1024].rearrange("p (h d) -> p h d", d=64)),
                   reads=[xb.b], writes=[vaug[p].b])
            if m >= 4:
                zb = z_k[p]
                S.dma("sp", zb.ap[:, 0:512], ck_b[(m - 4) * 128:(m - 3) * 128, :], writes=[zb.b])
                S.dma("sp", zb.ap[:, 512:1024], cv_b[(m - 4) * 128:(m - 3) * 128, :], writes=[zb.b])
                S.dve(V("tensor_copy", out=knb.ap[:, 512:1024], in_=zb.ap[:, 0:512]), reads=[zb.b], writes=[knb.b])
                S.pool(V("tensor_copy", out=vaug[p].ap[:, 8:16, 0:64], in_=zb.ap[:, 512:1024].rearrange("p (h d) -> p h d", d=64)),
                       reads=[zb.b], writes=[vaug[p].b])
            nb_ = 8 if m >= 4 else 4
            transposes_bf(knb, nb_, kT_sb, 0)
            S.dma("sp", kT_scr[tidx][:, 0:nb_, :], kT_sb.ap[:, 0:nb_, :], reads=[kT_sb.b], writes=[kscr_b[tidx]])
            S.dma("sp", v_scr[tidx][:, 0:2 * nb_, :], vaug[p].ap[:, 0:2 * nb_, :], reads=[vaug[p].b], writes=[vscr_b[tidx]])
            zi = z_ki[p]
            S.dma("sp", zi.ap, ck_i[m * 128:(m + 1) * 128, :], writes=[zi.b])
            ki_to_kiT(zi.ap, zi.b, tidx, kibs[p])

        def merge_ops(a_, b_):
            out = []
            i = j = 0
            while i < len(a_) or j < len(b_):
                if j >= len(b_) or (i < len(a_) and i * max(1, len(b_)) <= j * max(1, len(a_))):
                    out.append(a_[i])
                    i += 1
                else:
                    out.append(b_[j])
                    j += 1
            return out

        p1_load(0)
        p1_front(0)
        pend_q = []
        for t in range(NT1):
            S.begin_defer()
            p1_back_k(t)
            ops_k = S.end_defer()
            ops = merge_ops(ops_k, pend_q)
            pend_q = []
            if t % 2 == 0:
                S.begin_defer()
                p1_back_q(t)
                pend_q = S.end_defer()
            if t + 1 < NT1:
                nch = 13 if (t + 1) % 2 == 0 else 5
                per = (len(ops) + nch - 1) // nch
                p1_front(t + 1, after_chunk=lambda: S.replay(ops, per))
            S.replay(ops, len(ops))
        S.replay(pend_q, len(pend_q))

        S.barrier()
        A2 = Arena(arena_t, PBASE, ARENA_BYTES)
        NMAX = 64 * 128
        isc = [A2.alloc([128, NMAX], F32) for _ in range(2)]
        junk = A2.alloc([128, NMAX], U8)
        mb = [A2.alloc([128, NMAX], BF16) for _ in range(2)]
        kbuf = [A2.alloc([128, 4, 4, 128], BF16) for _ in range(2)]
        vbuf = [A2.alloc([128, 4, 8, 65], BF16) for _ in range(2)]
        bandb = A2.alloc([128, 8, 768], BF16)
        bstage = A2.alloc([128, 768], F32)
        qTb = [A2.alloc([128, 16, 128], BF16) for _ in range(2)]
        qiTb = [A2.alloc([128, 8, 128], BF16) for _ in range(3)]
        wib = [A2.alloc([128, 8], F32) for _ in range(3)]
        diag = [A2.alloc([128, 8, 128], BF16) for _ in range(3)]
        Rb = [A2.alloc([128, 512], BF16) for _ in range(4)]
        pTb = [A2.alloc([128, 512], BF16) for _ in range(4)]
        LAG = 3
        SBANKS = [3, 4, 7, 0, 1]
        HBANKS = [0, 1, 3, 4, 7]
        oab = [A2.alloc([128, 1024], BF16) for _ in range(2)]
        bst = [A2.alloc([128, 4 * (KSTEPS + 2)], F32) for _ in range(2)]
        lnb = A2.alloc([128, 8], F32)
        rsb = [A2.alloc([128, 1], F32) for _ in range(8)]
        cstg = [A2.alloc([128, 1024], F32) for _ in range(2)]
        cout = [A2.alloc([128, 1024], BF16) for _ in range(2)]
        conv_i = [0]

        def conv_src(i):
            if i < 4:
                return w_oa[i * 128:(i + 1) * 128, :]
            if i < 8:
                return w_ob[(i - 4) * 128:(i - 3) * 128, :]
            if i < 16:
                return w_out[(i - 8) * 128:(i - 7) * 128, :]
            if i < 48:
                c, j = (i - 16) // 4, (i - 16) % 4
                return w_up[c * 128:(c + 1) * 128, j * 1024:(j + 1) * 1024]
            return w_down[(i - 48) * 128:(i - 47) * 128, :]

        def conv_steps(k):
            for _ in range(k):
                i = conv_i[0]
                if i >= 80:
                    return
                conv_i[0] += 1
                S.dma("pool", cstg[i % 2].ap, conv_src(i), writes=[cstg[i % 2].b])
                S.pool(V("tensor_copy", out=cout[i % 2].ap, in_=cstg[i % 2].ap), reads=[cstg[i % 2].b], writes=[cout[i % 2].b])
                S.dma("pool", wbf_scr[:, i * 1024:(i + 1) * 1024], cout[i % 2].ap, reads=[cout[i % 2].b], writes=[wscr_b])

        rot = {"sh": 0, "R": 0, "st": 0, "pT": 0, "kv": 0}
        K1 = KSTEPS + 2

        def load_band(src_d):
            for h in range(8):
                S.dma("sp", bstage.ap, src_d[h], writes=[bstage.b])
                S.act(V("activation", out=bandb.ap[:, h, :], in_=bstage.ap, func=AF.Copy, scale=8.0), reads=[bstage.b], writes=[bandb.b])

        def groups_of(tiles):
            gs = []
            for pos, t in enumerate(tiles):
                if gs and len(gs[-1]) < 4 and gs[-1][-1][1] + 1 == t:
                    gs[-1].append((pos, t))
                else:
                    gs.append([(pos, t)])
            return gs

        def p2_load_idx(s):
            b = s % 3
            S.dma("sp", qiTb[b].ap[0:64, 0::2, :], qi_scr[s][0:64, :, :], reads=[qscr_b[s]], writes=[qiTb[b].b])
            S.dma("sp", qiTb[b].ap[64:128, 1::2, :], qi_scr[s][64:128, :, :], reads=[qscr_b[s]], writes=[qiTb[b].b])
            S.dma("sp", wib[b].ap, wi_scr[s], reads=[qscr_b[s]], writes=[wib[b].b])
            for h in range(8):
                S.pool(V("tensor_scalar", out=diag[b].ap[:, h, :], in0=ident_b.ap, scalar1=wib[b].ap[:, h:h + 1], scalar2=None, op0=ALU.mult),
                       reads=[ident_b.b, wib[b].b], writes=[diag[b].b])

        def p2_load_q(s):
            b = s % 2
            S.dma("sp", qTb[b].ap[0:64, 0::2, :], q_scr[s][0:64, :, :], reads=[qscr_b[s]], writes=[qTb[b].b])
            S.dma("sp", qTb[b].ap[64:128, 1::2, :], q_scr[s][64:128, :, :], reads=[qscr_b[s]], writes=[qTb[b].b])

        def p2_index(s, tiles):
            b = s % 2
            b3 = s % 3
            steps = [(g, h) for g in groups_of(tiles) for h in range(8)]
            q = []

            def emit_d(pd):
                (pg, ph, pR, pn) = pd
                S.pe(V("matmul", PS[:, 2, 0:pn], lhsT=diag[b3].ap[:, ph, :], rhs=pR.ap[:, 0:pn], start=(ph == 0), stop=(ph == 7)),
                     reads=[diag[b3].b, pR.b], writes=[psb[2]])
                if ph == 7:
                    c0 = pg[0][0] * 128
                    S.act(V("activation", out=isc[b].ap[:, c0:c0 + pn], in_=PS[:, 2, 0:pn], func=AF.Copy),
                          reads=[psb[2]], writes=[isc[b].b])

            for (g, h) in steps:
                n = len(g) * 128
                t0 = g[0][1]
                bank = HBANKS[rot["sh"] % 5]
                rot["sh"] += 1
                base = 64 * (h % 2)
                S.pe(V("matmul", PS[:, bank, 0:n], lhsT=qiTb[b3].ap[:, h, :],
                       rhs=kiT.ap[:, t0 * 128:t0 * 128 + n], start=True, stop=True),
                     reads=[qiTb[b3].b] + [kiT_b[t] for (_, t) in g], writes=[psb[bank]])
                R = Rb[rot["R"] % 4]
                rot["R"] += 1
                S.act(V("activation", out=R.ap[:, 0:n], in_=PS[:, bank, 0:n], func=AF.Relu), reads=[psb[bank]], writes=[R.b])
                q.append((g, h, R, n))
                if len(q) > LAG:
                    emit_d(q.pop(0))
            while q:
                emit_d(q.pop(0))

        def p2_bisect(s, tiles, mask_list):
            b = s % 2
            N = len(tiles) * 128
            X = isc[b]
            st_ = bst[b]
            amax = st_.ap[:, 0:1]
            wk = st_.ap[:, K1:2 * K1]
            mid = st_.ap[:, 2 * K1:3 * K1]
            cnt = st_.ap[:, 3 * K1:4 * K1]
            w2 = st_.ap[:, 1:K1]
            S.dve(V("tensor_reduce", out=amax, in_=X.ap[:, 0:N], axis=AX.X, op=ALU.max, apply_absolute_value=True),
                  reads=[X.b], writes=[st_.b])
            for (pos, mcol) in mask_list:
                S.dve(V("tensor_tensor", out=X.ap[:, pos * 128:(pos + 1) * 128], in0=X.ap[:, pos * 128:(pos + 1) * 128],
                        in1=masks.ap[:, mcol * 128:(mcol + 1) * 128], op=ALU.add), reads=[X.b, masks.b], writes=[X.b])
            S.dve(V("tensor_scalar", out=wk, in0=pow2.ap[:, 0:K1], scalar1=amax, scalar2=None, op0=ALU.mult),
                  reads=[st_.b, pow2.b], writes=[st_.b])
            S.dve(V("tensor_scalar", out=w2, in0=pow2.ap[:, K1 + 1:2 * K1], scalar1=amax, scalar2=None, op0=ALU.mult),
                  reads=[st_.b, pow2.b], writes=[st_.b])
            S.dve(V("memset", mid[:, 0:1], 0.0), writes=[st_.b])
            for k in range(KSTEPS):
                S.dve(V("tensor_scalar", out=junk.ap[:, 0:N], in0=X.ap[:, 0:N], scalar1=mid[:, k:k + 1], scalar2=None,
                        op0=ALU.is_ge, op1=ALU.add, accum_out=cnt[:, k:k + 1]), reads=[X.b, st_.b], writes=[junk.b, st_.b])
                S.dve(V("scalar_tensor_tensor", out=cnt[:, k:k + 1], in0=cnt[:, k:k + 1], scalar=255.5, in1=w2[:, k:k + 1],
                        op0=ALU.is_ge, op1=ALU.mult), reads=[st_.b], writes=[st_.b])
                S.dve(V("scalar_tensor_tensor", out=mid[:, k + 1:k + 2], in0=mid[:, k:k + 1], scalar=wk[:, k + 1:k + 2],
                        in1=cnt[:, k:k + 1], op0=ALU.subtract, op1=ALU.add), reads=[st_.b], writes=[st_.b])
            thr = cnt[:, KSTEPS:KSTEPS + 1]
            S.dve(V("tensor_tensor", out=thr, in0=mid[:, KSTEPS:KSTEPS + 1], in1=wk[:, KSTEPS:KSTEPS + 1], op=ALU.subtract),
                  reads=[st_.b], writes=[st_.b])
            S.dve(V("tensor_scalar", out=mb[b].ap[:, 0:N], in0=X.ap[:, 0:N], scalar1=thr, scalar2=NEG, op0=ALU.is_lt, op1=ALU.mult),
                  reads=[X.b, st_.b], writes=[mb[b].b])

        def attend(s, tiles, pair0, vh0, bias_mode):
            b = s % 2
            qT = qTb[b]
            steps = []
            for g in groups_of(tiles):
                kv = rot["kv"] % 2
                rot["kv"] += 1
                ng = len(g)
                t0 = g[0][1]
                steps.append(("load", g, kv))
                for j, (pos, t) in enumerate(g):
                    for hg in range(2):
                        steps.append(("qk", pos, j, hg, kv))
            first = [True, True]
            nsteps_left = [sum(1 for x in steps if x[0] == "qk" and x[3] == hg) for hg in range(2)]
            pendq = []

            def emit_pv(pd):
                (pos, j, hg, kv, pt) = pd
                nsteps_left[hg] -= 1
                for hh in range(4):
                    h = 4 * hg + hh
                    S.pe(V("matmul", PS[:, 5 + hg, hh * 65:(hh + 1) * 65], lhsT=pt.ap[:, hh * 128:(hh + 1) * 128],
                           rhs=vbuf[kv].ap[:, j, h, :], start=(first[hg] and hh == 0), stop=(nsteps_left[hg] == 0),
                           skip_group_check=True), reads=[pt.b, vbuf[kv].b], writes=[psb[5 + hg]])
                first[hg] = False

            for stp in steps:
                if stp[0] == "load":
                    (_, g, kv) = stp
                    ng = len(g)
                    t0 = g[0][1]
                    S.dma("sp", kbuf[kv].ap[:, 0:ng], kT_scr[t0:t0 + ng].rearrange("t p c k -> p t c k")[:, :, pair0:pair0 + 4, :],
                          reads=[kscr_b[t] for (_, t) in g], writes=[kbuf[kv].b])
                    S.dma("sp", vbuf[kv].ap[:, 0:ng], v_scr[t0:t0 + ng].rearrange("t p h d -> p t h d")[:, :, vh0:vh0 + 8, :],
                          reads=[vscr_b[t] for (_, t) in g], writes=[vbuf[kv].b])
                    continue
                (_, pos, j, hg, kv) = stp
                sbank = SBANKS[rot["st"] % 5]
                rot["st"] += 1
                if bias_mode is None:
                    S.pe(V("matmul", PS[:, sbank, :], lhsT=mb[b].ap[:, pos * 128:(pos + 1) * 128], rhs=I4.ap, start=True, stop=False,
                           skip_group_check=True), reads=[mb[b].b, I4.b], writes=[psb[sbank]])
                for hh in range(4):
                    h = 4 * hg + hh
                    pr = h // 2
                    base = 64 * (h % 2)
                    if bias_mode is not None:
                        m = bias_mode[pos]
                        S.pe(V("matmul", PS[:, sbank, hh * 128:(hh + 1) * 128], lhsT=bandb.ap[:, h, m * 128:(m + 1) * 128], rhs=ident_b.ap,
                               start=True, stop=False, skip_group_check=True), reads=[bandb.b, ident_b.b], writes=[psb[sbank]])
                    S.pe(V("matmul", PS[:, sbank, hh * 128:(hh + 1) * 128], lhsT=kbuf[kv].ap[:, j, pr, :],
                           rhs=qT.ap[:, 2 * pair0 + h, :], start=False, stop=True, skip_group_check=True),
                         reads=[kbuf[kv].b, qT.b], writes=[psb[sbank]])
                pt = pTb[rot["pT"] % 4]
                rot["pT"] += 1
                S.act(V("activation", out=pt.ap, in_=PS[:, sbank, :], func=AF.Exp, scale=0.125), reads=[psb[sbank]], writes=[pt.b])
                pendq.append((pos, j, hg, kv, pt))
                if len(pendq) > LAG:
                    emit_pv(pendq.pop(0))
            while pendq:
                emit_pv(pendq.pop(0))
            col0 = 0 if bias_mode is None else 512
            for hg in range(2):
                ov = PS[:, 5 + hg, 0:260].rearrange("p (h d) -> p h d", d=65)
                S.act(V("activation", out=lnb.ap[:, hg * 4:(hg + 1) * 4], in_=ov[:, :, 64], func=AF.Ln), reads=[psb[5 + hg]], writes=[lnb.b])
                for hh in range(4):
                    h = 4 * hg + hh
                    S.act(V("activation", out=rsb[h].ap, in_=lnb.ap[:, h:h + 1], func=AF.Exp, scale=-1.0), reads=[lnb.b], writes=[rsb[h].b])
                    S.act(V("activation", out=oab[b].ap[:, col0 + h * 64:col0 + (h + 1) * 64], in_=ov[:, hh, 0:64], func=AF.Copy,
                            scale=rsb[h].ap), reads=[psb[5 + hg], rsb[h].b], writes=[oab[b].b])

        slots = []
        for i in range(32):
            tiles = list(range(2 * i + 2))
            masks_l = [(2 * i, 0), (2 * i + 1, 1)]
            band = [(2 * i - 4 + m, m) for m in range(6) if 2 * i - 4 + m >= 0]
            slots.append((tiles, masks_l, band))
        slots.append((list(range(65, 73)) + [64], [(8, 0)], [(69 + m, m) for m in range(4)] + [(64, 4)]))

        NS2 = min(NSLOT, int(os.environ.get("KNS", NSLOT))) if "2" in PH else 0
        for i in range(2):
            S.pool(V("memset", qTb[i].ap, 0.0), writes=[qTb[i].b])
        for i in range(3):
            S.pool(V("memset", qiTb[i].ap, 0.0), writes=[qiTb[i].b])

        def mark(n):
            if os.environ.get("KDBG"):
                print("MARK", n, S.uid, flush=True)
        if NS2:
          mark("p2start")
          load_band(bandp_d)
          mark("band")
          p2_load_idx(0)
          mark("loadidx0")
          p2_index(0, slots[0][0])
          mark("index0")
          p2_bisect(0, slots[0][0], slots[0][1])
          mark("bisect0")
          p2_load_idx(1)
          p2_index(1, slots[1][0])
          if NS2 > 2:
              p2_load_idx(2)
          p2_load_q(0)
          mark("pre-loop")
        for s in range(NS2):
            if s + 3 < NS2:
                p2_load_idx(s + 3)
            if s + 2 < NS2:
                p2_index(s + 2, slots[s + 2][0])
            attend(s, slots[s][0], 0, 0, None)
            mark("dsa%d" % s)
            if s + 1 < NS2:
                p2_load_q(s + 1)
                p2_bisect(s + 1, slots[s + 1][0], slots[s + 1][1])
            if s == NSLOT - 1:
                load_band(bands_d)
            band = slots[s][2]
            mark("bis%d" % (s + 1))
            attend(s, [t for (t, m) in band], 4, 8, [m for (t, m) in band])
            mark("band%d" % s)
            if "3" in PH:
                conv_steps(3)
            S.dma("pool", oab_scr[s], oab[s % 2].ap, reads=[oab[s % 2].b], writes=[oscr_b[s]])

        S.barrier()
        A3 = Arena(arena_t, PBASE - NTS * 256, ARENA_BYTES)
        wall = A3.alloc([128, 80 * 1024], BF16)

        def wview(c0, nchunk, ncol):
            t_ = T(wall.ap[:, c0 * 1024:c0 * 1024 + nchunk * ncol].rearrange("p (c n) -> p c n", n=ncol))
            t_.b = wall.b
            return t_
        woa = wview(0, 4, D)
        wob = wview(4, 4, D)
        wo = wview(8, 8, D)
        wu = wview(16, 8, 4096)
        wd = wview(48, 32, D)
        o_in = A3.alloc([128, 1024], BF16)
        oT = A3.alloc([128, 8, 128], BF16)
        gt = A3.alloc([128, 2048], BF16)
        xin = A3.alloc([128, D], F32)
        t1 = A3.alloc([128, D], F32)
        t2 = A3.alloc([128, D], F32)
        x2s = [A3.alloc([128, D], F32) for _ in range(2)]
        x2Ts = [A3.alloc([128, 8, 128], BF16) for _ in range(2)]
        hT = A3.alloc([128, 32, 128], BF16)
        rl = A3.alloc([128, 512], F32)
        ss3 = A3.alloc([128, 1], F32)
        r3s = [A3.alloc([128, 1], F32) for _ in range(2)]

        if "3" in PH:
            conv_steps(80)
            for k in range(10):
                S.dma("sp", wall.ap[:, k * 8192:(k + 1) * 8192], wbf_scr[:, k * 8192:(k + 1) * 8192], reads=[wscr_b], writes=[wall.b])

        def tr_bf_p3(src, dst):
            pv = PSb16(0)
            for c in range(8):
                S.pe(V("transpose", out=pv[:, c * 128:(c + 1) * 128], in_=src.ap[:, c * 128:(c + 1) * 128], identity=ident_b.ap),
                     reads=[src.b, ident_b.b], writes=[psb[0]])
            S.dve(V("tensor_copy", out=dst.ap, in_=pv.rearrange("p (c k) -> p c k", k=128)), reads=[psb[0]], writes=[dst.b])

        def p3_stage_a(s):
            x2 = x2s[s % 2]
            x2T = x2Ts[s % 2]
            r3 = r3s[s % 2]
            S.dma("sp", o_in.ap, oab_scr[s], reads=[oscr_b[s]], writes=[o_in.b])
            S.dma("sp", gt.ap, gate_scr[s], reads=[qscr_b[s]], writes=[gt.b])
            S.dma("sp", xin.ap, x_own[s * 128:(s + 1) * 128, :], writes=[xin.b])
            tr_bf_p3(o_in, oT)
            for half in range(2):
                for c in range(4):
                    S.pe(V("matmul", PS[:, 2 + half, :], lhsT=oT.ap[:, c, :], rhs=woa.ap[:, c, half * 512:(half + 1) * 512],
                           start=(c == 0), stop=(c == 3)), reads=[oT.b, woa.b], writes=[psb[2 + half]])
                S.dve(V("tensor_tensor", out=t1.ap[:, half * 512:(half + 1) * 512], in0=PS[:, 2 + half, :],
                        in1=gt.ap[:, half * 512:(half + 1) * 512], op=ALU.mult), reads=[psb[2 + half], gt.b], writes=[t1.b])
            for half in range(2):
                for c in range(4):
                    S.pe(V("matmul", PS[:, 2 + half, :], lhsT=oT.ap[:, 4 + c, :], rhs=wob.ap[:, c, half * 512:(half + 1) * 512],
                           start=(c == 0), stop=(c == 3)), reads=[oT.b, wob.b], writes=[psb[2 + half]])
                S.dve(V("tensor_tensor", out=t2.ap[:, half * 512:(half + 1) * 512], in0=PS[:, 2 + half, :],
                        in1=gt.ap[:, 1024 + half * 512:1024 + (half + 1) * 512], op=ALU.mult), reads=[psb[2 + half], gt.b], writes=[t2.b])
            S.pool(V("tensor_tensor", out=o_in.ap, in0=t1.ap, in1=t2.ap, op=ALU.add), reads=[t1.b, t2.b], writes=[o_in.b])
            tr_bf_p3(o_in, oT)
            for half in range(2):
                for c in range(8):
                    S.pe(V("matmul", PS[:, 2 + half, :], lhsT=oT.ap[:, c, :], rhs=wo.ap[:, c, half * 512:(half + 1) * 512],
                           start=(c == 0), stop=(c == 7)), reads=[oT.b, wo.b], writes=[psb[2 + half]])
                S.dve(V("tensor_tensor", out=x2.ap[:, half * 512:(half + 1) * 512], in0=PS[:, 2 + half, :],
                        in1=xin.ap[:, half * 512:(half + 1) * 512], op=ALU.add), reads=[psb[2 + half], xin.b], writes=[x2.b])
            S.act(V("activation", out=t1.ap, in_=x2.ap, func=AF.Square, accum_out=ss3.ap), reads=[x2.b], writes=[t1.b, ss3.b])
            S.dve(V("tensor_scalar", out=r3.ap, in0=ss3.ap, scalar1=1.0 / D, scalar2=EPS, op0=ALU.mult, op1=ALU.add),
                  reads=[ss3.b], writes=[r3.b])
            S.dve(V("reciprocal", out=r3.ap, in_=r3.ap), reads=[r3.b], writes=[r3.b])
            for c in range(8):
                bank = c // 4
                S.pe(V("transpose", out=PS[:, bank, (c % 4) * 128:(c % 4 + 1) * 128], in_=x2.ap[:, c * 128:(c + 1) * 128],
                       identity=ident_f.ap), reads=[x2.b, ident_f.b], writes=[psb[bank]])
            for c in range(8):
                bank = c // 4
                src = PS[:, bank, (c % 4) * 128:(c % 4 + 1) * 128]
                S.dve(V("tensor_scalar", out=x2T.ap[:, c, :], in0=src, scalar1=gffn.ap[:, c:c + 1], scalar2=None, op0=ALU.mult),
                      reads=[psb[bank], gffn.b], writes=[x2T.b])

        def p3_stage_b(s, hook):
            x2 = x2s[s % 2]
            x2T = x2Ts[s % 2]
            r3 = r3s[s % 2]
            for f4 in range(8):
                bank = 4 + (f4 % 2)
                for ff in range(4):
                    f = f4 * 4 + ff
                    for c in range(8):
                        S.pe(V("matmul", PS[:, bank, ff * 128:(ff + 1) * 128], lhsT=wu.ap[:, c, f * 128:(f + 1) * 128], rhs=x2T.ap[:, c, :],
                               start=(c == 0), stop=(c == 7), skip_group_check=True), reads=[wu.b, x2T.b], writes=[psb[bank]])
                S.act(V("activation", out=rl.ap, in_=PS[:, bank, :], func=AF.Relu), reads=[psb[bank]], writes=[rl.b])
                S.dve(V("tensor_tensor", out=hT.ap[:, f4 * 4:(f4 + 1) * 4, :].rearrange("p a b -> p (a b)"), in0=rl.ap,
                        in1=rl.ap, op=ALU.mult), reads=[rl.b], writes=[hT.b])
                hook()
            for half in range(2):
                for f in range(32):
                    S.pe(V("matmul", PS[:, 6 + half, :], lhsT=hT.ap[:, f, :], rhs=wd.ap[:, f, half * 512:(half + 1) * 512],
                           start=(f == 0), stop=(f == 31)), reads=[hT.b, wd.b], writes=[psb[6 + half]])
                    if f % 8 == 7:
                        hook()
                S.dve(V("scalar_tensor_tensor", out=x2.ap[:, half * 512:(half + 1) * 512], in0=PS[:, 6 + half, :], scalar=r3.ap,
                        in1=x2.ap[:, half * 512:(half + 1) * 512], op0=ALU.mult, op1=ALU.add),
                      reads=[psb[6 + half], r3.b, x2.b], writes=[x2.b])
            S.dma("sp", y_o[s * 128:(s + 1) * 128, :], x2.ap, reads=[x2.b])

        NS3 = min(NSLOT, int(os.environ.get("KNS3", NSLOT))) if "3" in PH else 0
        if NS3:
            p3_stage_a(0)
        for s in range(NS3):
            ops = []
            if s + 1 < NS3:
                S.begin_defer()
                p3_stage_a(s + 1)
                ops = S.end_defer()
            per = (len(ops) + 14) // 15
            p3_stage_b(s, lambda: S.replay(ops, per))
            S.replay(ops, len(ops))

        S.emit(nc, st)
    return nc


def _rope_tables(pos):
    half = 8
    inv = (np.float32(500000.0) ** (-np.arange(half, dtype=np.float32) / np.float32(half))).astype(np.float32)
    ang = pos.astype(np.float32)[:, None] * inv[None, :]
    return np.cos(ang).astype(np.float32), np.sin(ang).astype(np.float32)


def _band_tiles(table, true_tile_of_m, visible_fn):
    out = np.full((8, 128, 768), NEG, np.float32)
    qi = np.arange(128)[:, None]
    kj = np.arange(128)[None, :]
    for m in range(6):
        off = true_tile_of_m[m]
        if off is None:
            continue
        dist = off * 128 + qi - kj
        idx = np.clip(dist, -128, 128) + 128
        vis = visible_fn(off, qi, kj)
        for h in range(8):
            g = table[idx, h]
            out[h, :, m * 128:(m + 1) * 128] = np.where(vis, g, np.float32(NEG))
    return out


def _vis_prompt(off, qi, kj):
    dc = 2 * off + qi // 64 - kj // 64
    return (dc >= 0) & (dc <= 8)


_CACHE = {}


def kernel(x_prompt, x_sample, cache_k_a, cache_v_a, cache_k_idx, cache_k_b, cache_v_b,
           norm_mix, w_in, qnorm_a, knorm_a, knorm_idx, qnorm_b, knorm_b, rel_bias_b,
           w_o_a, w_o_b, w_out, norm_ffn, w_up, w_down):
    f = lambda a: np.ascontiguousarray(np.asarray(a, dtype=np.float32))
    x_prompt, x_sample = f(x_prompt), f(x_sample)
    w_in_ = f(w_in)[0]
    sp = np.cumsum([0, 512, 512, 512, 512, 64, 8, 512, 512, 512, 1024, 1024])
    seg = {n: (sp[i], sp[i + 1]) for i, n in enumerate(["qa", "ka", "va", "qi", "ki", "wi", "qb", "kb", "vb", "ga", "gb"])}
    order = ["ka", "kb", "va", "vb", "ki", "qa", "qb", "qi", "ga", "gb", "wi"]
    w_in_r = np.ascontiguousarray(np.concatenate([w_in_[:, seg[n][0]:seg[n][1]] for n in order], axis=1))
    rep = lambda g, n: np.tile(f(g)[0], n)
    g_kab = np.ascontiguousarray(np.broadcast_to(np.concatenate([rep(knorm_a, 8), rep(knorm_b, 8)])[None], (128, 1024)))
    g_qab = np.ascontiguousarray(np.broadcast_to(np.concatenate([rep(qnorm_a, 8), rep(qnorm_b, 8)])[None], (128, 1024)))
    g_ki = np.ascontiguousarray(np.broadcast_to(f(knorm_idx)[0][None], (128, 64)))
    gmix = np.ascontiguousarray(f(norm_mix)[0].reshape(8, 128).T)
    gffn = np.ascontiguousarray(f(norm_ffn)[0].reshape(8, 128).T)
    K1 = KSTEPS + 2
    p2 = np.concatenate([2.0 ** (-np.arange(K1)), 2.0 ** (1.0 - np.arange(K1))]).astype(np.float32)
    pow2 = np.ascontiguousarray(np.broadcast_to(p2[None], (128, 2 * K1)))
    table = f(rel_bias_b)[0]
    qi = np.arange(128)[:, None]
    kj = np.arange(128)[None, :]
    own_mask = np.where((kj // 64) <= (qi // 64), 0.0, -1e30).astype(np.float32)
    band_s = _band_tiles(table, [4, 3, 2, 1, 0, None], _vis_prompt)

    in_maps = []
    for c in range(8):
        b, hf = c // 2, c % 2
        xb = x_prompt[b].reshape(64, 128, D)
        order_t = []
        for i in range(32):
            order_t += [2 * i + hf, 2 * i + 1 - hf]
        xs_pad = np.zeros((128, D), np.float32)
        xs_pad[:64] = x_sample[c]
        x_all = np.concatenate([xb[order_t].reshape(64 * 128, D), xs_pad], axis=0)
        pos = np.concatenate([(np.array(order_t)[:, None] * 128 + np.arange(128)[None]).reshape(-1),
                              1024 + np.arange(64), np.zeros(64, np.int64)])
        cos_all, sin_all = _rope_tables(pos)
        x_own = np.concatenate([xb[hf::2].reshape(32 * 128, D), xs_pad], axis=0)
        foreign = np.full((128, 128), 0.0 if hf == 1 else -1e30, np.float32)
        masks = np.ascontiguousarray(np.concatenate([own_mask, foreign], axis=1))
        if hf == 0:
            offs = [4, 3, 2, 1, 0, None]
        else:
            offs = [None, 4, 1, 2, 0 - 0, 0]
            offs = [4, None, 2, 3, 0, 1]
        band_p = _band_tiles(table, offs, _vis_prompt)
        in_maps.append(dict(
            x_all=np.ascontiguousarray(x_all), cos_all=cos_all, sin_all=sin_all, x_own=np.ascontiguousarray(x_own),
            ck_a=f(cache_k_a)[0, c].reshape(1024, 512), cv_a=f(cache_v_a)[0, c].reshape(1024, 512),
            ck_i=f(cache_k_idx)[0, c], ck_b=f(cache_k_b)[0, c].reshape(512, 512), cv_b=f(cache_v_b)[0, c].reshape(512, 512),
            w_in=w_in_r, w_oa=f(w_o_a)[0], w_ob=f(w_o_b)[0], w_out=f(w_out)[0], w_up=f(w_up)[0], w_down=f(w_down)[0],
            gmix=gmix, gffn=gffn, g_kab=g_kab, g_qab=g_qab, g_ki=g_ki, ident=np.eye(128, dtype=np.float32),
            pow2=pow2, masks=masks, band_p=band_p, band_s=band_s))

    if "nc" not in _CACHE:
        _CACHE["nc"] = build_program()
    import os
    NCORE = int(os.environ.get("KCORES", "8"))
    res = run_bass_kernel_spmd(_CACHE["nc"], in_maps[:NCORE], core_ids=list(range(NCORE)))
    R = list(res.results) + [res.results[0]] * (8 - NCORE)

    y_p = np.zeros((4, 64, 128, D), np.float32)
    ka_p = np.zeros((4, 64, 128, 512), np.float32)
    va_p = np.zeros((4, 64, 128, 512), np.float32)
    ki_p = np.zeros((4, 64, 128, 64), np.float32)
    kb_p = np.zeros((4, 4, 128, 512), np.float32)
    vb_p = np.zeros((4, 4, 128, 512), np.float32)
    y_s = np.zeros((8, 64, D), np.float32)
    ka_s = np.zeros((8, 64, 512), np.float32)
    va_s = np.zeros((8, 64, 512), np.float32)
    ki_s = np.zeros((8, 64, 64), np.float32)
    kb_s = np.zeros((8, 64, 512), np.float32)
    vb_s = np.zeros((8, 64, 512), np.float32)
    for c in range(8):
        b, hf = c // 2, c % 2
        r = R[c]
        y_p[b, hf::2] = r["y_o"][:4096].reshape(32, 128, D)
        ka_p[b, hf::2] = r["ka_o"][:4096].reshape(32, 128, 512)
        va_p[b, hf::2] = r["va_o"][:4096].reshape(32, 128, 512)
        ki_p[b, hf::2] = r["ki_o"][:4096].reshape(32, 128, 64)
        kb_p[b, hf::2] = r["kb_o"][:256].reshape(2, 128, 512)
        vb_p[b, hf::2] = r["vb_o"][:256].reshape(2, 128, 512)
        y_s[c] = r["y_o"][4096:4160]
        ka_s[c] = r["ka_o"][4096:4160]
        va_s[c] = r["va_o"][4096:4160]
        ki_s[c] = r["ki_o"][4096:4160]
        kb_s[c] = r["kb_o"][256:320]
        vb_s[c] = r["vb_o"][256:320]
    return (y_p.reshape(4, 8192, D), y_s,
            ka_p.reshape(1, 4, 8192, 8, 64), va_p.reshape(1, 4, 8192, 8, 64), ki_p.reshape(1, 4, 8192, 64),
            kb_p.reshape(1, 4, 512, 8, 64), vb_p.reshape(1, 4, 512, 8, 64),
            ka_s.reshape(1, 8, 64, 8, 64), va_s.reshape(1, 8, 64, 8, 64), ki_s.reshape(1, 8, 64, 64),
            kb_s.reshape(1, 8, 64, 8, 64), vb_s.reshape(1, 8, 64, 8, 64))
```

```python
import numpy as np
from contextlib import ExitStack
import concourse.bass as bass
import concourse.mybir as mybir
from concourse.bass_utils import run_bass_kernel_spmd

F32 = mybir.dt.float32
BF16 = mybir.dt.bfloat16
U8 = mybir.dt.uint8
ALU = mybir.AluOpType
AF = mybir.ActivationFunctionType
AX = mybir.AxisListType

D = 1024
NT = 65
NTS = 73
NSLOT = 33
DIN = 5704
KSTEPS = 16
NEG = -30000.0
EPS = 1e-6


class Buf:
    __slots__ = ("w", "r")

    def __init__(self):
        self.w = None
        self.r = {}


class Instr:
    __slots__ = ("eng", "fn", "deps", "signal", "sem", "val", "is_dma", "uid")


ENGS = ("pe", "act", "dve", "pool", "sp")
NRING = 8


class Sched:
    def __init__(self):
        self.q = {e: [] for e in ENGS}
        self.uid = 0
        self.bar = []
        self.bar_done = set()
        self.deferred = None

    def begin_defer(self):
        self.deferred = []

    def end_defer(self):
        ops = self.deferred
        self.deferred = None
        return ops

    def replay(self, ops, k):
        for _ in range(min(k, len(ops))):
            self.add(*ops.pop(0))

    def barrier(self):
        lst = []
        for e in ENGS:
            comp = [i for i in self.q[e] if not i.is_dma]
            if comp:
                lst.append(comp[-1])
            lst += [i for i in self.q[e] if i.is_dma][-NRING:]
        self.bar = lst
        self.bar_done = set()

    def add(self, eng, fn, reads=(), writes=(), dma=False):
        import os
        if self.deferred is not None:
            self.deferred.append((eng, fn, tuple(reads), tuple(writes), dma))
            return None
        if self.uid >= int(os.environ.get("KMAX", "100000000")):
            return None
        ins = Instr()
        ins.eng = eng
        ins.fn = fn
        ins.is_dma = dma
        ins.signal = dma
        ins.uid = self.uid
        self.uid += 1
        deps = {}

        def need(d, raw):
            if d is None or d is ins:
                return
            if (not d.is_dma) and (not dma) and d.eng == eng:
                if eng == "pe":
                    return
            deps[d.uid] = d

        for b in reads:
            need(b.w, True)
        for b in writes:
            need(b.w, False)
            for rd in b.r.values():
                need(rd, False)
        if self.bar and eng not in self.bar_done:
            self.bar_done.add(eng)
            for d in self.bar:
                deps[d.uid] = d
        for b in reads:
            b.r[("dma", ins.uid) if dma else eng] = ins
        for b in writes:
            b.w = ins
            b.r = {}
        ins.deps = list(deps.values())
        for d in ins.deps:
            d.signal = True
        self.q[eng].append(ins)
        return ins

    def pe(self, fn, reads=(), writes=()):
        return self.add("pe", fn, reads, writes)

    def act(self, fn, reads=(), writes=()):
        return self.add("act", fn, reads, writes)

    def dve(self, fn, reads=(), writes=()):
        return self.add("dve", fn, reads, writes)

    def pool(self, fn, reads=(), writes=()):
        return self.add("pool", fn, reads, writes)

    def dma(self, queue, out, in_, reads=(), writes=(), **kw):
        return self.add(queue, lambda e: e.dma_start(out=out, in_=in_, **kw), reads, writes, dma=True)

    def emit(self, nc, stack):
        esem = {e: stack.enter_context(nc.semaphore("s_" + e)) for e in ENGS}
        rings = {e: [stack.enter_context(nc.semaphore("r_%s%d" % (e, i))) for i in range(NRING)]
                 for e in ("sp", "pool", "act")}
        final_ring = {}
        for e in ENGS:
            cnt = 0
            nd = 0
            for ins in self.q[e]:
                if ins.is_dma:
                    ins.sem = rings[e][nd % NRING]
                    ins.val = 16 * (nd // NRING + 1)
                    final_ring[(e, nd % NRING)] = (ins.sem, ins.val)
                    nd += 1
                elif ins.signal:
                    cnt += 1
                    ins.sem = esem[e]
                    ins.val = cnt
        block = stack.enter_context(nc.Block())
        handles = {"pe": "tensor", "act": "scalar", "dve": "vector", "pool": "gpsimd", "sp": "sync"}

        def make(e):
            def body(h):
                waited = {}

                def wait(sem, val):
                    k = id(sem)
                    if waited.get(k, 0) >= val:
                        return
                    waited[k] = val
                    h.wait_ge(sem, val)

                nd = 0
                for ins in self.q[e]:
                    for d in ins.deps:
                        wait(d.sem, d.val)
                    if ins.is_dma:
                        if nd >= NRING:
                            wait(ins.sem, ins.val - 16)
                        nd += 1
                        ins.fn(h).then_inc(ins.sem, 16)
                    else:
                        r = ins.fn(h)
                        if ins.signal:
                            r.then_inc(ins.sem, 1)
                if e == "sp":
                    for (sem, val) in final_ring.values():
                        wait(sem, val)
            return body

        for e in ENGS:
            getattr(block, handles[e])(make(e))


def V(name, *a, **k):
    return lambda e: getattr(e, name)(*a, **k)


class T:
    __slots__ = ("ap", "b")

    def __init__(self, ap):
        self.ap = ap
        self.b = Buf()


_DSZ = {F32: 4, BF16: 2, U8: 1}


class Arena:
    def __init__(self, ap, base, limit):
        self.ap = ap
        self.off = base
        self.limit = limit

    def alloc(self, shape, dt):
        n = 1
        for s in shape[1:]:
            n *= s
        nb = (n * _DSZ[dt] + 31) // 32 * 32
        assert self.off + nb <= self.limit, ("arena overflow", self.off + nb, self.limit)
        v = self.ap[:, self.off // 2:(self.off + nb) // 2]
        self.off += nb
        if dt != BF16:
            v = v.bitcast(dt)
        v = v[:, 0:n]
        if len(shape) == 3:
            v = v.rearrange("p (a b) -> p a b", b=shape[2])
        elif len(shape) == 4:
            v = v.rearrange("p (a b c) -> p a b c", b=shape[2], c=shape[3])
        if shape[0] != 128:
            v = v[0:shape[0]]
        return T(v)


def build_program():
    nc = bass.Bass("TRN2", target_bir_lowering=False)

    def din(name, shape, dt=F32):
        return nc.dram_tensor(name, list(shape), dt, kind="ExternalInput").ap()

    def dout(name, shape, dt=F32):
        return nc.dram_tensor(name, list(shape), dt, kind="ExternalOutput").ap()

    def dscr(name, shape, dt):
        return nc.dram_tensor(name, list(shape), dt, kind="Internal").ap()

    x_all = din("x_all", [NT * 128, D])
    cos_all = din("cos_all", [NT * 128, 8])
    sin_all = din("sin_all", [NT * 128, 8])
    x_own = din("x_own", [NSLOT * 128, D])
    ck_a = din("ck_a", [1024, 512])
    cv_a = din("cv_a", [1024, 512])
    ck_i = din("ck_i", [1024, 64])
    ck_b = din("ck_b", [512, 512])
    cv_b = din("cv_b", [512, 512])
    w_in = din("w_in", [D, DIN])
    w_oa = din("w_oa", [512, D])
    w_ob = din("w_ob", [512, D])
    w_out = din("w_out", [D, D])
    w_up = din("w_up", [D, 4096])
    w_down = din("w_down", [4096, D])
    gmix_d = din("gmix", [128, 8])
    gffn_d = din("gffn", [128, 8])
    g_kab_d = din("g_kab", [128, 1024])
    g_qab_d = din("g_qab", [128, 1024])
    g_ki_d = din("g_ki", [128, 64])
    ident_d = din("ident", [128, 128])
    pow2_d = din("pow2", [128, 2 * (KSTEPS + 2)])
    mask_d = din("masks", [128, 256])
    bandp_d = din("band_p", [8, 128, 768])
    bands_d = din("band_s", [8, 128, 768])

    y_o = dout("y_o", [NSLOT * 128, D])
    ka_o = dout("ka_o", [NSLOT * 128, 512])
    va_o = dout("va_o", [NSLOT * 128, 512])
    ki_o = dout("ki_o", [NSLOT * 128, 64])
    kb_o = dout("kb_o", [3 * 128, 512])
    vb_o = dout("vb_o", [3 * 128, 512])

    kT_scr = dscr("kT_scr", [NTS, 128, 8, 128], BF16)
    v_scr = dscr("v_scr", [NTS, 128, 16, 65], BF16)
    q_scr = dscr("q_scr", [NSLOT, 128, 8, 128], BF16)
    qi_scr = dscr("qi_scr", [NSLOT, 128, 4, 128], BF16)
    gate_scr = dscr("gate_scr", [NSLOT, 128, 2048], BF16)
    wi_scr = dscr("wi_scr", [NSLOT, 128, 8], F32)
    oab_scr = dscr("oab_scr", [NSLOT, 128, 1024], BF16)
    wbf_scr = dscr("wbf_scr", [128, 80 * 1024], BF16)
    wscr_b = Buf()
    kscr_b = [Buf() for _ in range(NTS)]
    vscr_b = [Buf() for _ in range(NTS)]
    qscr_b = [Buf() for _ in range(NSLOT)]
    oscr_b = [Buf() for _ in range(NSLOT)]

    S = Sched()
    with ExitStack() as st:
        ARENA_BYTES = 206 * 1024
        arena_t = st.enter_context(nc.sbuf_tensor("arena", [128, ARENA_BYTES // 2], BF16))
        PS = st.enter_context(nc.psum_tensor("ps", [128, 8, 512], F32))
        psb = [Buf() for _ in range(8)]

        def PSb16(bank):
            return PS[:, bank, :].bitcast(BF16)

        A0 = Arena(arena_t, 0, ARENA_BYTES)
        ident_f = A0.alloc([128, 128], F32)
        ident_b = A0.alloc([128, 128], BF16)
        I4 = A0.alloc([128, 512], BF16)
        gmix = A0.alloc([128, 8], F32)
        gffn = A0.alloc([128, 8], F32)
        pow2 = A0.alloc([128, 2 * (KSTEPS + 2)], F32)
        masks = A0.alloc([128, 256], F32)
        kiT = A0.alloc([128, NTS * 128], BF16)
        kiT_b = [Buf() for _ in range(NTS)]
        PBASE = A0.off

        S.dma("sp", ident_f.ap, ident_d, writes=[ident_f.b])
        S.dma("sp", gmix.ap, gmix_d, writes=[gmix.b])
        S.dma("sp", gffn.ap, gffn_d, writes=[gffn.b])
        S.dma("sp", pow2.ap, pow2_d, writes=[pow2.b])
        S.dma("sp", masks.ap, mask_d, writes=[masks.b])
        S.dve(V("tensor_copy", out=ident_b.ap, in_=ident_f.ap), reads=[ident_f.b], writes=[ident_b.b])
        for j in range(4):
            S.dve(V("tensor_copy", out=I4.ap[:, j * 128:(j + 1) * 128], in_=ident_f.ap), reads=[ident_f.b], writes=[I4.b])

        A1 = Arena(arena_t, PBASE, ARENA_BYTES)
        w1 = A1.alloc([128, 8, DIN], BF16)
        g_kab = A1.alloc([128, 1024], F32)
        g_qab = A1.alloc([128, 1024], F32)
        g_ki = A1.alloc([128, 64], F32)
        xt = [A1.alloc([128, D], F32) for _ in range(2)]
        xT = [A1.alloc([128, 8, 128], BF16) for _ in range(2)]
        xT_b = [[Buf() for _ in range(8)] for _ in range(2)]
        cs = [A1.alloc([128, 8], F32) for _ in range(4)]
        sn = [A1.alloc([128, 8], F32) for _ in range(4)]
        ssx = [A1.alloc([128, 1], F32) for _ in range(2)]
        rstd = [A1.alloc([128, 1], F32) for _ in range(2)]
        z_k = [A1.alloc([128, 1024], F32) for _ in range(2)]
        z_v = [A1.alloc([128, 1024], F32) for _ in range(2)]
        z_q = [A1.alloc([128, 1024], F32) for _ in range(2)]
        z_qi = [A1.alloc([128, 512], F32) for _ in range(2)]
        z_ki = [A1.alloc([128, 64], F32) for _ in range(2)]
        wi_sb = [A1.alloc([128, 8], F32) for _ in range(2)]
        sq_tmps = [A1.alloc([128, 1024], F32) for _ in range(2)]
        hs_k = [A1.alloc([128, 16], F32) for _ in range(2)]
        hs_q = [A1.alloc([128, 16], F32) for _ in range(2)]
        hs_i = [A1.alloc([128, 1], F32) for _ in range(2)]
        knbs = [A1.alloc([128, 1024], BF16) for _ in range(2)]
        qnb = A1.alloc([128, 1024], BF16)
        qib = A1.alloc([128, 512], BF16)
        kibs = [A1.alloc([128, 128], BF16) for _ in range(2)]
        kT_sbs = [A1.alloc([128, 8, 128], BF16) for _ in range(2)]
        qT_sb = A1.alloc([128, 8, 128], BF16)
        qiT_sb = A1.alloc([128, 4, 128], BF16)
        vaug = [A1.alloc([128, 16, 65], BF16) for _ in range(2)]
        gates = A1.alloc([128, 2048], BF16)
        rt_default = [A1.alloc([128, 16, 8], F32) for _ in range(4)]

        w1_b = [Buf() for _ in range(3)]
        wstg = [A1.alloc([128, 1024], F32) for _ in range(2)]
        rtq = []
        for i_ in range(4):
            t_ = T(wstg[0].ap[:, i_ * 128:(i_ + 1) * 128].rearrange("p (h e) -> p h e", e=8))
            t_.b = wstg[0].b
            rtq.append(t_)
        wj = 0
        for c0 in range(0, DIN, 1024):
            c1 = min(DIN, c0 + 1024)
            n = c1 - c0
            for c in range(8):
                stg = wstg[wj % 2]
                S.dma("sp", stg.ap[:, 0:n], w_in[c * 128:(c + 1) * 128, c0:c1], writes=[stg.b])
                if wj % 2 == 0:
                    S.dve(V("tensor_copy", out=w1.ap[:, c, c0:c1], in_=stg.ap[:, 0:n]), reads=[stg.b], writes=[w1_b[c0 // 2048]])
                else:
                    S.act(V("activation", out=w1.ap[:, c, c0:c1], in_=stg.ap[:, 0:n], func=AF.Copy), reads=[stg.b], writes=[w1_b[c0 // 2048]])
                wj += 1
        S.dma("sp", g_kab.ap, g_kab_d, writes=[g_kab.b])
        S.dma("sp", g_qab.ap, g_qab_d, writes=[g_qab.b])
        S.dma("sp", g_ki.ap, g_ki_d, writes=[g_ki.b])
        for i in range(2):
            S.pool(V("memset", vaug[i].ap[:, :, 64:65], 1.0), writes=[vaug[i].b])

        zrot = [0]
        trot = [0]

        def headnorm(z, nh, gains, hs, sq_tmp):
            n = nh * 64
            S.act(V("activation", out=sq_tmp.ap[:, 0:n], in_=z.ap[:, 0:n], func=AF.Square), reads=[z.b], writes=[sq_tmp.b])
            S.dve(V("tensor_reduce", out=hs.ap[:, 0:nh], in_=sq_tmp.ap[:, 0:n].rearrange("p (h d) -> p h d", d=64),
                    axis=AX.X, op=ALU.add), reads=[sq_tmp.b], writes=[hs.b])
            S.dve(V("tensor_scalar", out=hs.ap[:, 0:nh], in0=hs.ap[:, 0:nh], scalar1=1.0 / 64, scalar2=EPS,
                    op0=ALU.mult, op1=ALU.add), reads=[hs.b], writes=[hs.b])
            S.act(V("activation", out=hs.ap[:, 0:nh], in_=hs.ap[:, 0:nh], func=AF.Sqrt), reads=[hs.b], writes=[hs.b])
            S.dve(V("reciprocal", out=hs.ap[:, 0:nh], in_=hs.ap[:, 0:nh]), reads=[hs.b], writes=[hs.b])
            zv = z.ap[:, 0:n].rearrange("p (h d) -> p h d", d=64)
            S.dve(V("tensor_tensor", out=zv, in0=zv, in1=hs.ap[:, 0:nh].unsqueeze(2).to_broadcast([128, nh, 64]),
                    op=ALU.mult), reads=[z.b, hs.b], writes=[z.b])
            S.dve(V("tensor_tensor", out=z.ap[:, 0:n], in0=z.ap[:, 0:n], in1=gains.ap, op=ALU.mult),
                  reads=[z.b, gains.b], writes=[z.b])

        def rope(z, col0, nh, cs_t, sn_t, rt=None):
            rt = rt_default if rt is None else rt
            v = z.ap[:, col0:col0 + nh * 64].rearrange("p (h d) -> p h d", d=64)
            x1 = v[:, :, 0:8]
            x2 = v[:, :, 8:16]
            cb = cs_t.ap.unsqueeze(1).to_broadcast([128, nh, 8])
            sb_ = sn_t.ap.unsqueeze(1).to_broadcast([128, nh, 8])
            t = [r.ap[:, 0:nh, :] for r in rt]
            rd = [z.b, cs_t.b, sn_t.b]
            S.pool(V("tensor_tensor", out=t[0], in0=x1, in1=cb, op=ALU.mult), reads=rd, writes=[rt[0].b])
            S.pool(V("tensor_tensor", out=t[1], in0=x2, in1=sb_, op=ALU.mult), reads=rd, writes=[rt[1].b])
            S.pool(V("tensor_tensor", out=t[2], in0=x2, in1=cb, op=ALU.mult), reads=rd, writes=[rt[2].b])
            S.pool(V("tensor_tensor", out=t[3], in0=x1, in1=sb_, op=ALU.mult), reads=rd, writes=[rt[3].b])
            S.pool(V("tensor_tensor", out=x1, in0=t[0], in1=t[1], op=ALU.subtract), reads=[rt[0].b, rt[1].b], writes=[z.b])
            S.pool(V("tensor_tensor", out=x2, in0=t[2], in1=t[3], op=ALU.add), reads=[rt[2].b, rt[3].b], writes=[z.b])

        def transposes_bf(src, ncol_blocks, dst, dst_c0, bank=6):
            pv = PSb16(bank)
            for c in range(ncol_blocks):
                S.pe(V("transpose", out=pv[:, c * 128:(c + 1) * 128], in_=src.ap[:, c * 128:(c + 1) * 128],
                       identity=ident_b.ap), reads=[src.b, ident_b.b], writes=[psb[bank]])
            S.dve(V("tensor_copy", out=dst.ap[:, dst_c0:dst_c0 + ncol_blocks, :],
                    in_=pv[:, 0:ncol_blocks * 128].rearrange("p (c k) -> p c k", k=128)),
                  reads=[psb[bank]], writes=[dst.b])

        def ki_to_kiT(src_f32_ap, src_b, tidx, kib):
            S.pool(V("tensor_copy", out=kib.ap[:, 0:64], in_=src_f32_ap), reads=[src_b], writes=[kib.b])
            S.pool(V("tensor_copy", out=kib.ap[:, 64:128], in_=src_f32_ap), reads=[src_b], writes=[kib.b])
            bank = 6
            pv = PSb16(bank)
            S.pe(V("transpose", out=pv[:, 0:128], in_=kib.ap, identity=ident_b.ap), reads=[kib.b, ident_b.b], writes=[psb[bank]])
            S.dve(V("tensor_copy", out=kiT.ap[:, tidx * 128:(tidx + 1) * 128], in_=pv[:, 0:128]),
                  reads=[psb[bank]], writes=[kiT_b[tidx]])

        KCH = [(0, 512, "ka"), (512, 1024, "kb"), (1024, 1536, "va"), (1536, 2048, "vb"), (2048, 2112, "ki")]
        QCH = [(2112, 2624, "qa"), (2624, 3136, "qb"), (3136, 3648, "qi"), (3648, 4160, "g0"), (4160, 4672, "g1"),
               (4672, 5184, "g2"), (5184, 5696, "g3"), (5696, 5704, "wi")]

        def p1_load(t):
            p = t % 2
            S.dma("sp", xt[p].ap, x_all[t * 128:(t + 1) * 128, :], writes=[xt[p].b])
            S.dma("sp", cs[t % 4].ap, cos_all[t * 128:(t + 1) * 128, :], writes=[cs[t % 4].b])
            S.dma("sp", sn[t % 4].ap, sin_all[t * 128:(t + 1) * 128, :], writes=[sn[t % 4].b])

        def p1_front(t, after_chunk=None):
            own = (t % 2 == 0)
            slot = t // 2
            p = t % 2
            po = slot % 2
            xb = xt[p]
            sq_tmp = sq_tmps[p]
            S.act(V("activation", out=sq_tmp.ap, in_=xb.ap, func=AF.Square, accum_out=ssx[p].ap),
                  reads=[xb.b], writes=[sq_tmp.b, ssx[p].b])
            S.dve(V("tensor_scalar", out=rstd[p].ap, in0=ssx[p].ap, scalar1=1.0 / D, scalar2=EPS, op0=ALU.mult, op1=ALU.add),
                  reads=[ssx[p].b], writes=[rstd[p].b])
            S.act(V("activation", out=rstd[p].ap, in_=rstd[p].ap, func=AF.Sqrt), reads=[rstd[p].b], writes=[rstd[p].b])
            S.dve(V("reciprocal", out=rstd[p].ap, in_=rstd[p].ap), reads=[rstd[p].b], writes=[rstd[p].b])
            for c in range(8):
                bank = c // 4
                S.pe(V("transpose", out=PS[:, bank, (c % 4) * 128:(c % 4 + 1) * 128], in_=xb.ap[:, c * 128:(c + 1) * 128],
                       identity=ident_f.ap), reads=[xb.b, ident_f.b], writes=[psb[bank]])
            for c in range(8):
                bank = c // 4
                src = PS[:, bank, (c % 4) * 128:(c % 4 + 1) * 128]
                S.dve(V("tensor_scalar", out=xT[p].ap[:, c, :], in0=src, scalar1=gmix.ap[:, c:c + 1], scalar2=None,
                        op0=ALU.mult), reads=[psb[bank], gmix.b], writes=[xT_b[p][c]])
            if t + 1 < NT1:
                p1_load(t + 1)
            chunks = KCH + (QCH if own else [])
            for (c0, c1, kind) in chunks:
                n = c1 - c0
                bank = 2 + (zrot[0] % 4)
                zrot[0] += 1
                for c in range(8):
                    S.pe(V("matmul", PS[:, bank, 0:n], lhsT=xT[p].ap[:, c, :], rhs=w1.ap[:, c, c0:c1], start=(c == 0), stop=(c == 7)),
                         reads=[xT_b[p][c], w1_b[c0 // 2048], w1_b[(c1 - 1) // 2048]], writes=[psb[bank]])
                src = PS[:, bank, 0:n]
                rd = [psb[bank], rstd[p].b]
                sc = rstd[p].ap
                if kind == "ka":
                    S.act(V("activation", out=z_k[p].ap[:, 0:512], in_=src, func=AF.Copy, scale=sc), reads=rd, writes=[z_k[p].b])
                elif kind == "kb":
                    S.act(V("activation", out=z_k[p].ap[:, 512:1024], in_=src, func=AF.Copy, scale=sc), reads=rd, writes=[z_k[p].b])
                elif kind == "va":
                    S.act(V("activation", out=z_v[p].ap[:, 0:512], in_=src, func=AF.Copy, scale=sc), reads=rd, writes=[z_v[p].b])
                elif kind == "vb":
                    S.act(V("activation", out=z_v[p].ap[:, 512:1024], in_=src, func=AF.Copy, scale=sc), reads=rd, writes=[z_v[p].b])
                elif kind == "ki":
                    S.act(V("activation", out=z_ki[p].ap, in_=src, func=AF.Copy, scale=sc), reads=rd, writes=[z_ki[p].b])
                elif kind == "qa":
                    S.act(V("activation", out=z_q[po].ap[:, 0:512], in_=src, func=AF.Copy, scale=sc), reads=rd, writes=[z_q[po].b])
                elif kind == "qb":
                    S.act(V("activation", out=z_q[po].ap[:, 512:1024], in_=src, func=AF.Copy, scale=sc), reads=rd, writes=[z_q[po].b])
                elif kind == "qi":
                    S.act(V("activation", out=z_qi[po].ap, in_=src, func=AF.Copy, scale=sc), reads=rd, writes=[z_qi[po].b])
                elif kind[0] == "g":
                    j = int(kind[1])
                    S.act(V("activation", out=gates.ap[:, j * 512:(j + 1) * 512], in_=src, func=AF.Sigmoid, scale=sc), reads=rd, writes=[gates.b])
                elif kind == "wi":
                    S.act(V("activation", out=wi_sb[po].ap, in_=src, func=AF.Copy, scale=sc), reads=rd, writes=[wi_sb[po].b])
                if after_chunk is not None:
                    after_chunk()
        def p1_back_k(t):
            own = (t % 2 == 0)
            slot = t // 2
            p = t % 2
            po = slot % 2
            sq_tmp = sq_tmps[p]
            knb = knbs[p]
            kib = kibs[p]
            kT_sb = kT_sbs[p]
            headnorm(z_k[p], 16, g_kab, hs_k[p], sq_tmp)
            rope(z_k[p], 0, 8, cs[t % 4], sn[t % 4])
            headnorm(z_ki[p], 1, g_ki, hs_i[p], sq_tmp)
            rope(z_ki[p], 0, 1, cs[t % 4], sn[t % 4])
            if own:
                r0 = slot * 128
                S.dma("sp", ka_o[r0:r0 + 128, :], z_k[p].ap[:, 0:512], reads=[z_k[p].b])
                S.dma("sp", va_o[r0:r0 + 128, :], z_v[p].ap[:, 0:512], reads=[z_v[p].b])
                S.dma("sp", ki_o[r0:r0 + 128, :], z_ki[p].ap, reads=[z_ki[p].b])
                if slot >= 30:
                    rb = (slot - 30) * 128
                    S.dma("sp", kb_o[rb:rb + 128, :], z_k[p].ap[:, 512:1024], reads=[z_k[p].b])
                    S.dma("sp", vb_o[rb:rb + 128, :], z_v[p].ap[:, 512:1024], reads=[z_v[p].b])
            S.dve(V("tensor_copy", out=knb.ap, in_=z_k[p].ap), reads=[z_k[p].b], writes=[knb.b])
            transposes_bf(knb, 8, kT_sb, 0)
            S.dma("sp", kT_scr[t], kT_sb.ap, reads=[kT_sb.b], writes=[kscr_b[t]])
            S.act(V("activation", out=vaug[p].ap[:, :, 0:64], in_=z_v[p].ap.rearrange("p (h d) -> p h d", d=64), func=AF.Copy),
                  reads=[z_v[p].b], writes=[vaug[p].b])
            S.dma("sp", v_scr[t], vaug[p].ap, reads=[vaug[p].b], writes=[vscr_b[t]])
            ki_to_kiT(z_ki[p].ap, z_ki[p].b, t, kib)
            if own:
                S.dma("sp", gate_scr[slot], gates.ap, reads=[gates.b], writes=[qscr_b[slot]])
                S.dma("sp", wi_scr[slot], wi_sb[po].ap, reads=[wi_sb[po].b], writes=[qscr_b[slot]])

        def p1_back_q(t):
            slot = t // 2
            p = t % 2
            po = slot % 2
            sq_tmp = sq_tmps[p]
            headnorm(z_q[po], 16, g_qab, hs_q[po], sq_tmp)
            rope(z_q[po], 0, 8, cs[t % 4], sn[t % 4], rtq)
            rope(z_qi[po], 0, 8, cs[t % 4], sn[t % 4], rtq)
            S.dve(V("tensor_copy", out=qnb.ap, in_=z_q[po].ap), reads=[z_q[po].b], writes=[qnb.b])
            transposes_bf(qnb, 8, qT_sb, 0, bank=7)
            S.dma("sp", q_scr[slot], qT_sb.ap, reads=[qT_sb.b], writes=[qscr_b[slot]])
            S.dve(V("tensor_copy", out=qib.ap, in_=z_qi[po].ap), reads=[z_qi[po].b], writes=[qib.b])
            transposes_bf(qib, 4, qiT_sb, 0, bank=7)
            S.dma("sp", qi_scr[slot], qiT_sb.ap, reads=[qiT_sb.b], writes=[qscr_b[slot]])

        import os
        PH = os.environ.get("KPH", "123")
        NT1 = int(os.environ.get("KNT", NT))
        for m in (range(8) if "c" in PH or "2" in PH else []):
            p = m % 2
            tidx = 65 + m
            xb = xt[p]
            knb = knbs[p]
            kT_sb = kT_sbs[p]
            S.dma("sp", xb.ap[:, 0:512], ck_a[m * 128:(m + 1) * 128, :], writes=[xb.b])
            S.dma("sp", xb.ap[:, 512:1024], cv_a[m * 128:(m + 1) * 128, :], writes=[xb.b])
            S.dve(V("tensor_copy", out=knb.ap[:, 0:512], in_=xb.ap[:, 0:512]), reads=[xb.b], writes=[knb.b])
            S.pool(V("tensor_copy", out=vaug[p].ap[:, 0:8, 0:64], in_=xb.ap[:, 512:1024].rearrange("p (h d) -> p h d", d=64)),
                   reads=[xb.b], writes=[vaug[p].b])
            if m >= 4:
                zb = z_k[p]
                S.dma("sp", zb.ap[:, 0:512], ck_b[(m - 4) * 128:(m - 3) * 128, :], writes=[zb.b])
                S.dma("sp", zb.ap[:, 512:1024], cv_b[(m - 4) * 128:(m - 3) * 128, :], writes=[zb.b])
                S.dve(V("tensor_copy", out=knb.ap[:, 512:1024], in_=zb.ap[:, 0:512]), reads=[zb.b], writes=[knb.b])
                S.pool(V("tensor_copy", out=vaug[p].ap[:, 8:16, 0:64], in_=zb.ap[:, 512:1024].rearrange("p (h d) -> p h d", d=64)),
                       reads=[zb.b], writes=[vaug[p].b])
            nb_ = 8 if m >= 4 else 4
            transposes_bf(knb, nb_, kT_sb, 0)
            S.dma("sp", kT_scr[tidx][:, 0:nb_, :], kT_sb.ap[:, 0:nb_, :], reads=[kT_sb.b], writes=[kscr_b[tidx]])
            S.dma("sp", v_scr[tidx][:, 0:2 * nb_, :], vaug[p].ap[:, 0:2 * nb_, :], reads=[vaug[p].b], writes=[vscr_b[tidx]])
            zi = z_ki[p]
            S.dma("sp", zi.ap, ck_i[m * 128:(m + 1) * 128, :], writes=[zi.b])
            ki_to_kiT(zi.ap, zi.b, tidx, kibs[p])

        def merge_ops(a_, b_):
            out = []
            i = j = 0
            while i < len(a_) or j < len(b_):
                if j >= len(b_) or (i < len(a_) and i * max(1, len(b_)) <= j * max(1, len(a_))):
                    out.append(a_[i])
                    i += 1
                else:
                    out.append(b_[j])
                    j += 1
            return out

        p1_load(0)
        p1_front(0)
        pend_q = []
        for t in range(NT1):
            S.begin_defer()
            p1_back_k(t)
            ops_k = S.end_defer()
            ops = merge_ops(ops_k, pend_q)
            pend_q = []
            if t % 2 == 0:
                S.begin_defer()
                p1_back_q(t)
                pend_q = S.end_defer()
            if t + 1 < NT1:
                nch = 13 if (t + 1) % 2 == 0 else 5
                per = (len(ops) + nch - 1) // nch
                p1_front(t + 1, after_chunk=lambda: S.replay(ops, per))
            S.replay(ops, len(ops))
        S.replay(pend_q, len(pend_q))

        S.barrier()
        A2 = Arena(arena_t, PBASE, ARENA_BYTES)
        NMAX = 64 * 128
        isc = [A2.alloc([128, NMAX], F32) for _ in range(2)]
        junk = A2.alloc([128, NMAX], U8)
        mb = [A2.alloc([128, NMAX], BF16) for _ in range(2)]
        kbuf = [A2.alloc([128, 4, 4, 128], BF16) for _ in range(2)]
        vbuf = [A2.alloc([128, 4, 8, 65], BF16) for _ in range(2)]
        bandb = A2.alloc([128, 8, 768], BF16)
        bstage = A2.alloc([128, 768], F32)
        qTb = [A2.alloc([128, 16, 128], BF16) for _ in range(2)]
        qiTb = [A2.alloc([128, 8, 128], BF16) for _ in range(3)]
        wib = [A2.alloc([128, 8], F32) for _ in range(3)]
        diag = [A2.alloc([128, 8, 128], BF16) for _ in range(3)]
        Rb = [A2.alloc([128, 512], BF16) for _ in range(4)]
        pTb = [A2.alloc([128, 512], BF16) for _ in range(4)]
        LAG = 3
        SBANKS = [3, 4, 7, 0, 1]
        HBANKS = [0, 1, 3, 4, 7]
        oab = [A2.alloc([128, 1024], BF16) for _ in range(2)]
        bst = [A2.alloc([128, 4 * (KSTEPS + 2)], F32) for _ in range(2)]
        lnb = A2.alloc([128, 8], F32)
        rsb = [A2.alloc([128, 1], F32) for _ in range(8)]
        cstg = [A2.alloc([128, 1024], F32) for _ in range(2)]
        cout = [A2.alloc([128, 1024], BF16) for _ in range(2)]
        conv_i = [0]

        def conv_src(i):
            if i < 4:
                return w_oa[i * 128:(i + 1) * 128, :]
            if i < 8:
                return w_ob[(i - 4) * 128:(i - 3) * 128, :]
            if i < 16:
                return w_out[(i - 8) * 128:(i - 7) * 128, :]
            if i < 48:
                c, j = (i - 16) // 4, (i - 16) % 4
                return w_up[c * 128:(c + 1) * 128, j * 1024:(j + 1) * 1024]
            return w_down[(i - 48) * 128:(i - 47) * 128, :]

        def conv_steps(k):
            for _ in range(k):
                i = conv_i[0]
                if i >= 80:
                    return
                conv_i[0] += 1
                S.dma("pool", cstg[i % 2].ap, conv_src(i), writes=[cstg[i % 2].b])
                S.pool(V("tensor_copy", out=cout[i % 2].ap, in_=cstg[i % 2].ap), reads=[cstg[i % 2].b], writes=[cout[i % 2].b])
                S.dma("pool", wbf_scr[:, i * 1024:(i + 1) * 1024], cout[i % 2].ap, reads=[cout[i % 2].b], writes=[wscr_b])

        rot = {"sh": 0, "R": 0, "st": 0, "pT": 0, "kv": 0}
        K1 = KSTEPS + 2

        def load_band(src_d):
            for h in range(8):
                S.dma("sp", bstage.ap, src_d[h], writes=[bstage.b])
                S.act(V("activation", out=bandb.ap[:, h, :], in_=bstage.ap, func=AF.Copy, scale=8.0), reads=[bstage.b], writes=[bandb.b])

        def groups_of(tiles):
            gs = []
            for pos, t in enumerate(tiles):
                if gs and len(gs[-1]) < 4 and gs[-1][-1][1] + 1 == t:
                    gs[-1].append((pos, t))
                else:
                    gs.append([(pos, t)])
            return gs

        def p2_load_idx(s):
            b = s % 3
            S.dma("sp", qiTb[b].ap[0:64, 0::2, :], qi_scr[s][0:64, :, :], reads=[qscr_b[s]], writes=[qiTb[b].b])
            S.dma("sp", qiTb[b].ap[64:128, 1::2, :], qi_scr[s][64:128, :, :], reads=[qscr_b[s]], writes=[qiTb[b].b])
            S.dma("sp", wib[b].ap, wi_scr[s], reads=[qscr_b[s]], writes=[wib[b].b])
            for h in range(8):
                S.pool(V("tensor_scalar", out=diag[b].ap[:, h, :], in0=ident_b.ap, scalar1=wib[b].ap[:, h:h + 1], scalar2=None, op0=ALU.mult),
                       reads=[ident_b.b, wib[b].b], writes=[diag[b].b])

        def p2_load_q(s):
            b = s % 2
            S.dma("sp", qTb[b].ap[0:64, 0::2, :], q_scr[s][0:64, :, :], reads=[qscr_b[s]], writes=[qTb[b].b])
            S.dma("sp", qTb[b].ap[64:128, 1::2, :], q_scr[s][64:128, :, :], reads=[qscr_b[s]], writes=[qTb[b].b])

        def p2_index(s, tiles):
            b = s % 2
            b3 = s % 3
            steps = [(g, h) for g in groups_of(tiles) for h in range(8)]
            q = []

            def emit_d(pd):
                (pg, ph, pR, pn) = pd
                S.pe(V("matmul", PS[:, 2, 0:pn], lhsT=diag[b3].ap[:, ph, :], rhs=pR.ap[:, 0:pn], start=(ph == 0), stop=(ph == 7)),
                     reads=[diag[b3].b, pR.b], writes=[psb[2]])
                if ph == 7:
                    c0 = pg[0][0] * 128
                    S.act(V("activation", out=isc[b].ap[:, c0:c0 + pn], in_=PS[:, 2, 0:pn], func=AF.Copy),
                          reads=[psb[2]], writes=[isc[b].b])

            for (g, h) in steps:
                n = len(g) * 128
                t0 = g[0][1]
                bank = HBANKS[rot["sh"] % 5]
                rot["sh"] += 1
                base = 64 * (h % 2)
                S.pe(V("matmul", PS[:, bank, 0:n], lhsT=qiTb[b3].ap[:, h, :],
                       rhs=kiT.ap[:, t0 * 128:t0 * 128 + n], start=True, stop=True),
                     reads=[qiTb[b3].b] + [kiT_b[t] for (_, t) in g], writes=[psb[bank]])
                R = Rb[rot["R"] % 4]
                rot["R"] += 1
                S.act(V("activation", out=R.ap[:, 0:n], in_=PS[:, bank, 0:n], func=AF.Relu), reads=[psb[bank]], writes=[R.b])
                q.append((g, h, R, n))
                if len(q) > LAG:
                    emit_d(q.pop(0))
            while q:
                emit_d(q.pop(0))

        def p2_bisect(s, tiles, mask_list):
            b = s % 2
            N = len(tiles) * 128
            X = isc[b]
            st_ = bst[b]
            amax = st_.ap[:, 0:1]
            wk = st_.ap[:, K1:2 * K1]
            mid = st_.ap[:, 2 * K1:3 * K1]
            cnt = st_.ap[:, 3 * K1:4 * K1]
            w2 = st_.ap[:, 1:K1]
            S.dve(V("tensor_reduce", out=amax, in_=X.ap[:, 0:N], axis=AX.X, op=ALU.max, apply_absolute_value=True),
                  reads=[X.b], writes=[st_.b])
            for (pos, mcol) in mask_list:
                S.dve(V("tensor_tensor", out=X.ap[:, pos * 128:(pos + 1) * 128], in0=X.ap[:, pos * 128:(pos + 1) * 128],
                        in1=masks.ap[:, mcol * 128:(mcol + 1) * 128], op=ALU.add), reads=[X.b, masks.b], writes=[X.b])
            S.dve(V("tensor_scalar", out=wk, in0=pow2.ap[:, 0:K1], scalar1=amax, scalar2=None, op0=ALU.mult),
                  reads=[st_.b, pow2.b], writes=[st_.b])
            S.dve(V("tensor_scalar", out=w2, in0=pow2.ap[:, K1 + 1:2 * K1], scalar1=amax, scalar2=None, op0=ALU.mult),
                  reads=[st_.b, pow2.b], writes=[st_.b])
            S.dve(V("memset", mid[:, 0:1], 0.0), writes=[st_.b])
            for k in range(KSTEPS):
                S.dve(V("tensor_scalar", out=junk.ap[:, 0:N], in0=X.ap[:, 0:N], scalar1=mid[:, k:k + 1], scalar2=None,
                        op0=ALU.is_ge, op1=ALU.add, accum_out=cnt[:, k:k + 1]), reads=[X.b, st_.b], writes=[junk.b, st_.b])
                S.dve(V("scalar_tensor_tensor", out=cnt[:, k:k + 1], in0=cnt[:, k:k + 1], scalar=255.5, in1=w2[:, k:k + 1],
                        op0=ALU.is_ge, op1=ALU.mult), reads=[st_.b], writes=[st_.b])
                S.dve(V("scalar_tensor_tensor", out=mid[:, k + 1:k + 2], in0=mid[:, k:k + 1], scalar=wk[:, k + 1:k + 2],
                        in1=cnt[:, k:k + 1], op0=ALU.subtract, op1=ALU.add), reads=[st_.b], writes=[st_.b])
            thr = cnt[:, KSTEPS:KSTEPS + 1]
            S.dve(V("tensor_tensor", out=thr, in0=mid[:, KSTEPS:KSTEPS + 1], in1=wk[:, KSTEPS:KSTEPS + 1], op=ALU.subtract),
                  reads=[st_.b], writes=[st_.b])
            S.dve(V("tensor_scalar", out=mb[b].ap[:, 0:N], in0=X.ap[:, 0:N], scalar1=thr, scalar2=NEG, op0=ALU.is_lt, op1=ALU.mult),
                  reads=[X.b, st_.b], writes=[mb[b].b])

        def attend(s, tiles, pair0, vh0, bias_mode):
            b = s % 2
            qT = qTb[b]
            steps = []
            for g in groups_of(tiles):
                kv = rot["kv"] % 2
                rot["kv"] += 1
                ng = len(g)
                t0 = g[0][1]
                steps.append(("load", g, kv))
                for j, (pos, t) in enumerate(g):
                    for hg in range(2):
                        steps.append(("qk", pos, j, hg, kv))
            first = [True, True]
            nsteps_left = [sum(1 for x in steps if x[0] == "qk" and x[3] == hg) for hg in range(2)]
            pendq = []

            def emit_pv(pd):
                (pos, j, hg, kv, pt) = pd
                nsteps_left[hg] -= 1
                for hh in range(4):
                    h = 4 * hg + hh
                    S.pe(V("matmul", PS[:, 5 + hg, hh * 65:(hh + 1) * 65], lhsT=pt.ap[:, hh * 128:(hh + 1) * 128],
                           rhs=vbuf[kv].ap[:, j, h, :], start=(first[hg] and hh == 0), stop=(nsteps_left[hg] == 0),
                           skip_group_check=True), reads=[pt.b, vbuf[kv].b], writes=[psb[5 + hg]])
                first[hg] = False

            for stp in steps:
                if stp[0] == "load":
                    (_, g, kv) = stp
                    ng = len(g)
                    t0 = g[0][1]
                    S.dma("sp", kbuf[kv].ap[:, 0:ng], kT_scr[t0:t0 + ng].rearrange("t p c k -> p t c k")[:, :, pair0:pair0 + 4, :],
                          reads=[kscr_b[t] for (_, t) in g], writes=[kbuf[kv].b])
                    S.dma("sp", vbuf[kv].ap[:, 0:ng], v_scr[t0:t0 + ng].rearrange("t p h d -> p t h d")[:, :, vh0:vh0 + 8, :],
                          reads=[vscr_b[t] for (_, t) in g], writes=[vbuf[kv].b])
                    continue
                (_, pos, j, hg, kv) = stp
                sbank = SBANKS[rot["st"] % 5]
                rot["st"] += 1
                if bias_mode is None:
                    S.pe(V("matmul", PS[:, sbank, :], lhsT=mb[b].ap[:, pos * 128:(pos + 1) * 128], rhs=I4.ap, start=True, stop=False,
                           skip_group_check=True), reads=[mb[b].b, I4.b], writes=[psb[sbank]])
                for hh in range(4):
                    h = 4 * hg + hh
                    pr = h // 2
                    base = 64 * (h % 2)
                    if bias_mode is not None:
                        m = bias_mode[pos]
                        S.pe(V("matmul", PS[:, sbank, hh * 128:(hh + 1) * 128], lhsT=bandb.ap[:, h, m * 128:(m + 1) * 128], rhs=ident_b.ap,
                               start=True, stop=False, skip_group_check=True), reads=[bandb.b, ident_b.b], writes=[psb[sbank]])
                    S.pe(V("matmul", PS[:, sbank, hh * 128:(hh + 1) * 128], lhsT=kbuf[kv].ap[:, j, pr, :],
                           rhs=qT.ap[:, 2 * pair0 + h, :], start=False, stop=True, skip_group_check=True),
                         reads=[kbuf[kv].b, qT.b], writes=[psb[sbank]])
                pt = pTb[rot["pT"] % 4]
                rot["pT"] += 1
                S.act(V("activation", out=pt.ap, in_=PS[:, sbank, :], func=AF.Exp, scale=0.125), reads=[psb[sbank]], writes=[pt.b])
                pendq.append((pos, j, hg, kv, pt))
                if len(pendq) > LAG:
                    emit_pv(pendq.pop(0))
            while pendq:
                emit_pv(pendq.pop(0))
            col0 = 0 if bias_mode is None else 512
            for hg in range(2):
                ov = PS[:, 5 + hg, 0:260].rearrange("p (h d) -> p h d", d=65)
                S.act(V("activation", out=lnb.ap[:, hg * 4:(hg + 1) * 4], in_=ov[:, :, 64], func=AF.Ln), reads=[psb[5 + hg]], writes=[lnb.b])
                for hh in range(4):
                    h = 4 * hg + hh
                    S.act(V("activation", out=rsb[h].ap, in_=lnb.ap[:, h:h + 1], func=AF.Exp, scale=-1.0), reads=[lnb.b], writes=[rsb[h].b])
                    S.act(V("activation", out=oab[b].ap[:, col0 + h * 64:col0 + (h + 1) * 64], in_=ov[:, hh, 0:64], func=AF.Copy,
                            scale=rsb[h].ap), reads=[psb[5 + hg], rsb[h].b], writes=[oab[b].b])

        slots = []
        for i in range(32):
            tiles = list(range(2 * i + 2))
            masks_l = [(2 * i, 0), (2 * i + 1, 1)]
            band = [(2 * i - 4 + m, m) for m in range(6) if 2 * i - 4 + m >= 0]
            slots.append((tiles, masks_l, band))
        slots.append((list(range(65, 73)) + [64], [(8, 0)], [(69 + m, m) for m in range(4)] + [(64, 4)]))

        NS2 = min(NSLOT, int(os.environ.get("KNS", NSLOT))) if "2" in PH else 0
        for i in range(2):
            S.pool(V("memset", qTb[i].ap, 0.0), writes=[qTb[i].b])
        for i in range(3):
            S.pool(V("memset", qiTb[i].ap, 0.0), writes=[qiTb[i].b])

        def mark(n):
            if os.environ.get("KDBG"):
                print("MARK", n, S.uid, flush=True)
        if NS2:
          mark("p2start")
          load_band(bandp_d)
          mark("band")
          p2_load_idx(0)
          mark("loadidx0")
          p2_index(0, slots[0][0])
          mark("index0")
          p2_bisect(0, slots[0][0], slots[0][1])
          mark("bisect0")
          p2_load_idx(1)
          p2_index(1, slots[1][0])
          if NS2 > 2:
              p2_load_idx(2)
          p2_load_q(0)
          mark("pre-loop")
        for s in range(NS2):
            if s + 3 < NS2:
                p2_load_idx(s + 3)
            if s + 2 < NS2:
                p2_index(s + 2, slots[s + 2][0])
            attend(s, slots[s][0], 0, 0, None)
            mark("dsa%d" % s)
            if s + 1 < NS2:
                p2_load_q(s + 1)
                p2_bisect(s + 1, slots[s + 1][0], slots[s + 1][1])
            if s == NSLOT - 1:
                load_band(bands_d)
            band = slots[s][2]
            mark("bis%d" % (s + 1))
            attend(s, [t for (t, m) in band], 4, 8, [m for (t, m) in band])
            mark("band%d" % s)
            if "3" in PH:
                conv_steps(3)
            S.dma("pool", oab_scr[s], oab[s % 2].ap, reads=[oab[s % 2].b], writes=[oscr_b[s]])

        S.barrier()
        A3 = Arena(arena_t, PBASE - NTS * 256, ARENA_BYTES)
        wall = A3.alloc([128, 80 * 1024], BF16)

        def wview(c0, nchunk, ncol):
            t_ = T(wall.ap[:, c0 * 1024:c0 * 1024 + nchunk * ncol].rearrange("p (c n) -> p c n", n=ncol))
            t_.b = wall.b
            return t_
        wall_bs = [Buf() for _ in range(10)]
        woa = wview(0, 4, D)
        wob = wview(4, 4, D)
        wo = wview(8, 8, D)
        wu = wview(16, 8, 4096)
        wd = wview(48, 32, D)
        woa.b = wall_bs[0]
        wob.b = wall_bs[0]
        wo.b = wall_bs[1]
        o_in = A3.alloc([128, 1024], BF16)
        oT = A3.alloc([128, 8, 128], BF16)
        gt = A3.alloc([128, 2048], BF16)
        xin = A3.alloc([128, D], F32)
        t1 = A3.alloc([128, D], F32)
        t2 = A3.alloc([128, D], F32)
        x2s = [A3.alloc([128, D], F32) for _ in range(2)]
        x2Ts = [A3.alloc([128, 8, 128], BF16) for _ in range(2)]
        hT = A3.alloc([128, 32, 128], BF16)
        rl = A3.alloc([128, 512], F32)
        ss3 = A3.alloc([128, 1], F32)
        r3s = [A3.alloc([128, 1], F32) for _ in range(2)]

        if "3" in PH:
            conv_steps(80)
            for k in range(10):
                S.dma("sp", wall.ap[:, k * 8192:(k + 1) * 8192], wbf_scr[:, k * 8192:(k + 1) * 8192], reads=[wscr_b], writes=[wall_bs[k]])

        def tr_bf_p3(src, dst):
            pv = PSb16(0)
            for c in range(8):
                S.pe(V("transpose", out=pv[:, c * 128:(c + 1) * 128], in_=src.ap[:, c * 128:(c + 1) * 128], identity=ident_b.ap),
                     reads=[src.b, ident_b.b], writes=[psb[0]])
            S.dve(V("tensor_copy", out=dst.ap, in_=pv.rearrange("p (c k) -> p c k", k=128)), reads=[psb[0]], writes=[dst.b])

        def p3_stage_a(s):
            x2 = x2s[s % 2]
            x2T = x2Ts[s % 2]
            r3 = r3s[s % 2]
            S.dma("sp", o_in.ap, oab_scr[s], reads=[oscr_b[s]], writes=[o_in.b])
            S.dma("sp", gt.ap, gate_scr[s], reads=[qscr_b[s]], writes=[gt.b])
            S.dma("sp", xin.ap, x_own[s * 128:(s + 1) * 128, :], writes=[xin.b])
            tr_bf_p3(o_in, oT)
            for half in range(2):
                for c in range(4):
                    S.pe(V("matmul", PS[:, 2 + half, :], lhsT=oT.ap[:, c, :], rhs=woa.ap[:, c, half * 512:(half + 1) * 512],
                           start=(c == 0), stop=(c == 3)), reads=[oT.b, woa.b], writes=[psb[2 + half]])
                S.dve(V("tensor_tensor", out=t1.ap[:, half * 512:(half + 1) * 512], in0=PS[:, 2 + half, :],
                        in1=gt.ap[:, half * 512:(half + 1) * 512], op=ALU.mult), reads=[psb[2 + half], gt.b], writes=[t1.b])
            for half in range(2):
                for c in range(4):
                    S.pe(V("matmul", PS[:, 2 + half, :], lhsT=oT.ap[:, 4 + c, :], rhs=wob.ap[:, c, half * 512:(half + 1) * 512],
                           start=(c == 0), stop=(c == 3)), reads=[oT.b, wob.b], writes=[psb[2 + half]])
                S.dve(V("tensor_tensor", out=t2.ap[:, half * 512:(half + 1) * 512], in0=PS[:, 2 + half, :],
                        in1=gt.ap[:, 1024 + half * 512:1024 + (half + 1) * 512], op=ALU.mult), reads=[psb[2 + half], gt.b], writes=[t2.b])
            S.pool(V("tensor_tensor", out=o_in.ap, in0=t1.ap, in1=t2.ap, op=ALU.add), reads=[t1.b, t2.b], writes=[o_in.b])
            tr_bf_p3(o_in, oT)
            for half in range(2):
                for c in range(8):
                    S.pe(V("matmul", PS[:, 2 + half, :], lhsT=oT.ap[:, c, :], rhs=wo.ap[:, c, half * 512:(half + 1) * 512],
                           start=(c == 0), stop=(c == 7)), reads=[oT.b, wo.b], writes=[psb[2 + half]])
                S.dve(V("tensor_tensor", out=x2.ap[:, half * 512:(half + 1) * 512], in0=PS[:, 2 + half, :],
                        in1=xin.ap[:, half * 512:(half + 1) * 512], op=ALU.add), reads=[psb[2 + half], xin.b], writes=[x2.b])
            S.act(V("activation", out=t1.ap, in_=x2.ap, func=AF.Square, accum_out=ss3.ap), reads=[x2.b], writes=[t1.b, ss3.b])
            S.dve(V("tensor_scalar", out=r3.ap, in0=ss3.ap, scalar1=1.0 / D, scalar2=EPS, op0=ALU.mult, op1=ALU.add),
                  reads=[ss3.b], writes=[r3.b])
            S.dve(V("reciprocal", out=r3.ap, in_=r3.ap), reads=[r3.b], writes=[r3.b])
            for c in range(8):
                bank = c // 4
                S.pe(V("transpose", out=PS[:, bank, (c % 4) * 128:(c % 4 + 1) * 128], in_=x2.ap[:, c * 128:(c + 1) * 128],
                       identity=ident_f.ap), reads=[x2.b, ident_f.b], writes=[psb[bank]])
            for c in range(8):
                bank = c // 4
                src = PS[:, bank, (c % 4) * 128:(c % 4 + 1) * 128]
                S.dve(V("tensor_scalar", out=x2T.ap[:, c, :], in0=src, scalar1=gffn.ap[:, c:c + 1], scalar2=None, op0=ALU.mult),
                      reads=[psb[bank], gffn.b], writes=[x2T.b])

        def p3_stage_b(s, hook):
            x2 = x2s[s % 2]
            x2T = x2Ts[s % 2]
            r3 = r3s[s % 2]
            for f4 in range(8):
                bank = 4 + (f4 % 2)
                for ff in range(4):
                    f = f4 * 4 + ff
                    for c in range(8):
                        S.pe(V("matmul", PS[:, bank, ff * 128:(ff + 1) * 128], lhsT=wu.ap[:, c, f * 128:(f + 1) * 128], rhs=x2T.ap[:, c, :],
                               start=(c == 0), stop=(c == 7), skip_group_check=True), reads=[wall_bs[2 + c // 2], x2T.b], writes=[psb[bank]])
                S.act(V("activation", out=rl.ap, in_=PS[:, bank, :], func=AF.Relu), reads=[psb[bank]], writes=[rl.b])
                S.dve(V("tensor_tensor", out=hT.ap[:, f4 * 4:(f4 + 1) * 4, :].rearrange("p a b -> p (a b)"), in0=rl.ap,
                        in1=rl.ap, op=ALU.mult), reads=[rl.b], writes=[hT.b])
                hook()
            for half in range(2):
                for f in range(32):
                    S.pe(V("matmul", PS[:, 6 + half, :], lhsT=hT.ap[:, f, :], rhs=wd.ap[:, f, half * 512:(half + 1) * 512],
                           start=(f == 0), stop=(f == 31)), reads=[hT.b, wall_bs[6 + f // 8]], writes=[psb[6 + half]])
                    if f % 8 == 7:
                        hook()
                S.dve(V("scalar_tensor_tensor", out=x2.ap[:, half * 512:(half + 1) * 512], in0=PS[:, 6 + half, :], scalar=r3.ap,
                        in1=x2.ap[:, half * 512:(half + 1) * 512], op0=ALU.mult, op1=ALU.add),
                      reads=[psb[6 + half], r3.b, x2.b], writes=[x2.b])
            S.dma("sp", y_o[s * 128:(s + 1) * 128, :], x2.ap, reads=[x2.b])

        NS3 = min(NSLOT, int(os.environ.get("KNS3", NSLOT))) if "3" in PH else 0
        if NS3:
            p3_stage_a(0)
        for s in range(NS3):
            ops = []
            if s + 1 < NS3:
                S.begin_defer()
                p3_stage_a(s + 1)
                ops = S.end_defer()
            per = (len(ops) + 14) // 15
            p3_stage_b(s, lambda: S.replay(ops, per))
            S.replay(ops, len(ops))

        S.emit(nc, st)
    return nc


def _rope_tables(pos):
    half = 8
    inv = (np.float32(500000.0) ** (-np.arange(half, dtype=np.float32) / np.float32(half))).astype(np.float32)
    ang = pos.astype(np.float32)[:, None] * inv[None, :]
    return np.cos(ang).astype(np.float32), np.sin(ang).astype(np.float32)


def _band_tiles(table, true_tile_of_m, visible_fn):
    out = np.full((8, 128, 768), NEG, np.float32)
    qi = np.arange(128)[:, None]
    kj = np.arange(128)[None, :]
    for m in range(6):
        off = true_tile_of_m[m]
        if off is None:
            continue
        dist = off * 128 + qi - kj
        idx = np.clip(dist, -128, 128) + 128
        vis = visible_fn(off, qi, kj)
        for h in range(8):
            g = table[idx, h]
            out[h, :, m * 128:(m + 1) * 128] = np.where(vis, g, np.float32(NEG))
    return out


def _vis_prompt(off, qi, kj):
    dc = 2 * off + qi // 64 - kj // 64
    return (dc >= 0) & (dc <= 8)


_CACHE = {}


def kernel(x_prompt, x_sample, cache_k_a, cache_v_a, cache_k_idx, cache_k_b, cache_v_b,
           norm_mix, w_in, qnorm_a, knorm_a, knorm_idx, qnorm_b, knorm_b, rel_bias_b,
           w_o_a, w_o_b, w_out, norm_ffn, w_up, w_down):
    f = lambda a: np.ascontiguousarray(np.asarray(a, dtype=np.float32))
    x_prompt, x_sample = f(x_prompt), f(x_sample)
    w_in_ = f(w_in)[0]
    sp = np.cumsum([0, 512, 512, 512, 512, 64, 8, 512, 512, 512, 1024, 1024])
    seg = {n: (sp[i], sp[i + 1]) for i, n in enumerate(["qa", "ka", "va", "qi", "ki", "wi", "qb", "kb", "vb", "ga", "gb"])}
    order = ["ka", "kb", "va", "vb", "ki", "qa", "qb", "qi", "ga", "gb", "wi"]
    w_in_r = np.ascontiguousarray(np.concatenate([w_in_[:, seg[n][0]:seg[n][1]] for n in order], axis=1))
    rep = lambda g, n: np.tile(f(g)[0], n)
    g_kab = np.ascontiguousarray(np.broadcast_to(np.concatenate([rep(knorm_a, 8), rep(knorm_b, 8)])[None], (128, 1024)))
    g_qab = np.ascontiguousarray(np.broadcast_to(np.concatenate([rep(qnorm_a, 8), rep(qnorm_b, 8)])[None], (128, 1024)))
    g_ki = np.ascontiguousarray(np.broadcast_to(f(knorm_idx)[0][None], (128, 64)))
    gmix = np.ascontiguousarray(f(norm_mix)[0].reshape(8, 128).T)
    gffn = np.ascontiguousarray(f(norm_ffn)[0].reshape(8, 128).T)
    K1 = KSTEPS + 2
    p2 = np.concatenate([2.0 ** (-np.arange(K1)), 2.0 ** (1.0 - np.arange(K1))]).astype(np.float32)
    pow2 = np.ascontiguousarray(np.broadcast_to(p2[None], (128, 2 * K1)))
    table = f(rel_bias_b)[0]
    qi = np.arange(128)[:, None]
    kj = np.arange(128)[None, :]
    own_mask = np.where((kj // 64) <= (qi // 64), 0.0, -1e30).astype(np.float32)
    band_s = _band_tiles(table, [4, 3, 2, 1, 0, None], _vis_prompt)

    in_maps = []
    for c in range(8):
        b, hf = c // 2, c % 2
        xb = x_prompt[b].reshape(64, 128, D)
        order_t = []
        for i in range(32):
            order_t += [2 * i + hf, 2 * i + 1 - hf]
        xs_pad = np.zeros((128, D), np.float32)
        xs_pad[:64] = x_sample[c]
        x_all = np.concatenate([xb[order_t].reshape(64 * 128, D), xs_pad], axis=0)
        pos = np.concatenate([(np.array(order_t)[:, None] * 128 + np.arange(128)[None]).reshape(-1),
                              1024 + np.arange(64), np.zeros(64, np.int64)])
        cos_all, sin_all = _rope_tables(pos)
        x_own = np.concatenate([xb[hf::2].reshape(32 * 128, D), xs_pad], axis=0)
        foreign = np.full((128, 128), 0.0 if hf == 1 else -1e30, np.float32)
        masks = np.ascontiguousarray(np.concatenate([own_mask, foreign], axis=1))
        if hf == 0:
            offs = [4, 3, 2, 1, 0, None]
        else:
            offs = [None, 4, 1, 2, 0 - 0, 0]
            offs = [4, None, 2, 3, 0, 1]
        band_p = _band_tiles(table, offs, _vis_prompt)
        in_maps.append(dict(
            x_all=np.ascontiguousarray(x_all), cos_all=cos_all, sin_all=sin_all, x_own=np.ascontiguousarray(x_own),
            ck_a=f(cache_k_a)[0, c].reshape(1024, 512), cv_a=f(cache_v_a)[0, c].reshape(1024, 512),
            ck_i=f(cache_k_idx)[0, c], ck_b=f(cache_k_b)[0, c].reshape(512, 512), cv_b=f(cache_v_b)[0, c].reshape(512, 512),
            w_in=w_in_r, w_oa=f(w_o_a)[0], w_ob=f(w_o_b)[0], w_out=f(w_out)[0], w_up=f(w_up)[0], w_down=f(w_down)[0],
            gmix=gmix, gffn=gffn, g_kab=g_kab, g_qab=g_qab, g_ki=g_ki, ident=np.eye(128, dtype=np.float32),
            pow2=pow2, masks=masks, band_p=band_p, band_s=band_s))

    if "nc" not in _CACHE:
        _CACHE["nc"] = build_program()
    import os
    NCORE = int(os.environ.get("KCORES", "8"))
    res = run_bass_kernel_spmd(_CACHE["nc"], in_maps[:NCORE], core_ids=list(range(NCORE)))
    R = list(res.results) + [res.results[0]] * (8 - NCORE)

    y_p = np.zeros((4, 64, 128, D), np.float32)
    ka_p = np.zeros((4, 64, 128, 512), np.float32)
    va_p = np.zeros((4, 64, 128, 512), np.float32)
    ki_p = np.zeros((4, 64, 128, 64), np.float32)
    kb_p = np.zeros((4, 4, 128, 512), np.float32)
    vb_p = np.zeros((4, 4, 128, 512), np.float32)
    y_s = np.zeros((8, 64, D), np.float32)
    ka_s = np.zeros((8, 64, 512), np.float32)
    va_s = np.zeros((8, 64, 512), np.float32)
    ki_s = np.zeros((8, 64, 64), np.float32)
    kb_s = np.zeros((8, 64, 512), np.float32)
    vb_s = np.zeros((8, 64, 512), np.float32)
    for c in range(8):
        b, hf = c // 2, c % 2
        r = R[c]
        y_p[b, hf::2] = r["y_o"][:4096].reshape(32, 128, D)
        ka_p[b, hf::2] = r["ka_o"][:4096].reshape(32, 128, 512)
        va_p[b, hf::2] = r["va_o"][:4096].reshape(32, 128, 512)
        ki_p[b, hf::2] = r["ki_o"][:4096].reshape(32, 128, 64)
        kb_p[b, hf::2] = r["kb_o"][:256].reshape(2, 128, 512)
        vb_p[b, hf::2] = r["vb_o"][:256].reshape(2, 128, 512)
        y_s[c] = r["y_o"][4096:4160]
        ka_s[c] = r["ka_o"][4096:4160]
        va_s[c] = r["va_o"][4096:4160]
        ki_s[c] = r["ki_o"][4096:4160]
        kb_s[c] = r["kb_o"][256:320]
        vb_s[c] = r["vb_o"][256:320]
    return (y_p.reshape(4, 8192, D), y_s,
            ka_p.reshape(1, 4, 8192, 8, 64), va_p.reshape(1, 4, 8192, 8, 64), ki_p.reshape(1, 4, 8192, 64),
            kb_p.reshape(1, 4, 512, 8, 64), vb_p.reshape(1, 4, 512, 8, 64),
            ka_s.reshape(1, 8, 64, 8, 64), va_s.reshape(1, 8, 64, 8, 64), ki_s.reshape(1, 8, 64, 64),
            kb_s.reshape(1, 8, 64, 8, 64), vb_s.reshape(1, 8, 64, 8, 64))
```

```python
import numpy as np
from contextlib import ExitStack
import concourse.bass as bass
import concourse.mybir as mybir
from concourse.bass_utils import run_bass_kernel_spmd

F32 = mybir.dt.float32
BF16 = mybir.dt.bfloat16
U8 = mybir.dt.uint8
ALU = mybir.AluOpType
AF = mybir.ActivationFunctionType
AX = mybir.AxisListType

D = 1024
NT = 65
NTS = 73
NSLOT = 33
DIN = 5704
KSTEPS = 16
NEG = -30000.0
EPS = 1e-6


class Buf:
    __slots__ = ("w", "r")

    def __init__(self):
        self.w = None
        self.r = {}


class Instr:
    __slots__ = ("eng", "fn", "deps", "signal", "sem", "val", "is_dma", "uid")


ENGS = ("pe", "act", "dve", "pool", "sp")
NRING = 8


class Sched:
    def __init__(self):
        self.q = {e: [] for e in ENGS}
        self.uid = 0
        self.bar = []
        self.bar_done = set()
        self.deferred = None

    def begin_defer(self):
        self.deferred = []

    def end_defer(self):
        ops = self.deferred
        self.deferred = None
        return ops

    def replay(self, ops, k):
        for _ in range(min(k, len(ops))):
            self.add(*ops.pop(0))

    def barrier(self):
        lst = []
        for e in ENGS:
            comp = [i for i in self.q[e] if not i.is_dma]
            if comp:
                lst.append(comp[-1])
            lst += [i for i in self.q[e] if i.is_dma][-NRING:]
        self.bar = lst
        self.bar_done = set()

    def add(self, eng, fn, reads=(), writes=(), dma=False):
        import os
        if self.deferred is not None:
            self.deferred.append((eng, fn, tuple(reads), tuple(writes), dma))
            return None
        if self.uid >= int(os.environ.get("KMAX", "100000000")):
            return None
        ins = Instr()
        ins.eng = eng
        ins.fn = fn
        ins.is_dma = dma
        ins.signal = dma
        ins.uid = self.uid
        self.uid += 1
        deps = {}

        def need(d, raw):
            if d is None or d is ins:
                return
            if (not d.is_dma) and (not dma) and d.eng == eng:
                if eng == "pe":
                    return
            deps[d.uid] = d

        for b in reads:
            need(b.w, True)
        for b in writes:
            need(b.w, False)
            for rd in b.r.values():
                need(rd, False)
        if self.bar and eng not in self.bar_done:
            self.bar_done.add(eng)
            for d in self.bar:
                deps[d.uid] = d
        for b in reads:
            b.r[("dma", ins.uid) if dma else eng] = ins
        for b in writes:
            b.w = ins
            b.r = {}
        ins.deps = list(deps.values())
        for d in ins.deps:
            d.signal = True
        self.q[eng].append(ins)
        return ins

    def pe(self, fn, reads=(), writes=()):
        return self.add("pe", fn, reads, writes)

    def act(self, fn, reads=(), writes=()):
        return self.add("act", fn, reads, writes)

    def dve(self, fn, reads=(), writes=()):
        return self.add("dve", fn, reads, writes)

    def pool(self, fn, reads=(), writes=()):
        return self.add("pool", fn, reads, writes)

    def dma(self, queue, out, in_, reads=(), writes=(), **kw):
        return self.add(queue, lambda e: e.dma_start(out=out, in_=in_, **kw), reads, writes, dma=True)

    def emit(self, nc, stack):
        esem = {e: stack.enter_context(nc.semaphore("s_" + e)) for e in ENGS}
        rings = {e: [stack.enter_context(nc.semaphore("r_%s%d" % (e, i))) for i in range(NRING)]
                 for e in ("sp", "pool", "act")}
        final_ring = {}
        for e in ENGS:
            cnt = 0
            nd = 0
            for ins in self.q[e]:
                if ins.is_dma:
                    ins.sem = rings[e][nd % NRING]
                    ins.val = 16 * (nd // NRING + 1)
                    final_ring[(e, nd % NRING)] = (ins.sem, ins.val)
                    nd += 1
                elif ins.signal:
                    cnt += 1
                    ins.sem = esem[e]
                    ins.val = cnt
        block = stack.enter_context(nc.Block())
        handles = {"pe": "tensor", "act": "scalar", "dve": "vector", "pool": "gpsimd", "sp": "sync"}

        def make(e):
            def body(h):
                waited = {}

                def wait(sem, val):
                    k = id(sem)
                    if waited.get(k, 0) >= val:
                        return
                    waited[k] = val
                    h.wait_ge(sem, val)

                nd = 0
                for ins in self.q[e]:
                    for d in ins.deps:
                        wait(d.sem, d.val)
                    if ins.is_dma:
                        if nd >= NRING:
                            wait(ins.sem, ins.val - 16)
                        nd += 1
                        ins.fn(h).then_inc(ins.sem, 16)
                    else:
                        r = ins.fn(h)
                        if ins.signal:
                            r.then_inc(ins.sem, 1)
                if e == "sp":
                    for (sem, val) in final_ring.values():
                        wait(sem, val)
            return body

        for e in ENGS:
            getattr(block, handles[e])(make(e))


def V(name, *a, **k):
    return lambda e: getattr(e, name)(*a, **k)


class T:
    __slots__ = ("ap", "b")

    def __init__(self, ap):
        self.ap = ap
        self.b = Buf()


_DSZ = {F32: 4, BF16: 2, U8: 1}


class Arena:
    def __init__(self, ap, base, limit):
        self.ap = ap
        self.off = base
        self.limit = limit

    def alloc(self, shape, dt):
        n = 1
        for s in shape[1:]:
            n *= s
        nb = (n * _DSZ[dt] + 31) // 32 * 32
        assert self.off + nb <= self.limit, ("arena overflow", self.off + nb, self.limit)
        v = self.ap[:, self.off // 2:(self.off + nb) // 2]
        self.off += nb
        if dt != BF16:
            v = v.bitcast(dt)
        v = v[:, 0:n]
        if len(shape) == 3:
            v = v.rearrange("p (a b) -> p a b", b=shape[2])
        elif len(shape) == 4:
            v = v.rearrange("p (a b c) -> p a b c", b=shape[2], c=shape[3])
        if shape[0] != 128:
            v = v[0:shape[0]]
        return T(v)


def build_program():
    nc = bass.Bass("TRN2", target_bir_lowering=False)

    def din(name, shape, dt=F32):
        return nc.dram_tensor(name, list(shape), dt, kind="ExternalInput").ap()

    def dout(name, shape, dt=F32):
        return nc.dram_tensor(name, list(shape), dt, kind="ExternalOutput").ap()

    def dscr(name, shape, dt):
        return nc.dram_tensor(name, list(shape), dt, kind="Internal").ap()

    x_all = din("x_all", [NT * 128, D])
    cos_all = din("cos_all", [NT * 128, 8])
    sin_all = din("sin_all", [NT * 128, 8])
    x_own = din("x_own", [NSLOT * 128, D])
    ck_a = din("ck_a", [1024, 512])
    cv_a = din("cv_a", [1024, 512])
    ck_i = din("ck_i", [1024, 64])
    ck_b = din("ck_b", [512, 512])
    cv_b = din("cv_b", [512, 512])
    w_in = din("w_in", [D, DIN])
    w_oa = din("w_oa", [512, D])
    w_ob = din("w_ob", [512, D])
    w_out = din("w_out", [D, D])
    w_up = din("w_up", [D, 4096])
    w_down = din("w_down", [4096, D])
    gmix_d = din("gmix", [128, 8])
    gffn_d = din("gffn", [128, 8])
    g_kab_d = din("g_kab", [128, 1024])
    g_qab_d = din("g_qab", [128, 1024])
    g_ki_d = din("g_ki", [128, 64])
    ident_d = din("ident", [128, 128])
    pow2_d = din("pow2", [128, 2 * (KSTEPS + 2)])
    mask_d = din("masks", [128, 256])
    bandp_d = din("band_p", [8, 128, 768])
    bands_d = din("band_s", [8, 128, 768])

    y_o = dout("y_o", [NSLOT * 128, D])
    ka_o = dout("ka_o", [NSLOT * 128, 512])
    va_o = dout("va_o", [NSLOT * 128, 512])
    ki_o = dout("ki_o", [NSLOT * 128, 64])
    kb_o = dout("kb_o", [3 * 128, 512])
    vb_o = dout("vb_o", [3 * 128, 512])

    kT_scr = dscr("kT_scr", [NTS, 128, 8, 128], BF16)
    v_scr = dscr("v_scr", [NTS, 128, 16, 65], BF16)
    q_scr = dscr("q_scr", [NSLOT, 128, 8, 128], BF16)
    qi_scr = dscr("qi_scr", [NSLOT, 128, 4, 128], BF16)
    gate_scr = dscr("gate_scr", [NSLOT, 128, 2048], BF16)
    wi_scr = dscr("wi_scr", [NSLOT, 128, 8], F32)
    oab_scr = dscr("oab_scr", [NSLOT, 128, 1024], BF16)
    wbf_scr = dscr("wbf_scr", [128, 80 * 1024], BF16)
    wscr_b = Buf()
    kscr_b = [Buf() for _ in range(NTS)]
    vscr_b = [Buf() for _ in range(NTS)]
    qscr_b = [Buf() for _ in range(NSLOT)]
    oscr_b = [Buf() for _ in range(NSLOT)]

    S = Sched()
    with ExitStack() as st:
        ARENA_BYTES = 206 * 1024
        arena_t = st.enter_context(nc.sbuf_tensor("arena", [128, ARENA_BYTES // 2], BF16))
        PS = st.enter_context(nc.psum_tensor("ps", [128, 8, 512], F32))
        psb = [Buf() for _ in range(8)]

        def PSb16(bank):
            return PS[:, bank, :].bitcast(BF16)

        A0 = Arena(arena_t, 0, ARENA_BYTES)
        ident_f = A0.alloc([128, 128], F32)
        ident_b = A0.alloc([128, 128], BF16)
        I4 = A0.alloc([128, 512], BF16)
        gmix = A0.alloc([128, 8], F32)
        gffn = A0.alloc([128, 8], F32)
        pow2 = A0.alloc([128, 2 * (KSTEPS + 2)], F32)
        masks = A0.alloc([128, 256], F32)
        kiT = A0.alloc([128, NTS * 128], BF16)
        kiT_b = [Buf() for _ in range(NTS)]
        PBASE = A0.off

        S.dma("sp", ident_f.ap, ident_d, writes=[ident_f.b])
        S.dma("sp", gmix.ap, gmix_d, writes=[gmix.b])
        S.dma("sp", gffn.ap, gffn_d, writes=[gffn.b])
        S.dma("sp", pow2.ap, pow2_d, writes=[pow2.b])
        S.dma("sp", masks.ap, mask_d, writes=[masks.b])
        S.dve(V("tensor_copy", out=ident_b.ap, in_=ident_f.ap), reads=[ident_f.b], writes=[ident_b.b])
        for j in range(4):
            S.dve(V("tensor_copy", out=I4.ap[:, j * 128:(j + 1) * 128], in_=ident_f.ap), reads=[ident_f.b], writes=[I4.b])

        A1 = Arena(arena_t, PBASE, ARENA_BYTES)
        w1 = A1.alloc([128, 8, DIN], BF16)
        g_kab = A1.alloc([128, 1024], F32)
        g_qab = A1.alloc([128, 1024], F32)
        g_ki = A1.alloc([128, 64], F32)
        xt = [A1.alloc([128, D], F32) for _ in range(2)]
        xT = [A1.alloc([128, 8, 128], BF16) for _ in range(2)]
        xT_b = [[Buf() for _ in range(8)] for _ in range(2)]
        cs = [A1.alloc([128, 8], F32) for _ in range(4)]
        sn = [A1.alloc([128, 8], F32) for _ in range(4)]
        ssx = [A1.alloc([128, 1], F32) for _ in range(2)]
        rstd = [A1.alloc([128, 1], F32) for _ in range(2)]
        z_k = [A1.alloc([128, 1024], F32) for _ in range(2)]
        z_v = [A1.alloc([128, 1024], F32) for _ in range(2)]
        z_q = [A1.alloc([128, 1024], F32) for _ in range(2)]
        z_qi = [A1.alloc([128, 512], F32) for _ in range(2)]
        z_ki = [A1.alloc([128, 64], F32) for _ in range(2)]
        wi_sb = [A1.alloc([128, 8], F32) for _ in range(2)]
        sq_tmps = [A1.alloc([128, 1024], F32) for _ in range(2)]
        hs_k = [A1.alloc([128, 16], F32) for _ in range(2)]
        hs_q = [A1.alloc([128, 16], F32) for _ in range(2)]
        hs_i = [A1.alloc([128, 1], F32) for _ in range(2)]
        knbs = [A1.alloc([128, 1024], BF16) for _ in range(2)]
        qnb = A1.alloc([128, 1024], BF16)
        qib = A1.alloc([128, 512], BF16)
        kibs = [A1.alloc([128, 128], BF16) for _ in range(2)]
        kT_sbs = [A1.alloc([128, 8, 128], BF16) for _ in range(2)]
        qT_sb = A1.alloc([128, 8, 128], BF16)
        qiT_sb = A1.alloc([128, 4, 128], BF16)
        vaug = [A1.alloc([128, 16, 65], BF16) for _ in range(2)]
        gates = A1.alloc([128, 2048], BF16)
        rt_default = [A1.alloc([128, 16, 8], F32) for _ in range(4)]

        w1_b = [Buf() for _ in range(6)]
        wstg = [A1.alloc([128, 1024], F32) for _ in range(2)]
        rtq = []
        for i_ in range(4):
            t_ = T(wstg[0].ap[:, i_ * 128:(i_ + 1) * 128].rearrange("p (h e) -> p h e", e=8))
            t_.b = wstg[0].b
            rtq.append(t_)
        wj = 0
        for c0 in range(0, DIN, 1024):
            c1 = min(DIN, c0 + 1024)
            n = c1 - c0
            for c in range(8):
                stg = wstg[wj % 2]
                S.dma("act", stg.ap[:, 0:n], w_in[c * 128:(c + 1) * 128, c0:c1], writes=[stg.b])
                if wj % 2 == 0:
                    S.dve(V("tensor_copy", out=w1.ap[:, c, c0:c1], in_=stg.ap[:, 0:n]), reads=[stg.b], writes=[w1_b[c0 // 1024]])
                else:
                    S.act(V("activation", out=w1.ap[:, c, c0:c1], in_=stg.ap[:, 0:n], func=AF.Copy), reads=[stg.b], writes=[w1_b[c0 // 1024]])
                wj += 1
        S.dma("sp", g_kab.ap, g_kab_d, writes=[g_kab.b])
        S.dma("sp", g_qab.ap, g_qab_d, writes=[g_qab.b])
        S.dma("sp", g_ki.ap, g_ki_d, writes=[g_ki.b])
        for i in range(2):
            S.pool(V("memset", vaug[i].ap[:, :, 64:65], 1.0), writes=[vaug[i].b])

        zrot = [0]
        trot = [0]

        def headnorm(z, nh, gains, hs, sq_tmp):
            n = nh * 64
            S.act(V("activation", out=sq_tmp.ap[:, 0:n], in_=z.ap[:, 0:n], func=AF.Square), reads=[z.b], writes=[sq_tmp.b])
            S.dve(V("tensor_reduce", out=hs.ap[:, 0:nh], in_=sq_tmp.ap[:, 0:n].rearrange("p (h d) -> p h d", d=64),
                    axis=AX.X, op=ALU.add), reads=[sq_tmp.b], writes=[hs.b])
            S.dve(V("tensor_scalar", out=hs.ap[:, 0:nh], in0=hs.ap[:, 0:nh], scalar1=1.0 / 64, scalar2=EPS,
                    op0=ALU.mult, op1=ALU.add), reads=[hs.b], writes=[hs.b])
            S.act(V("activation", out=hs.ap[:, 0:nh], in_=hs.ap[:, 0:nh], func=AF.Sqrt), reads=[hs.b], writes=[hs.b])
            S.dve(V("reciprocal", out=hs.ap[:, 0:nh], in_=hs.ap[:, 0:nh]), reads=[hs.b], writes=[hs.b])
            zv = z.ap[:, 0:n].rearrange("p (h d) -> p h d", d=64)
            S.dve(V("tensor_tensor", out=zv, in0=zv, in1=hs.ap[:, 0:nh].unsqueeze(2).to_broadcast([128, nh, 64]),
                    op=ALU.mult), reads=[z.b, hs.b], writes=[z.b])
            S.dve(V("tensor_tensor", out=z.ap[:, 0:n], in0=z.ap[:, 0:n], in1=gains.ap, op=ALU.mult),
                  reads=[z.b, gains.b], writes=[z.b])

        def rope(z, col0, nh, cs_t, sn_t, rt=None):
            rt = rt_default if rt is None else rt
            v = z.ap[:, col0:col0 + nh * 64].rearrange("p (h d) -> p h d", d=64)
            x1 = v[:, :, 0:8]
            x2 = v[:, :, 8:16]
            cb = cs_t.ap.unsqueeze(1).to_broadcast([128, nh, 8])
            sb_ = sn_t.ap.unsqueeze(1).to_broadcast([128, nh, 8])
            t = [r.ap[:, 0:nh, :] for r in rt]
            rd = [z.b, cs_t.b, sn_t.b]
            S.pool(V("tensor_tensor", out=t[0], in0=x1, in1=cb, op=ALU.mult), reads=rd, writes=[rt[0].b])
            S.pool(V("tensor_tensor", out=t[1], in0=x2, in1=sb_, op=ALU.mult), reads=rd, writes=[rt[1].b])
            S.pool(V("tensor_tensor", out=t[2], in0=x2, in1=cb, op=ALU.mult), reads=rd, writes=[rt[2].b])
            S.pool(V("tensor_tensor", out=t[3], in0=x1, in1=sb_, op=ALU.mult), reads=rd, writes=[rt[3].b])
            S.pool(V("tensor_tensor", out=x1, in0=t[0], in1=t[1], op=ALU.subtract), reads=[rt[0].b, rt[1].b], writes=[z.b])
            S.pool(V("tensor_tensor", out=x2, in0=t[2], in1=t[3], op=ALU.add), reads=[rt[2].b, rt[3].b], writes=[z.b])

        def transposes_bf(src, ncol_blocks, dst, dst_c0, bank=6):
            pv = PSb16(bank)
            for c in range(ncol_blocks):
                S.pe(V("transpose", out=pv[:, c * 128:(c + 1) * 128], in_=src.ap[:, c * 128:(c + 1) * 128],
                       identity=ident_b.ap), reads=[src.b, ident_b.b], writes=[psb[bank]])
            S.dve(V("tensor_copy", out=dst.ap[:, dst_c0:dst_c0 + ncol_blocks, :],
                    in_=pv[:, 0:ncol_blocks * 128].rearrange("p (c k) -> p c k", k=128)),
                  reads=[psb[bank]], writes=[dst.b])

        def ki_to_kiT(src_f32_ap, src_b, tidx, kib):
            S.pool(V("tensor_copy", out=kib.ap[:, 0:64], in_=src_f32_ap), reads=[src_b], writes=[kib.b])
            S.pool(V("tensor_copy", out=kib.ap[:, 64:128], in_=src_f32_ap), reads=[src_b], writes=[kib.b])
            bank = 6
            pv = PSb16(bank)
            S.pe(V("transpose", out=pv[:, 0:128], in_=kib.ap, identity=ident_b.ap), reads=[kib.b, ident_b.b], writes=[psb[bank]])
            S.dve(V("tensor_copy", out=kiT.ap[:, tidx * 128:(tidx + 1) * 128], in_=pv[:, 0:128]),
                  reads=[psb[bank]], writes=[kiT_b[tidx]])

        KCH = [(0, 512, "ka"), (512, 1024, "kb"), (1024, 1536, "va"), (1536, 2048, "vb"), (2048, 2112, "ki")]
        QCH = [(2112, 2624, "qa"), (2624, 3136, "qb"), (3136, 3648, "qi"), (3648, 4160, "g0"), (4160, 4672, "g1"),
               (4672, 5184, "g2"), (5184, 5696, "g3"), (5696, 5704, "wi")]

        def p1_load(t):
            p = t % 2
            S.dma("sp", xt[p].ap, x_all[t * 128:(t + 1) * 128, :], writes=[xt[p].b])
            S.dma("sp", cs[t % 4].ap, cos_all[t * 128:(t + 1) * 128, :], writes=[cs[t % 4].b])
            S.dma("sp", sn[t % 4].ap, sin_all[t * 128:(t + 1) * 128, :], writes=[sn[t % 4].b])

        def p1_front(t, after_chunk=None):
            own = (t % 2 == 0)
            slot = t // 2
            p = t % 2
            po = slot % 2
            xb = xt[p]
            sq_tmp = sq_tmps[p]
            S.act(V("activation", out=sq_tmp.ap, in_=xb.ap, func=AF.Square, accum_out=ssx[p].ap),
                  reads=[xb.b], writes=[sq_tmp.b, ssx[p].b])
            S.dve(V("tensor_scalar", out=rstd[p].ap, in0=ssx[p].ap, scalar1=1.0 / D, scalar2=EPS, op0=ALU.mult, op1=ALU.add),
                  reads=[ssx[p].b], writes=[rstd[p].b])
            S.act(V("activation", out=rstd[p].ap, in_=rstd[p].ap, func=AF.Sqrt), reads=[rstd[p].b], writes=[rstd[p].b])
            S.dve(V("reciprocal", out=rstd[p].ap, in_=rstd[p].ap), reads=[rstd[p].b], writes=[rstd[p].b])
            for c in range(8):
                bank = c // 4
                S.pe(V("transpose", out=PS[:, bank, (c % 4) * 128:(c % 4 + 1) * 128], in_=xb.ap[:, c * 128:(c + 1) * 128],
                       identity=ident_f.ap), reads=[xb.b, ident_f.b], writes=[psb[bank]])
            for c in range(8):
                bank = c // 4
                src = PS[:, bank, (c % 4) * 128:(c % 4 + 1) * 128]
                S.dve(V("tensor_scalar", out=xT[p].ap[:, c, :], in0=src, scalar1=gmix.ap[:, c:c + 1], scalar2=None,
                        op0=ALU.mult), reads=[psb[bank], gmix.b], writes=[xT_b[p][c]])
            if t + 1 < NT1:
                p1_load(t + 1)
            chunks = KCH + (QCH if own else [])
            for (c0, c1, kind) in chunks:
                n = c1 - c0
                bank = 2 + (zrot[0] % 4)
                zrot[0] += 1
                for c in range(8):
                    S.pe(V("matmul", PS[:, bank, 0:n], lhsT=xT[p].ap[:, c, :], rhs=w1.ap[:, c, c0:c1], start=(c == 0), stop=(c == 7)),
                         reads=[xT_b[p][c], w1_b[c0 // 1024], w1_b[(c1 - 1) // 1024]], writes=[psb[bank]])
                src = PS[:, bank, 0:n]
                rd = [psb[bank], rstd[p].b]
                sc = rstd[p].ap
                if kind == "ka":
                    S.act(V("activation", out=z_k[p].ap[:, 0:512], in_=src, func=AF.Copy, scale=sc), reads=rd, writes=[z_k[p].b])
                elif kind == "kb":
                    S.act(V("activation", out=z_k[p].ap[:, 512:1024], in_=src, func=AF.Copy, scale=sc), reads=rd, writes=[z_k[p].b])
                elif kind == "va":
                    S.act(V("activation", out=z_v[p].ap[:, 0:512], in_=src, func=AF.Copy, scale=sc), reads=rd, writes=[z_v[p].b])
                elif kind == "vb":
                    S.act(V("activation", out=z_v[p].ap[:, 512:1024], in_=src, func=AF.Copy, scale=sc), reads=rd, writes=[z_v[p].b])
                elif kind == "ki":
                    S.act(V("activation", out=z_ki[p].ap, in_=src, func=AF.Copy, scale=sc), reads=rd, writes=[z_ki[p].b])
                elif kind == "qa":
                    S.act(V("activation", out=z_q[po].ap[:, 0:512], in_=src, func=AF.Copy, scale=sc), reads=rd, writes=[z_q[po].b])
                elif kind == "qb":
                    S.act(V("activation", out=z_q[po].ap[:, 512:1024], in_=src, func=AF.Copy, scale=sc), reads=rd, writes=[z_q[po].b])
                elif kind == "qi":
                    S.act(V("activation", out=z_qi[po].ap, in_=src, func=AF.Copy, scale=sc), reads=rd, writes=[z_qi[po].b])
                elif kind[0] == "g":
                    j = int(kind[1])
                    S.act(V("activation", out=gates.ap[:, j * 512:(j + 1) * 512], in_=src, func=AF.Sigmoid, scale=sc), reads=rd, writes=[gates.b])
                elif kind == "wi":
                    S.act(V("activation", out=wi_sb[po].ap, in_=src, func=AF.Copy, scale=sc), reads=rd, writes=[wi_sb[po].b])
                if after_chunk is not None:
                    after_chunk()
        def p1_back_k(t):
            own = (t % 2 == 0)
            slot = t // 2
            p = t % 2
            po = slot % 2
            sq_tmp = sq_tmps[p]
            knb = knbs[p]
            kib = kibs[p]
            kT_sb = kT_sbs[p]
            headnorm(z_k[p], 16, g_kab, hs_k[p], sq_tmp)
            rope(z_k[p], 0, 8, cs[t % 4], sn[t % 4])
            headnorm(z_ki[p], 1, g_ki, hs_i[p], sq_tmp)
            rope(z_ki[p], 0, 1, cs[t % 4], sn[t % 4])
            if own:
                r0 = slot * 128
                S.dma("sp", ka_o[r0:r0 + 128, :], z_k[p].ap[:, 0:512], reads=[z_k[p].b])
                S.dma("sp", va_o[r0:r0 + 128, :], z_v[p].ap[:, 0:512], reads=[z_v[p].b])
                S.dma("sp", ki_o[r0:r0 + 128, :], z_ki[p].ap, reads=[z_ki[p].b])
                if slot >= 30:
                    rb = (slot - 30) * 128
                    S.dma("sp", kb_o[rb:rb + 128, :], z_k[p].ap[:, 512:1024], reads=[z_k[p].b])
                    S.dma("sp", vb_o[rb:rb + 128, :], z_v[p].ap[:, 512:1024], reads=[z_v[p].b])
            S.dve(V("tensor_copy", out=knb.ap, in_=z_k[p].ap), reads=[z_k[p].b], writes=[knb.b])
            transposes_bf(knb, 8, kT_sb, 0)
            S.dma("sp", kT_scr[t], kT_sb.ap, reads=[kT_sb.b], writes=[kscr_b[t]])
            S.act(V("activation", out=vaug[p].ap[:, :, 0:64], in_=z_v[p].ap.rearrange("p (h d) -> p h d", d=64), func=AF.Copy),
                  reads=[z_v[p].b], writes=[vaug[p].b])
            S.dma("sp", v_scr[t], vaug[p].ap, reads=[vaug[p].b], writes=[vscr_b[t]])
            ki_to_kiT(z_ki[p].ap, z_ki[p].b, t, kib)
            if own:
                S.dma("sp", gate_scr[slot], gates.ap, reads=[gates.b], writes=[qscr_b[slot]])
                S.dma("sp", wi_scr[slot], wi_sb[po].ap, reads=[wi_sb[po].b], writes=[qscr_b[slot]])

        def p1_back_q(t):
            slot = t // 2
            p = t % 2
            po = slot % 2
            sq_tmp = sq_tmps[p]
            headnorm(z_q[po], 16, g_qab, hs_q[po], sq_tmp)
            rope(z_q[po], 0, 8, cs[t % 4], sn[t % 4], rtq)
            rope(z_qi[po], 0, 8, cs[t % 4], sn[t % 4], rtq)
            S.dve(V("tensor_copy", out=qnb.ap, in_=z_q[po].ap), reads=[z_q[po].b], writes=[qnb.b])
            transposes_bf(qnb, 8, qT_sb, 0, bank=7)
            S.dma("sp", q_scr[slot], qT_sb.ap, reads=[qT_sb.b], writes=[qscr_b[slot]])
            S.dve(V("tensor_copy", out=qib.ap, in_=z_qi[po].ap), reads=[z_qi[po].b], writes=[qib.b])
            transposes_bf(qib, 4, qiT_sb, 0, bank=7)
            S.dma("sp", qi_scr[slot], qiT_sb.ap, reads=[qiT_sb.b], writes=[qscr_b[slot]])

        import os
        PH = os.environ.get("KPH", "123")
        NT1 = int(os.environ.get("KNT", NT))
        for m in (range(8) if "c" in PH or "2" in PH else []):
            p = m % 2
            tidx = 65 + m
            xb = xt[p]
            knb = knbs[p]
            kT_sb = kT_sbs[p]
            S.dma("sp", xb.ap[:, 0:512], ck_a[m * 128:(m + 1) * 128, :], writes=[xb.b])
            S.dma("sp", xb.ap[:, 512:1024], cv_a[m * 128:(m + 1) * 128, :], writes=[xb.b])
            S.dve(V("tensor_copy", out=knb.ap[:, 0:512], in_=xb.ap[:, 0:512]), reads=[xb.b], writes=[knb.b])
            S.pool(V("tensor_copy", out=vaug[p].ap[:, 0:8, 0:64], in_=xb.ap[:, 512:1024].rearrange("p (h d) -> p h d", d=64)),
                   reads=[xb.b], writes=[vaug[p].b])
            if m >= 4:
                zb = z_k[p]
                S.dma("sp", zb.ap[:, 0:512], ck_b[(m - 4) * 128:(m - 3) * 128, :], writes=[zb.b])
                S.dma("sp", zb.ap[:, 512:1024], cv_b[(m - 4) * 128:(m - 3) * 128, :], writes=[zb.b])
                S.dve(V("tensor_copy", out=knb.ap[:, 512:1024], in_=zb.ap[:, 0:512]), reads=[zb.b], writes=[knb.b])
                S.pool(V("tensor_copy", out=vaug[p].ap[:, 8:16, 0:64], in_=zb.ap[:, 512:1024].rearrange("p (h d) -> p h d", d=64)),
                       reads=[zb.b], writes=[vaug[p].b])
            nb_ = 8 if m >= 4 else 4
            transposes_bf(knb, nb_, kT_sb, 0)
            S.dma("sp", kT_scr[tidx][:, 0:nb_, :], kT_sb.ap[:, 0:nb_, :], reads=[kT_sb.b], writes=[kscr_b[tidx]])
            S.dma("sp", v_scr[tidx][:, 0:2 * nb_, :], vaug[p].ap[:, 0:2 * nb_, :], reads=[vaug[p].b], writes=[vscr_b[tidx]])
            zi = z_ki[p]
            S.dma("sp", zi.ap, ck_i[m * 128:(m + 1) * 128, :], writes=[zi.b])
            ki_to_kiT(zi.ap, zi.b, tidx, kibs[p])

        def merge_ops(a_, b_):
            out = []
            i = j = 0
            while i < len(a_) or j < len(b_):
                if j >= len(b_) or (i < len(a_) and i * max(1, len(b_)) <= j * max(1, len(a_))):
                    out.append(a_[i])
                    i += 1
                else:
                    out.append(b_[j])
                    j += 1
            return out

        p1_load(0)
        p1_front(0)
        pend_q = []
        for t in range(NT1):
            S.begin_defer()
            p1_back_k(t)
            ops_k = S.end_defer()
            ops = merge_ops(ops_k, pend_q)
            pend_q = []
            if t % 2 == 0:
                S.begin_defer()
                p1_back_q(t)
                pend_q = S.end_defer()
            if t + 1 < NT1:
                nch = 13 if (t + 1) % 2 == 0 else 5
                per = (len(ops) + nch - 1) // nch
                p1_front(t + 1, after_chunk=lambda: S.replay(ops, per))
            S.replay(ops, len(ops))
        S.replay(pend_q, len(pend_q))

        S.barrier()
        A2 = Arena(arena_t, PBASE, ARENA_BYTES)
        NMAX = 64 * 128
        isc = [A2.alloc([128, NMAX], F32) for _ in range(2)]
        junk = A2.alloc([128, NMAX], U8)
        mb = [A2.alloc([128, NMAX], BF16) for _ in range(2)]
        kbuf = [A2.alloc([128, 4, 4, 128], BF16) for _ in range(2)]
        vbuf = [A2.alloc([128, 4, 8, 65], BF16) for _ in range(2)]
        bandb = A2.alloc([128, 8, 768], BF16)
        bstage = A2.alloc([128, 768], F32)
        qTb = [A2.alloc([128, 16, 128], BF16) for _ in range(2)]
        qiTb = [A2.alloc([128, 8, 128], BF16) for _ in range(3)]
        wib = [A2.alloc([128, 8], F32) for _ in range(3)]
        diag = [A2.alloc([128, 8, 128], BF16) for _ in range(3)]
        Rb = [A2.alloc([128, 512], BF16) for _ in range(4)]
        pTb = [A2.alloc([128, 512], BF16) for _ in range(4)]
        LAG = 3
        SBANKS = [3, 4, 7, 0, 1]
        HBANKS = [0, 1, 3, 4, 7]
        oab = [A2.alloc([128, 1024], BF16) for _ in range(2)]
        bst = [A2.alloc([128, 4 * (KSTEPS + 2)], F32) for _ in range(2)]
        lnb = A2.alloc([128, 8], F32)
        rsb = [A2.alloc([128, 1], F32) for _ in range(8)]
        cstg = [A2.alloc([128, 1024], F32) for _ in range(2)]
        cout = [A2.alloc([128, 1024], BF16) for _ in range(2)]
        conv_i = [0]

        def conv_src(i):
            if i < 4:
                return w_oa[i * 128:(i + 1) * 128, :]
            if i < 8:
                return w_ob[(i - 4) * 128:(i - 3) * 128, :]
            if i < 16:
                return w_out[(i - 8) * 128:(i - 7) * 128, :]
            if i < 48:
                c, j = (i - 16) // 4, (i - 16) % 4
                return w_up[c * 128:(c + 1) * 128, j * 1024:(j + 1) * 1024]
            return w_down[(i - 48) * 128:(i - 47) * 128, :]

        def conv_steps(k):
            for _ in range(k):
                i = conv_i[0]
                if i >= 80:
                    return
                conv_i[0] += 1
                S.dma("pool", cstg[i % 2].ap, conv_src(i), writes=[cstg[i % 2].b])
                S.pool(V("tensor_copy", out=cout[i % 2].ap, in_=cstg[i % 2].ap), reads=[cstg[i % 2].b], writes=[cout[i % 2].b])
                S.dma("pool", wbf_scr[:, i * 1024:(i + 1) * 1024], cout[i % 2].ap, reads=[cout[i % 2].b], writes=[wscr_b])

        rot = {"sh": 0, "R": 0, "st": 0, "pT": 0, "kv": 0}
        K1 = KSTEPS + 2

        def load_band(src_d):
            for h in range(8):
                S.dma("sp", bstage.ap, src_d[h], writes=[bstage.b])
                S.act(V("activation", out=bandb.ap[:, h, :], in_=bstage.ap, func=AF.Copy, scale=8.0), reads=[bstage.b], writes=[bandb.b])

        def groups_of(tiles):
            gs = []
            for pos, t in enumerate(tiles):
                if gs and len(gs[-1]) < 4 and gs[-1][-1][1] + 1 == t:
                    gs[-1].append((pos, t))
                else:
                    gs.append([(pos, t)])
            return gs

        def p2_load_idx(s):
            b = s % 3
            S.dma("sp", qiTb[b].ap[0:64, 0::2, :], qi_scr[s][0:64, :, :], reads=[qscr_b[s]], writes=[qiTb[b].b])
            S.dma("sp", qiTb[b].ap[64:128, 1::2, :], qi_scr[s][64:128, :, :], reads=[qscr_b[s]], writes=[qiTb[b].b])
            S.dma("sp", wib[b].ap, wi_scr[s], reads=[qscr_b[s]], writes=[wib[b].b])
            for h in range(8):
                S.pool(V("tensor_scalar", out=diag[b].ap[:, h, :], in0=ident_b.ap, scalar1=wib[b].ap[:, h:h + 1], scalar2=None, op0=ALU.mult),
                       reads=[ident_b.b, wib[b].b], writes=[diag[b].b])

        def p2_load_q(s):
            b = s % 2
            S.dma("sp", qTb[b].ap[0:64, 0::2, :], q_scr[s][0:64, :, :], reads=[qscr_b[s]], writes=[qTb[b].b])
            S.dma("sp", qTb[b].ap[64:128, 1::2, :], q_scr[s][64:128, :, :], reads=[qscr_b[s]], writes=[qTb[b].b])

        def p2_index(s, tiles):
            b = s % 2
            b3 = s % 3
            steps = [(g, h) for g in groups_of(tiles) for h in range(8)]
            q = []

            def emit_d(pd):
                (pg, ph, pR, pn) = pd
                S.pe(V("matmul", PS[:, 2, 0:pn], lhsT=diag[b3].ap[:, ph, :], rhs=pR.ap[:, 0:pn], start=(ph == 0), stop=(ph == 7)),
                     reads=[diag[b3].b, pR.b], writes=[psb[2]])
                if ph == 7:
                    c0 = pg[0][0] * 128
                    S.act(V("activation", out=isc[b].ap[:, c0:c0 + pn], in_=PS[:, 2, 0:pn], func=AF.Copy),
                          reads=[psb[2]], writes=[isc[b].b])

            for (g, h) in steps:
                n = len(g) * 128
                t0 = g[0][1]
                bank = HBANKS[rot["sh"] % 5]
                rot["sh"] += 1
                base = 64 * (h % 2)
                S.pe(V("matmul", PS[:, bank, 0:n], lhsT=qiTb[b3].ap[:, h, :],
                       rhs=kiT.ap[:, t0 * 128:t0 * 128 + n], start=True, stop=True),
                     reads=[qiTb[b3].b] + [kiT_b[t] for (_, t) in g], writes=[psb[bank]])
                R = Rb[rot["R"] % 4]
                rot["R"] += 1
                S.act(V("activation", out=R.ap[:, 0:n], in_=PS[:, bank, 0:n], func=AF.Relu), reads=[psb[bank]], writes=[R.b])
                q.append((g, h, R, n))
                if len(q) > LAG:
                    emit_d(q.pop(0))
            while q:
                emit_d(q.pop(0))

        def p2_bisect(s, tiles, mask_list):
            b = s % 2
            N = len(tiles) * 128
            X = isc[b]
            st_ = bst[b]
            amax = st_.ap[:, 0:1]
            wk = st_.ap[:, K1:2 * K1]
            mid = st_.ap[:, 2 * K1:3 * K1]
            cnt = st_.ap[:, 3 * K1:4 * K1]
            w2 = st_.ap[:, 1:K1]
            S.dve(V("tensor_reduce", out=amax, in_=X.ap[:, 0:N], axis=AX.X, op=ALU.max, apply_absolute_value=True),
                  reads=[X.b], writes=[st_.b])
            for (pos, mcol) in mask_list:
                S.dve(V("tensor_tensor", out=X.ap[:, pos * 128:(pos + 1) * 128], in0=X.ap[:, pos * 128:(pos + 1) * 128],
                        in1=masks.ap[:, mcol * 128:(mcol + 1) * 128], op=ALU.add), reads=[X.b, masks.b], writes=[X.b])
            S.dve(V("tensor_scalar", out=wk, in0=pow2.ap[:, 0:K1], scalar1=amax, scalar2=None, op0=ALU.mult),
                  reads=[st_.b, pow2.b], writes=[st_.b])
            S.dve(V("tensor_scalar", out=w2, in0=pow2.ap[:, K1 + 1:2 * K1], scalar1=amax, scalar2=None, op0=ALU.mult),
                  reads=[st_.b, pow2.b], writes=[st_.b])
            S.dve(V("memset", mid[:, 0:1], 0.0), writes=[st_.b])
            for k in range(KSTEPS):
                S.dve(V("tensor_scalar", out=junk.ap[:, 0:N], in0=X.ap[:, 0:N], scalar1=mid[:, k:k + 1], scalar2=None,
                        op0=ALU.is_ge, op1=ALU.add, accum_out=cnt[:, k:k + 1]), reads=[X.b, st_.b], writes=[junk.b, st_.b])
                S.dve(V("scalar_tensor_tensor", out=cnt[:, k:k + 1], in0=cnt[:, k:k + 1], scalar=255.5, in1=w2[:, k:k + 1],
                        op0=ALU.is_ge, op1=ALU.mult), reads=[st_.b], writes=[st_.b])
                S.dve(V("scalar_tensor_tensor", out=mid[:, k + 1:k + 2], in0=mid[:, k:k + 1], scalar=wk[:, k + 1:k + 2],
                        in1=cnt[:, k:k + 1], op0=ALU.subtract, op1=ALU.add), reads=[st_.b], writes=[st_.b])
            thr = cnt[:, KSTEPS:KSTEPS + 1]
            S.dve(V("tensor_tensor", out=thr, in0=mid[:, KSTEPS:KSTEPS + 1], in1=wk[:, KSTEPS:KSTEPS + 1], op=ALU.subtract),
                  reads=[st_.b], writes=[st_.b])
            S.dve(V("tensor_scalar", out=mb[b].ap[:, 0:N], in0=X.ap[:, 0:N], scalar1=thr, scalar2=NEG, op0=ALU.is_lt, op1=ALU.mult),
                  reads=[X.b, st_.b], writes=[mb[b].b])

        def attend(s, tiles, pair0, vh0, bias_mode):
            b = s % 2
            qT = qTb[b]
            steps = []
            for g in groups_of(tiles):
                kv = rot["kv"] % 2
                rot["kv"] += 1
                ng = len(g)
                t0 = g[0][1]
                steps.append(("load", g, kv))
                for j, (pos, t) in enumerate(g):
                    for hg in range(2):
                        steps.append(("qk", pos, j, hg, kv))
            first = [True, True]
            nsteps_left = [sum(1 for x in steps if x[0] == "qk" and x[3] == hg) for hg in range(2)]
            pendq = []

            def emit_pv(pd):
                (pos, j, hg, kv, pt) = pd
                nsteps_left[hg] -= 1
                for hh in range(4):
                    h = 4 * hg + hh
                    S.pe(V("matmul", PS[:, 5 + hg, hh * 65:(hh + 1) * 65], lhsT=pt.ap[:, hh * 128:(hh + 1) * 128],
                           rhs=vbuf[kv].ap[:, j, h, :], start=(first[hg] and hh == 0), stop=(nsteps_left[hg] == 0),
                           skip_group_check=True), reads=[pt.b, vbuf[kv].b], writes=[psb[5 + hg]])
                first[hg] = False

            for stp in steps:
                if stp[0] == "load":
                    (_, g, kv) = stp
                    ng = len(g)
                    t0 = g[0][1]
                    S.dma("sp", kbuf[kv].ap[:, 0:ng], kT_scr[t0:t0 + ng].rearrange("t p c k -> p t c k")[:, :, pair0:pair0 + 4, :],
                          reads=[kscr_b[t] for (_, t) in g], writes=[kbuf[kv].b])
                    S.dma("sp", vbuf[kv].ap[:, 0:ng], v_scr[t0:t0 + ng].rearrange("t p h d -> p t h d")[:, :, vh0:vh0 + 8, :],
                          reads=[vscr_b[t] for (_, t) in g], writes=[vbuf[kv].b])
                    continue
                (_, pos, j, hg, kv) = stp
                sbank = SBANKS[rot["st"] % 5]
                rot["st"] += 1
                if bias_mode is None:
                    S.pe(V("matmul", PS[:, sbank, :], lhsT=mb[b].ap[:, pos * 128:(pos + 1) * 128], rhs=I4.ap, start=True, stop=False,
                           skip_group_check=True), reads=[mb[b].b, I4.b], writes=[psb[sbank]])
                for hh in range(4):
                    h = 4 * hg + hh
                    pr = h // 2
                    base = 64 * (h % 2)
                    if bias_mode is not None:
                        m = bias_mode[pos]
                        S.pe(V("matmul", PS[:, sbank, hh * 128:(hh + 1) * 128], lhsT=bandb.ap[:, h, m * 128:(m + 1) * 128], rhs=ident_b.ap,
                               start=True, stop=False, skip_group_check=True), reads=[bandb.b, ident_b.b], writes=[psb[sbank]])
                    S.pe(V("matmul", PS[:, sbank, hh * 128:(hh + 1) * 128], lhsT=kbuf[kv].ap[:, j, pr, :],
                           rhs=qT.ap[:, 2 * pair0 + h, :], start=False, stop=True, skip_group_check=True),
                         reads=[kbuf[kv].b, qT.b], writes=[psb[sbank]])
                pt = pTb[rot["pT"] % 4]
                rot["pT"] += 1
                S.act(V("activation", out=pt.ap, in_=PS[:, sbank, :], func=AF.Exp, scale=0.125), reads=[psb[sbank]], writes=[pt.b])
                pendq.append((pos, j, hg, kv, pt))
                if len(pendq) > LAG:
                    emit_pv(pendq.pop(0))
            while pendq:
                emit_pv(pendq.pop(0))
            col0 = 0 if bias_mode is None else 512
            for hg in range(2):
                ov = PS[:, 5 + hg, 0:260].rearrange("p (h d) -> p h d", d=65)
                S.act(V("activation", out=lnb.ap[:, hg * 4:(hg + 1) * 4], in_=ov[:, :, 64], func=AF.Ln), reads=[psb[5 + hg]], writes=[lnb.b])
                for hh in range(4):
                    h = 4 * hg + hh
                    S.act(V("activation", out=rsb[h].ap, in_=lnb.ap[:, h:h + 1], func=AF.Exp, scale=-1.0), reads=[lnb.b], writes=[rsb[h].b])
                    S.act(V("activation", out=oab[b].ap[:, col0 + h * 64:col0 + (h + 1) * 64], in_=ov[:, hh, 0:64], func=AF.Copy,
                            scale=rsb[h].ap), reads=[psb[5 + hg], rsb[h].b], writes=[oab[b].b])

        slots = []
        for i in range(32):
            tiles = list(range(2 * i + 2))
            masks_l = [(2 * i, 0), (2 * i + 1, 1)]
            band = [(2 * i - 4 + m, m) for m in range(6) if 2 * i - 4 + m >= 0]
            slots.append((tiles, masks_l, band))
        slots.append((list(range(65, 73)) + [64], [(8, 0)], [(69 + m, m) for m in range(4)] + [(64, 4)]))

        NS2 = min(NSLOT, int(os.environ.get("KNS", NSLOT))) if "2" in PH else 0
        for i in range(2):
            S.pool(V("memset", qTb[i].ap, 0.0), writes=[qTb[i].b])
        for i in range(3):
            S.pool(V("memset", qiTb[i].ap, 0.0), writes=[qiTb[i].b])

        def mark(n):
            if os.environ.get("KDBG"):
                print("MARK", n, S.uid, flush=True)
        if NS2:
          mark("p2start")
          load_band(bandp_d)
          mark("band")
          p2_load_idx(0)
          mark("loadidx0")
          p2_index(0, slots[0][0])
          mark("index0")
          p2_bisect(0, slots[0][0], slots[0][1])
          mark("bisect0")
          p2_load_idx(1)
          p2_index(1, slots[1][0])
          if NS2 > 2:
              p2_load_idx(2)
          p2_load_q(0)
          mark("pre-loop")
        for s in range(NS2):
            if s + 3 < NS2:
                p2_load_idx(s + 3)
            if s + 2 < NS2:
                p2_index(s + 2, slots[s + 2][0])
            attend(s, slots[s][0], 0, 0, None)
            mark("dsa%d" % s)
            if s + 1 < NS2:
                p2_load_q(s + 1)
                p2_bisect(s + 1, slots[s + 1][0], slots[s + 1][1])
            if s == NSLOT - 1:
                load_band(bands_d)
            band = slots[s][2]
            mark("bis%d" % (s + 1))
            attend(s, [t for (t, m) in band], 4, 8, [m for (t, m) in band])
            mark("band%d" % s)
            if "3" in PH:
                conv_steps(3)
            S.dma("pool", oab_scr[s], oab[s % 2].ap, reads=[oab[s % 2].b], writes=[oscr_b[s]])

        S.barrier()
        A3 = Arena(arena_t, PBASE - NTS * 256, ARENA_BYTES)
        wall = A3.alloc([128, 80 * 1024], BF16)

        def wview(c0, nchunk, ncol):
            t_ = T(wall.ap[:, c0 * 1024:c0 * 1024 + nchunk * ncol].rearrange("p (c n) -> p c n", n=ncol))
            t_.b = wall.b
            return t_
        wall_bs = [Buf() for _ in range(10)]
        woa = wview(0, 4, D)
        wob = wview(4, 4, D)
        wo = wview(8, 8, D)
        wu = wview(16, 8, 4096)
        wd = wview(48, 32, D)
        woa.b = wall_bs[0]
        wob.b = wall_bs[0]
        wo.b = wall_bs[1]
        o_in = A3.alloc([128, 1024], BF16)
        oT = A3.alloc([128, 8, 128], BF16)
        gt = A3.alloc([128, 2048], BF16)
        xin = A3.alloc([128, D], F32)
        t1 = A3.alloc([128, D], F32)
        t2 = A3.alloc([128, D], F32)
        x2s = [A3.alloc([128, D], F32) for _ in range(2)]
        x2Ts = [A3.alloc([128, 8, 128], BF16) for _ in range(2)]
        hT = A3.alloc([128, 32, 128], BF16)
        rl = A3.alloc([128, 512], F32)
        ss3 = A3.alloc([128, 1], F32)
        r3s = [A3.alloc([128, 1], F32) for _ in range(2)]

        if "3" in PH:
            conv_steps(80)
            for k in range(10):
                S.dma("sp", wall.ap[:, k * 8192:(k + 1) * 8192], wbf_scr[:, k * 8192:(k + 1) * 8192], reads=[wscr_b], writes=[wall_bs[k]])

        def tr_bf_p3(src, dst):
            pv = PSb16(0)
            for c in range(8):
                S.pe(V("transpose", out=pv[:, c * 128:(c + 1) * 128], in_=src.ap[:, c * 128:(c + 1) * 128], identity=ident_b.ap),
                     reads=[src.b, ident_b.b], writes=[psb[0]])
            S.dve(V("tensor_copy", out=dst.ap, in_=pv.rearrange("p (c k) -> p c k", k=128)), reads=[psb[0]], writes=[dst.b])

        def p3_stage_a(s):
            x2 = x2s[s % 2]
            x2T = x2Ts[s % 2]
            r3 = r3s[s % 2]
            S.dma("sp", o_in.ap, oab_scr[s], reads=[oscr_b[s]], writes=[o_in.b])
            S.dma("sp", gt.ap, gate_scr[s], reads=[qscr_b[s]], writes=[gt.b])
            S.dma("sp", xin.ap, x_own[s * 128:(s + 1) * 128, :], writes=[xin.b])
            tr_bf_p3(o_in, oT)
            for half in range(2):
                for c in range(4):
                    S.pe(V("matmul", PS[:, 2 + half, :], lhsT=oT.ap[:, c, :], rhs=woa.ap[:, c, half * 512:(half + 1) * 512],
                           start=(c == 0), stop=(c == 3)), reads=[oT.b, woa.b], writes=[psb[2 + half]])
                S.dve(V("tensor_tensor", out=t1.ap[:, half * 512:(half + 1) * 512], in0=PS[:, 2 + half, :],
                        in1=gt.ap[:, half * 512:(half + 1) * 512], op=ALU.mult), reads=[psb[2 + half], gt.b], writes=[t1.b])
            for half in range(2):
                for c in range(4):
                    S.pe(V("matmul", PS[:, 2 + half, :], lhsT=oT.ap[:, 4 + c, :], rhs=wob.ap[:, c, half * 512:(half + 1) * 512],
                           start=(c == 0), stop=(c == 3)), reads=[oT.b, wob.b], writes=[psb[2 + half]])
                S.dve(V("tensor_tensor", out=t2.ap[:, half * 512:(half + 1) * 512], in0=PS[:, 2 + half, :],
                        in1=gt.ap[:, 1024 + half * 512:1024 + (half + 1) * 512], op=ALU.mult), reads=[psb[2 + half], gt.b], writes=[t2.b])
            S.pool(V("tensor_tensor", out=o_in.ap, in0=t1.ap, in1=t2.ap, op=ALU.add), reads=[t1.b, t2.b], writes=[o_in.b])
            tr_bf_p3(o_in, oT)
            for half in range(2):
                for c in range(8):
                    S.pe(V("matmul", PS[:, 2 + half, :], lhsT=oT.ap[:, c, :], rhs=wo.ap[:, c, half * 512:(half + 1) * 512],
                           start=(c == 0), stop=(c == 7)), reads=[oT.b, wo.b], writes=[psb[2 + half]])
                S.dve(V("tensor_tensor", out=x2.ap[:, half * 512:(half + 1) * 512], in0=PS[:, 2 + half, :],
                        in1=xin.ap[:, half * 512:(half + 1) * 512], op=ALU.add), reads=[psb[2 + half], xin.b], writes=[x2.b])
            S.act(V("activation", out=t1.ap, in_=x2.ap, func=AF.Square, accum_out=ss3.ap), reads=[x2.b], writes=[t1.b, ss3.b])
            S.dve(V("tensor_scalar", out=r3.ap, in0=ss3.ap, scalar1=1.0 / D, scalar2=EPS, op0=ALU.mult, op1=ALU.add),
                  reads=[ss3.b], writes=[r3.b])
            S.dve(V("reciprocal", out=r3.ap, in_=r3.ap), reads=[r3.b], writes=[r3.b])
            for c in range(8):
                bank = c // 4
                S.pe(V("transpose", out=PS[:, bank, (c % 4) * 128:(c % 4 + 1) * 128], in_=x2.ap[:, c * 128:(c + 1) * 128],
                       identity=ident_f.ap), reads=[x2.b, ident_f.b], writes=[psb[bank]])
            for c in range(8):
                bank = c // 4
                src = PS[:, bank, (c % 4) * 128:(c % 4 + 1) * 128]
                S.dve(V("tensor_scalar", out=x2T.ap[:, c, :], in0=src, scalar1=gffn.ap[:, c:c + 1], scalar2=None, op0=ALU.mult),
                      reads=[psb[bank], gffn.b], writes=[x2T.b])

        def p3_stage_b(s, hook):
            x2 = x2s[s % 2]
            x2T = x2Ts[s % 2]
            r3 = r3s[s % 2]
            for f4 in range(8):
                bank = 4 + (f4 % 2)
                for ff in range(4):
                    f = f4 * 4 + ff
                    for c in range(8):
                        S.pe(V("matmul", PS[:, bank, ff * 128:(ff + 1) * 128], lhsT=wu.ap[:, c, f * 128:(f + 1) * 128], rhs=x2T.ap[:, c, :],
                               start=(c == 0), stop=(c == 7), skip_group_check=True), reads=[wall_bs[2 + c // 2], x2T.b], writes=[psb[bank]])
                S.act(V("activation", out=rl.ap, in_=PS[:, bank, :], func=AF.Relu), reads=[psb[bank]], writes=[rl.b])
                S.dve(V("tensor_tensor", out=hT.ap[:, f4 * 4:(f4 + 1) * 4, :].rearrange("p a b -> p (a b)"), in0=rl.ap,
                        in1=rl.ap, op=ALU.mult), reads=[rl.b], writes=[hT.b])
                hook()
            for half in range(2):
                for f in range(32):
                    S.pe(V("matmul", PS[:, 6 + half, :], lhsT=hT.ap[:, f, :], rhs=wd.ap[:, f, half * 512:(half + 1) * 512],
                           start=(f == 0), stop=(f == 31)), reads=[hT.b, wall_bs[6 + f // 8]], writes=[psb[6 + half]])
                    if f % 8 == 7:
                        hook()
                S.dve(V("scalar_tensor_tensor", out=x2.ap[:, half * 512:(half + 1) * 512], in0=PS[:, 6 + half, :], scalar=r3.ap,
                        in1=x2.ap[:, half * 512:(half + 1) * 512], op0=ALU.mult, op1=ALU.add),
                      reads=[psb[6 + half], r3.b, x2.b], writes=[x2.b])
            S.dma("sp", y_o[s * 128:(s + 1) * 128, :], x2.ap, reads=[x2.b])

        NS3 = min(NSLOT, int(os.environ.get("KNS3", NSLOT))) if "3" in PH else 0
        if NS3:
            p3_stage_a(0)
        for s in range(NS3):
            ops = []
            if s + 1 < NS3:
                S.begin_defer()
                p3_stage_a(s + 1)
                ops = S.end_defer()
            per = (len(ops) + 14) // 15
            p3_stage_b(s, lambda: S.replay(ops, per))
            S.replay(ops, len(ops))

        S.emit(nc, st)
    return nc


def _rope_tables(pos):
    half = 8
    inv = (np.float32(500000.0) ** (-np.arange(half, dtype=np.float32) / np.float32(half))).astype(np.float32)
    ang = pos.astype(np.float32)[:, None] * inv[None, :]
    return np.cos(ang).astype(np.float32), np.sin(ang).astype(np.float32)


def _band_tiles(table, true_tile_of_m, visible_fn):
    out = np.full((8, 128, 768), NEG, np.float32)
    qi = np.arange(128)[:, None]
    kj = np.arange(128)[None, :]
    for m in range(6):
        off = true_tile_of_m[m]
        if off is None:
            continue
        dist = off * 128 + qi - kj
        idx = np.clip(dist, -128, 128) + 128
        vis = visible_fn(off, qi, kj)
        for h in range(8):
            g = table[idx, h]
            out[h, :, m * 128:(m + 1) * 128] = np.where(vis, g, np.float32(NEG))
    return out


def _vis_prompt(off, qi, kj):
    dc = 2 * off + qi // 64 - kj // 64
    return (dc >= 0) & (dc <= 8)


_CACHE = {}


def kernel(x_prompt, x_sample, cache_k_a, cache_v_a, cache_k_idx, cache_k_b, cache_v_b,
           norm_mix, w_in, qnorm_a, knorm_a, knorm_idx, qnorm_b, knorm_b, rel_bias_b,
           w_o_a, w_o_b, w_out, norm_ffn, w_up, w_down):
    f = lambda a: np.ascontiguousarray(np.asarray(a, dtype=np.float32))
    x_prompt, x_sample = f(x_prompt), f(x_sample)
    w_in_ = f(w_in)[0]
    sp = np.cumsum([0, 512, 512, 512, 512, 64, 8, 512, 512, 512, 1024, 1024])
    seg = {n: (sp[i], sp[i + 1]) for i, n in enumerate(["qa", "ka", "va", "qi", "ki", "wi", "qb", "kb", "vb", "ga", "gb"])}
    order = ["ka", "kb", "va", "vb", "ki", "qa", "qb", "qi", "ga", "gb", "wi"]
    w_in_r = np.ascontiguousarray(np.concatenate([w_in_[:, seg[n][0]:seg[n][1]] for n in order], axis=1))
    rep = lambda g, n: np.tile(f(g)[0], n)
    g_kab = np.ascontiguousarray(np.broadcast_to(np.concatenate([rep(knorm_a, 8), rep(knorm_b, 8)])[None], (128, 1024)))
    g_qab = np.ascontiguousarray(np.broadcast_to(np.concatenate([rep(qnorm_a, 8), rep(qnorm_b, 8)])[None], (128, 1024)))
    g_ki = np.ascontiguousarray(np.broadcast_to(f(knorm_idx)[0][None], (128, 64)))
    gmix = np.ascontiguousarray(f(norm_mix)[0].reshape(8, 128).T)
    gffn = np.ascontiguousarray(f(norm_ffn)[0].reshape(8, 128).T)
    K1 = KSTEPS + 2
    p2 = np.concatenate([2.0 ** (-np.arange(K1)), 2.0 ** (1.0 - np.arange(K1))]).astype(np.float32)
    pow2 = np.ascontiguousarray(np.broadcast_to(p2[None], (128, 2 * K1)))
    table = f(rel_bias_b)[0]
    qi = np.arange(128)[:, None]
    kj = np.arange(128)[None, :]
    own_mask = np.where((kj // 64) <= (qi // 64), 0.0, -1e30).astype(np.float32)
    band_s = _band_tiles(table, [4, 3, 2, 1, 0, None], _vis_prompt)

    in_maps = []
    for c in range(8):
        b, hf = c // 2, c % 2
        xb = x_prompt[b].reshape(64, 128, D)
        order_t = []
        for i in range(32):
            order_t += [2 * i + hf, 2 * i + 1 - hf]
        xs_pad = np.zeros((128, D), np.float32)
        xs_pad[:64] = x_sample[c]
        x_all = np.concatenate([xb[order_t].reshape(64 * 128, D), xs_pad], axis=0)
        pos = np.concatenate([(np.array(order_t)[:, None] * 128 + np.arange(128)[None]).reshape(-1),
                              1024 + np.arange(64), np.zeros(64, np.int64)])
        cos_all, sin_all = _rope_tables(pos)
        x_own = np.concatenate([xb[hf::2].reshape(32 * 128, D), xs_pad], axis=0)
        foreign = np.full((128, 128), 0.0 if hf == 1 else -1e30, np.float32)
        masks = np.ascontiguousarray(np.concatenate([own_mask, foreign], axis=1))
        if hf == 0:
            offs = [4, 3, 2, 1, 0, None]
        else:
            offs = [None, 4, 1, 2, 0 - 0, 0]
            offs = [4, None, 2, 3, 0, 1]
        band_p = _band_tiles(table, offs, _vis_prompt)
        in_maps.append(dict(
            x_all=np.ascontiguousarray(x_all), cos_all=cos_all, sin_all=sin_all, x_own=np.ascontiguousarray(x_own),
            ck_a=f(cache_k_a)[0, c].reshape(1024, 512), cv_a=f(cache_v_a)[0, c].reshape(1024, 512),
            ck_i=f(cache_k_idx)[0, c], ck_b=f(cache_k_b)[0, c].reshape(512, 512), cv_b=f(cache_v_b)[0, c].reshape(512, 512),
            w_in=w_in_r, w_oa=f(w_o_a)[0], w_ob=f(w_o_b)[0], w_out=f(w_out)[0], w_up=f(w_up)[0], w_down=f(w_down)[0],
            gmix=gmix, gffn=gffn, g_kab=g_kab, g_qab=g_qab, g_ki=g_ki, ident=np.eye(128, dtype=np.float32),
            pow2=pow2, masks=masks, band_p=band_p, band_s=band_s))

    if "nc" not in _CACHE:
        _CACHE["nc"] = build_program()
    import os
    NCORE = int(os.environ.get("KCORES", "8"))
    res = run_bass_kernel_spmd(_CACHE["nc"], in_maps[:NCORE], core_ids=list(range(NCORE)))
    R = list(res.results) + [res.results[0]] * (8 - NCORE)

    y_p = np.zeros((4, 64, 128, D), np.float32)
    ka_p = np.zeros((4, 64, 128, 512), np.float32)
    va_p = np.zeros((4, 64, 128, 512), np.float32)
    ki_p = np.zeros((4, 64, 128, 64), np.float32)
    kb_p = np.zeros((4, 4, 128, 512), np.float32)
    vb_p = np.zeros((4, 4, 128, 512), np.float32)
    y_s = np.zeros((8, 64, D), np.float32)
    ka_s = np.zeros((8, 64, 512), np.float32)
    va_s = np.zeros((8, 64, 512), np.float32)
    ki_s = np.zeros((8, 64, 64), np.float32)
    kb_s = np.zeros((8, 64, 512), np.float32)
    vb_s = np.zeros((8, 64, 512), np.float32)
    for c in range(8):
        b, hf = c // 2, c % 2
        r = R[c]
        y_p[b, hf::2] = r["y_o"][:4096].reshape(32, 128, D)
        ka_p[b, hf::2] = r["ka_o"][:4096].reshape(32, 128, 512)
        va_p[b, hf::2] = r["va_o"][:4096].reshape(32, 128, 512)
        ki_p[b, hf::2] = r["ki_o"][:4096].reshape(32, 128, 64)
        kb_p[b, hf::2] = r["kb_o"][:256].reshape(2, 128, 512)
        vb_p[b, hf::2] = r["vb_o"][:256].reshape(2, 128, 512)
        y_s[c] = r["y_o"][4096:4160]
        ka_s[c] = r["ka_o"][4096:4160]
        va_s[c] = r["va_o"][4096:4160]
        ki_s[c] = r["ki_o"][4096:4160]
        kb_s[c] = r["kb_o"][256:320]
        vb_s[c] = r["vb_o"][256:320]
    return (y_p.reshape(4, 8192, D), y_s,
            ka_p.reshape(1, 4, 8192, 8, 64), va_p.reshape(1, 4, 8192, 8, 64), ki_p.reshape(1, 4, 8192, 64),
            kb_p.reshape(1, 4, 512, 8, 64), vb_p.reshape(1, 4, 512, 8, 64),
            ka_s.reshape(1, 8, 64, 8, 64), va_s.reshape(1, 8, 64, 8, 64), ki_s.reshape(1, 8, 64, 64),
            kb_s.reshape(1, 8, 64, 8, 64), vb_s.reshape(1, 8, 64, 8, 64))
```
